# Optimizing a Trainium2 kernel written in Bass

```python
import jax, jax.numpy as jnp
from jax import lax
import numpy as np

D_MODEL = 1024
BATCH = 2
SEQ = 8192
DEPTH = 1

N_META = 16
HGRN_DK = 128
HGRN_HEADS = D_MODEL // HGRN_DK
HGRN_DV = D_MODEL // HGRN_HEADS
HGRN_WIDTH = HGRN_HEADS * HGRN_DK
CHUNK = 64
POOL_WINDOWS = (2, 4, 8, 16)
POOL_GROUPS = len(POOL_WINDOWS)
POOL_GROUP_DIM = 128
POOL_WIDTH = POOL_GROUPS * POOL_GROUP_DIM
D_FF = -(-8 * D_MODEL // (3 * 256)) * 256
EPS = 1e-6
SPLIT_SIZES = (HGRN_WIDTH, HGRN_WIDTH, HGRN_HEADS * HGRN_DV, HGRN_HEADS * HGRN_DV,
               POOL_WIDTH, D_MODEL, D_MODEL)
SPLIT_POINTS = tuple(int(v) for v in np.cumsum(SPLIT_SIZES)[:-1])
IN_WIDTH = int(sum(SPLIT_SIZES))

kernel_name = "hgrn2_pool_gated_hybrid"


def rmsnorm(x, g):
    xf = x.astype(jnp.float32)
    y = xf * lax.rsqrt(jnp.mean(xf * xf, axis=-1, keepdims=True) + EPS)
    return (y * g.astype(jnp.float32)).astype(x.dtype)


def hgrn2_chunked(q, k, v, log_f):
    B, L, H, DK = q.shape
    DV = v.shape[-1]
    nc = L // CHUNK

    def to_chunks(t):
        return t.reshape(B, nc, CHUNK, H, t.shape[-1]).transpose(1, 0, 3, 2, 4)

    qc, kc, vc, gc = to_chunks(q), to_chunks(k), to_chunks(v), to_chunks(log_f)
    causal = jnp.tril(jnp.ones((CHUNK, CHUNK), dtype=bool))[None, None, :, :, None]

    def step(S, inp):
        qi, ki, vi, gi = inp
        b = jnp.cumsum(gi, axis=2)
        b_last = b[:, :, -1:, :]
        o_inter = jnp.einsum('bhtd,bhde->bhte', qi * jnp.exp(b), S)
        diff = b[:, :, :, None, :] - b[:, :, None, :, :]
        decay = jnp.exp(jnp.where(causal, diff, -jnp.inf))
        scores = jnp.einsum('bhtd,bhsd,bhtsd->bhts', qi, ki, decay)
        o_intra = jnp.einsum('bhts,bhse->bhte', scores, vi)
        S_new = (jnp.exp(b_last)[:, :, 0, :, None] * S
                 + jnp.einsum('bhsd,bhse->bhde', ki * jnp.exp(b_last - b), vi))
        return S_new, o_inter + o_intra

    S0 = jnp.zeros((B, H, DK, DV), dtype=jnp.float32)
    _, o = lax.scan(step, S0, (qc, kc, vc, gc))
    return o.transpose(1, 0, 3, 2, 4).reshape(B, L, H, DV)


def causal_multiscale_pool(u):
    B, L, _ = u.shape
    uf = u.astype(jnp.float32).reshape(B, L, POOL_GROUPS, POOL_GROUP_DIM)
    cs = jnp.cumsum(uf, axis=1)
    pos = jnp.arange(1, L + 1, dtype=jnp.float32)
    outs = []
    for gi, w in enumerate(POOL_WINDOWS):
        c = cs[:, :, gi]
        shifted = jnp.pad(c, ((0, 0), (w, 0), (0, 0)))[:, :L]
        cnt = jnp.minimum(pos, float(w))[None, :, None]
        outs.append((c - shifted) / cnt - uf[:, :, gi])
    return jnp.stack(outs, axis=2)


def setup_inputs(seed: int = 0) -> dict:
    key = jax.random.key(seed)
    ks = jax.random.split(key, 17)
    f32 = jnp.float32

    def w(k, shape, fan_in):
        return jax.random.normal(k, shape, f32) * (fan_in ** -0.5)

    def gain(k, shape):
        return 1.0 + 0.02 * jax.random.normal(k, shape, f32)

    return {
        "x": jax.random.normal(ks[0], (BATCH, SEQ, D_MODEL), f32),
        "meta_tokens": jax.random.normal(ks[1], (N_META, D_MODEL), f32),
        "norm_mix_g": gain(ks[2], (DEPTH, D_MODEL)),
        "w_in": w(ks[3], (DEPTH, D_MODEL, IN_WIDTH), D_MODEL),
        "lb_raw": 0.5 * jax.random.normal(ks[4], (DEPTH + 1, HGRN_WIDTH), f32),
        "hgrn_norm_g": gain(ks[5], (DEPTH, HGRN_DV)),
        "pool_w": w(ks[6], (DEPTH, POOL_GROUPS, POOL_GROUP_DIM, POOL_GROUP_DIM), POOL_GROUP_DIM),
        "pool_scale": gain(ks[7], (DEPTH, POOL_WIDTH)),
        "w_branch_a": w(ks[8], (DEPTH, HGRN_HEADS * HGRN_DV, D_MODEL), HGRN_HEADS * HGRN_DV),
        "w_branch_b": w(ks[9], (DEPTH, POOL_WIDTH, D_MODEL), POOL_WIDTH),
        "w_out": w(ks[10], (DEPTH, D_MODEL, D_MODEL), D_MODEL),
        "norm_ffn_g": gain(ks[11], (DEPTH, D_MODEL)),
        "w_ffn_gate": w(ks[12], (DEPTH, D_MODEL, D_FF), D_MODEL),
        "w_ffn_up": w(ks[13], (DEPTH, D_MODEL, D_FF), D_MODEL),
        "w_ffn_down": w(ks[14], (DEPTH, D_FF, D_MODEL), D_FF),
        "norm_final_g": gain(ks[15], (D_MODEL,)),
    }


def reference(x, meta_tokens, norm_mix_g, w_in, lb_raw, hgrn_norm_g, pool_w, pool_scale,
              w_branch_a, w_branch_b, w_out, norm_ffn_g, w_ffn_gate, w_ffn_up, w_ffn_down,
              norm_final_g):
    f32 = jnp.float32
    B = x.shape[0]
    meta = jnp.broadcast_to(meta_tokens[None].astype(x.dtype), (B, N_META, D_MODEL))
    h = jnp.concatenate([meta, x], axis=1)
    L = h.shape[1]
    pad = CHUNK - N_META
    lb_all = jnp.cumsum(jax.nn.softmax(lb_raw.astype(f32), axis=0), axis=0)

    def front_pad_heads(t):
        return jnp.pad(t, ((0, 0), (pad, 0), (0, 0))).reshape(B, pad + L, HGRN_HEADS, -1)

    for l in range(DEPTH):
        n = rmsnorm(h, norm_mix_g[l])
        proj = n @ w_in[l]
        zq, zf, zi, zog, zpool, zga, zgb = jnp.split(proj, SPLIT_POINTS, axis=-1)

        lb = lb_all[l]
        q = jax.nn.silu(zq.astype(f32))
        f = lb + (1.0 - lb) * jax.nn.sigmoid(zf.astype(f32))
        k = 1.0 - f
        log_f = jnp.log(f)
        v = zi.astype(f32)
        o = hgrn2_chunked(front_pad_heads(q), front_pad_heads(k),
                          front_pad_heads(v), front_pad_heads(log_f))[:, pad:]
        o = rmsnorm(o, hgrn_norm_g[l]).reshape(B, L, HGRN_HEADS * HGRN_DV)
        y_a = (o * jax.nn.silu(zog.astype(f32))).astype(h.dtype)

        pooled = causal_multiscale_pool(zpool)
        y_b = jnp.einsum('blgc,gcd->blgd', pooled, pool_w[l].astype(f32)).reshape(B, L, POOL_WIDTH)
        y_b = (y_b * pool_scale[l].astype(f32)).astype(h.dtype)

        merged = (jax.nn.sigmoid(zga) * (y_a @ w_branch_a[l])
                  + jax.nn.sigmoid(zgb) * (y_b @ w_branch_b[l]))
        h = h + merged @ w_out[l]

        n2 = rmsnorm(h, norm_ffn_g[l])
        h = h + (jax.nn.silu(n2 @ w_ffn_gate[l]) * (n2 @ w_ffn_up[l])) @ w_ffn_down[l]

    return rmsnorm(h, norm_final_g)[:, N_META:]
```

```python
import numpy as np
import concourse.bass as bass
import concourse.mybir as mybir
from concourse.bass_utils import run_bass_kernel_spmd

F32 = mybir.dt.float32
BF16 = mybir.dt.bfloat16
AF = mybir.ActivationFunctionType
import os as _os
SILU = AF.Tanh if _os.environ.get('K_NOSILU') else AF.Silu
ALU = mybir.AluOpType
AX = mybir.AxisListType

NCORES = 8
D = 1024
H = 8
import os
NTOK = int(os.environ.get('K_NTOK', '2048'))
T = 512
NG = NTOK // T
CH = 64
NHALO = 16
DFF = 2816
INW = 6656
Q0, F0, I0, OG0, PL0, GA0, GB0 = 0, 1024, 2048, 3072, 4096, 4608, 5632
EPS = 1e-6
NCV = 56
NPRED = 3
SB_BASE = 16512
SB_LIMIT = 229376
NSLOT = 3
PREFETCH = 2


class Sem:
    def __init__(self, h):
        self.h = h
        self.count = 0


class Buf:
    __slots__ = ("name", "writer", "readers")

    def __init__(self, name=""):
        self.name = name
        self.writer = None
        self.readers = {}


class Sched:
    def __init__(self, nc):
        self.nc = nc
        self.eng = {'pe': nc.tensor, 'act': nc.scalar, 'dve': nc.vector, 'pool': nc.gpsimd, 'sp': nc.sync}
        self.esem = {e: Sem(nc.alloc_semaphore(f"s_{e}")) for e in self.eng}
        self.known = {e: {} for e in self.eng}

    def newsem(self, name):
        return Sem(self.nc.alloc_semaphore(name))

    def deps(self, e, reads, writes):
        deps = {}
        pes = self.esem['pe']

        def add(s, v):
            if e == 'pe' and s is pes:
                return
            if deps.get(s, 0) < v:
                deps[s] = v
        for b in reads:
            if b.writer is not None:
                add(*b.writer)
        for b in writes:
            if b.writer is not None:
                add(*b.writer)
            for s, v in b.readers.items():
                add(s, v)
        kn = self.known[e]
        for s, v in deps.items():
            if kn.get(s, 0) >= v:
                continue
            self.eng[e].wait_ge(s.h, v)
            kn[s] = v

    def mark(self, tag, reads, writes):
        s, v = tag
        for b in reads:
            if b.readers.get(s, 0) < v:
                b.readers[s] = v
        for b in writes:
            b.writer = tag
            b.readers = {}

    def op(self, e, fn, reads=(), writes=(), signal=True):
        self.deps(e, reads, writes)
        ins = fn(self.eng[e])
        s = self.esem[e]
        if signal:
            s.count += 1
            ins.then_inc(s.h, 1)
            tag = (s, s.count)
        else:
            tag = (s, s.count + 1)
        self.mark(tag, reads, writes)
        return ins

    def dma(self, e, out, in_, sem, reads=(), writes=(), **kw):
        self.deps(e, reads, writes)
        ins = self.eng[e].dma_start(out=out, in_=in_, **kw)
        sem.count += 16
        ins.then_inc(sem.h, 16)
        self.mark((sem, sem.count), reads, writes)
        return ins

    def wait_all(self, e, sems):
        for s in sems:
            if s.count > 0 and self.known[e].get(s, 0) < s.count:
                self.eng[e].wait_ge(s.h, s.count)
                self.known[e][s] = s.count


def _dtsize(dt):
    return 4 if dt == F32 else 2


class Arena:
    def __init__(self, nc):
        self.nc = nc
        self.top = SB_BASE
        self.n = 0

    def at(self, shape, dt, addr):
        self.n += 1
        return self.nc.alloc_sbuf_tensor_at(f"t{self.n}", list(shape), dt, offset=addr)

    def take(self, shape, dt):
        nbytes = int(np.prod(shape[1:])) * _dtsize(dt)
        nbytes = (nbytes + 63) // 64 * 64
        addr = self.top
        self.top += nbytes
        assert self.top <= SB_LIMIT, f"SBUF overflow {self.top}"
        return self.at(shape, dt, addr), addr


def weight_plan(mode="fused"):
    plan = []
    if mode == "R":
        plan.append(("w_in", 0, 8, PL0, 512))
    na = {"B": 0, "R": -1}.get(mode, NG)
    for g in range(-1, na):
        for half in range(2):
            plan.append(("w_in", 0, 8, F0 + half * 512, 512))
            plan.append(("w_in", 0, 8, I0 + half * 512, 512))
        if g == -1:
            plan.append(("w_in", 0, 8, PL0, 512))
    for g in range(NG if mode != "A" else 0):
        for half in range(2):
            plan.append(("w_in", 0, 8, F0 + half * 512, 512))
            plan.append(("w_in", 0, 8, Q0 + half * 512, 512))
            plan.append(("w_in", 0, 8, OG0 + half * 512, 512))
            plan.append(("w_in", 0, 8, I0 + half * 512, 512))
        plan.append(("w_in", 0, 8, PL0, 512))
        for dh in range(2):
            plan.append(("w_in", 0, 8, GA0 + dh * 512, 512))
            plan.append(("w_in", 0, 8, GB0 + dh * 512, 512))
            plan.append(("w_ba", 0, 8, dh * 512, 512))
            plan.append(("w_bb", 0, 4, dh * 512, 512))
        for half in range(2):
            plan.append(("w_out", 0, 8, half * 512, 512))
        for fblk in range(6):
            nc_ = 512 if fblk < 5 else 256
            plan.append(("w_g", 0, 8, fblk * 512, nc_))
            plan.append(("w_u", 0, 8, fblk * 512, nc_))
        for half in range(2):
            for kc0, nk in ((0, 8), (8, 8), (16, 6)):
                plan.append(("w_d", kc0, nk, half * 512, 512))
    return plan


def build_program(mode="fused"):
    nc = bass.Bass("TRN2", target_bir_lowering=False)
    S = Sched(nc)
    AR = Arena(nc)

    def din(name, shape):
        return nc.dram_tensor(name, list(shape), F32, kind="ExternalInput").ap()

    xm = din("xm", [NTOK, D])
    xh = din("xh", [NHALO, D])
    if mode == "R":
        xmeta = din("xmeta", [NHALO, D])
        xp = din("xp", [NPRED * NTOK, D])
    cvec = din("cvec", [128, NCV])
    gvec = din("gvec", [3, D])
    wd = {
        "w_in": din("w_in", [D, INW]), "w_ba": din("w_ba", [D, D]), "w_bb": din("w_bb", [512, D]),
        "w_out": din("w_out", [D, D]), "w_g": din("w_g", [D, DFF]), "w_u": din("w_u", [D, DFF]),
        "w_d": din("w_d", [DFF, D]),
    }
    pool_w = din("pool_w", [4, 128, 128])
    wshape = {"w_in": [D, INW], "w_ba": [D, D], "w_bb": [512, D], "w_out": [D, D], "w_g": [D, DFF], "w_u": [D, DFF], "w_d": [DFF, D]}
    wsc = {}
    if mode == "A":
        su = nc.dram_tensor("su", [128, 1032], F32, kind="ExternalOutput").ap()
        y = None
    else:
        y = nc.dram_tensor("y", [NTOK, D], F32, kind="ExternalOutput").ap()
    if mode == "B":
        gall = din("gall", [NCORES * 128, 1032])
    if mode == "fused":
        gin = nc.dram_tensor("gin", [128, 1032], F32)
        gout = nc.dram_tensor("gout", [NCORES * 128, 1032], F32)

    def T_(shape, dt):
        return AR.take(shape, dt)[0]

    a_ws0 = AR.top
    wslot = [T_([128, 8, 512], BF16) for _ in range(NSLOT)]; b_ws = [Buf() for _ in range(NSLOT)]
    xt2 = AR.at([128, 4, D], F32, a_ws0 + 8192); b_xt2 = [[Buf(), Buf()] for _ in range(4)]
    cv = T_([128, NCV], F32); b_cv = Buf("cv")
    cst = T_([128, 64], F32); b_cst = Buf("cst")
    gbc = [T_([128, D], F32) for _ in range(3)]; b_gbc = [Buf() for _ in range(3)]
    pw = T_([128, 4, 128], BF16); b_pw = Buf("pw")
    ident = T_([128, 128], BF16); b_ident = Buf("ident")
    ones32 = T_([128, 128], F32); b_ones = Buf("ones")
    rmask = T_([128, T], F32); b_rmask = Buf("rmask")
    cmask = T_([128, 256], F32); b_cmask = Buf("cmask")
    xt = T_([128, 4, D], F32); b_xt = [[Buf(), Buf()] for _ in range(4)]
    junk = T_([128, D], BF16); b_junk = Buf("junk")
    ntm = [T_([128, D], BF16) for _ in range(2)]; b_ntm = [Buf(), Buf()]
    nT = T_([128, 8, T], BF16); b_nT = [Buf() for _ in range(4)]
    stat = T_([128, 32], F32); b_stat = Buf("stat")
    tf, a_tf = AR.take([128, 8, T], F32); b_tf = [Buf() for _ in range(8)]
    qT, a_qT = AR.take([128, 8, T], F32); b_qT = [Buf() for _ in range(8)]
    sog, a_sog = AR.take([128, 8, T], BF16); b_sog = [Buf() for _ in range(8)]
    V = T_([128, 4, D], BF16); b_V = [[Buf(), Buf()] for _ in range(4)]
    lf, a_tmp = AR.take([128, T], F32); b_lf = Buf("lf")
    bS = T_([128, T], F32); b_bS = Buf("bS")
    Epos = T_([128, T], F32); b_Epos = Buf("Epos")
    QdT = T_([128, T], BF16); b_QdT = Buf("QdT")
    KbT = T_([128, T], BF16); b_KbT = Buf("KbT")
    QbT = T_([128, T], BF16); b_QbT = Buf("QbT")
    Kbtm = T_([128, 4, 128], BF16); b_Kbtm = Buf("Kbtm")
    ATs = T_([128, 256], BF16); b_AT = Buf("AT")
    Sch = T_([128, 9, 128], F32); b_Sch = [Buf() for _ in range(9)]
    Sbf = T_([128, 8, 128], BF16); b_Sbf = Buf("Sbf")
    eb = T_([128, 16], F32); b_eb = Buf("eb")
    a_tmp_end = AR.top
    Sst = T_([128, 8, 128], F32); b_Sst = [Buf() for _ in range(8)]
    Bsum = T_([128, 8], F32); b_Bsum = Buf("Bsum")
    b_Sh = Buf("Sh")
    Dh = T_([128, 8], F32); b_Dh = Buf("Dh")
    yaT = T_([128, 8, T], BF16); b_yaT = [Buf() for _ in range(8)]
    ybT = T_([128, 4, T], BF16); b_ybT = [Buf() for _ in range(4)]
    uT = T_([128, 4, NHALO + T], F32); b_uT = [Buf() for _ in range(4)]
    mergedT, a_mg = AR.take([128, 8, T], BF16); b_mg = [Buf() for _ in range(8)]
    hidT, a_hid = AR.take([128, 22, T], BF16); b_hid = [Buf() for _ in range(22)]
    wstg = [T_([128, 8, 256], F32) for _ in range(2)]; b_stg = [Buf(), Buf()]
    a_stg1 = AR.top - 8192
    a_ts1 = AR.top - 16384
    print("SBUF top", AR.top, "limit", SB_LIMIT)
    sA = AR.at([128, NHALO + T], F32, a_tmp); sB = AR.at([128, NHALO + T], F32, a_tmp + 2176)
    pooledT = AR.at([128, 4, T], BF16, a_tmp + 4352)
    assert a_tmp + 4352 + 4096 <= a_tmp_end
    b_tmp_all = [b_lf, b_bS, b_Epos, b_QdT, b_KbT, b_QbT, b_Kbtm, b_AT, b_Sbf, b_eb] + b_Sch
    tga = AR.at([128, 4, T], F32, a_qT); tgb = AR.at([128, 4, T], F32, a_qT + 8192)
    sg = AR.at([128, 4, T], F32, a_tf)
    xtmp = AR.at([128, 1032], F32, a_hid)
    WR = AR.at([128, 4, 8, 512], BF16, a_mg)
    assert a_mg + 4 * 8 * 512 * 2 <= AR.top and a_hid == a_mg + 8 * T * 2
    b_WR = Buf("WR")
    lf2 = [AR.at([128, T], F32, a_qT + i * 2048) for i in range(3)]
    bS2 = [AR.at([128, T], F32, a_qT + 6144 + i * 2048) for i in range(3)]
    onesT = AR.at([128, T], F32, a_qT + 12288)
    KbT2 = [AR.at([128, T], BF16, a_sog + i * 1024) for i in range(2)]
    Kbtm2 = [AR.at([128, 4, 128], BF16, a_sog + 2048 + i * 1024) for i in range(2)]
    b_lf2 = [Buf() for _ in range(3)]; b_bS2 = [Buf() for _ in range(3)]; b_KbT2 = [Buf(), Buf()]; b_Kbtm2 = [Buf(), Buf()]; b_onesT = Buf()
    b_eb2 = [Buf() for _ in range(4)]
    tfB = AR.at([128, 8, T], F32, a_tmp); b_tfB = [Buf() for _ in range(8)]
    assert a_tmp + 8 * T * 4 <= a_tmp_end
    VB = AR.at([128, 4, D], BF16, a_stg1); b_VB = [[Buf(), Buf()] for _ in range(4)]
    Sh = AR.at([128, 8, 128], F32, a_hid + 4160)
    assert 4160 + 4096 <= 22 * T * 2

    class TS:
        pass
    TS0 = TS()
    TS0.lf, TS0.bS, TS0.Epos, TS0.QdT, TS0.KbT, TS0.QbT, TS0.Kbtm, TS0.AT, TS0.Sch, TS0.Sbf, TS0.eb = lf, bS, Epos, QdT, KbT, QbT, Kbtm, ATs, Sch, Sbf, eb
    TS0.b_lf, TS0.b_bS, TS0.b_Epos, TS0.b_QdT, TS0.b_KbT, TS0.b_QbT, TS0.b_Kbtm, TS0.b_AT, TS0.b_Sch, TS0.b_Sbf, TS0.b_eb = \
        b_lf, b_bS, b_Epos, b_QdT, b_KbT, b_QbT, b_Kbtm, b_AT, b_Sch, b_Sbf, b_eb
    TS0.pc = 0
    TS1 = TS()
    _o = a_ts1
    TS1.lf = AR.at([128, T], F32, _o); TS1.bS = AR.at([128, T], F32, _o + 2048); TS1.Epos = AR.at([128, T], F32, _o + 4096)
    TS1.QdT = AR.at([128, T], BF16, _o + 6144); TS1.KbT = AR.at([128, T], BF16, _o + 7168); TS1.QbT = AR.at([128, T], BF16, _o + 8192)
    TS1.Kbtm = AR.at([128, 4, 128], BF16, _o + 9216); TS1.AT = AR.at([128, 256], BF16, _o + 10240)
    TS1.Sch = AR.at([128, 9, 128], F32, _o + 10752); TS1.Sbf = AR.at([128, 8, 128], BF16, _o + 15360); TS1.eb = AR.at([128, 8], F32, _o + 17408)
    assert _o + 17408 + 32 <= SB_LIMIT - 8, (_o, SB_LIMIT)
    TS1.b_lf, TS1.b_bS, TS1.b_Epos, TS1.b_QdT, TS1.b_KbT, TS1.b_QbT, TS1.b_Kbtm, TS1.b_AT, TS1.b_Sbf, TS1.b_eb = [Buf() for _ in range(10)]
    TS1.b_Sch = [Buf() for _ in range(9)]
    TS1.pc = 256
    b_ts1_all = [TS1.b_lf, TS1.b_bS, TS1.b_Epos, TS1.b_QdT, TS1.b_KbT, TS1.b_QbT, TS1.b_Kbtm, TS1.b_AT, TS1.b_Sbf, TS1.b_eb] + TS1.b_Sch
    NPP = 2
    pp = [nc.alloc_psum_tensor(f"pp{i}", [128, 512], F32) for i in range(NPP)]; b_pp = [Buf() for _ in range(NPP)]
    ptr = nc.alloc_psum_tensor("ptr", [128, 1024], BF16); b_ptr = Buf("ptr")
    pacc5 = [nc.alloc_psum_tensor(f"pa{i}", [128, 512], F32) for i in range(5)]; b_pacc5 = [Buf() for _ in range(5)]
    pa, po0, po1, pu0, pu1 = pacc5
    b_pa, b_po0, b_po1, b_pu0, b_pu1 = b_pacc5
    pacc = pacc5[0:4]; b_pacc = b_pacc5[0:4]
    pp_rr = [0]

    def next_pp():
        i = pp_rr[0] % NPP
        pp_rr[0] += 1
        return pp[i], b_pp[i]

    def act(out, in_, func, reads, writes, scale=1.0, bias=None, accum=None):
        kw = {}
        if bias is not None:
            kw["bias"] = bias
        if accum is not None:
            kw["accum_out"] = accum
        return S.op('act', lambda e: e.activation(out=out, in_=in_, func=func, scale=scale, **kw), reads, writes)

    def tt(out, in0, in1, op, reads, writes):
        return S.op('dve', lambda e: e.tensor_tensor(out=out, in0=in0, in1=in1, op=op), reads, writes)

    def stt(out, in0, scalar, in1, op0, op1, reads, writes):
        return S.op('dve', lambda e: e.scalar_tensor_tensor(out=out, in0=in0, scalar=scalar, in1=in1, op0=op0, op1=op1),
                    reads, writes)

    def ts(out, in0, s1, s2, op0, op1, reads, writes):
        if s2 is None:
            return S.op('dve', lambda e: e.tensor_scalar(out=out, in0=in0, scalar1=s1, scalar2=None, op0=op0), reads, writes)
        return S.op('dve', lambda e: e.tensor_scalar(out=out, in0=in0, scalar1=s1, scalar2=s2, op0=op0, op1=op1), reads, writes)

    def ptt(out, in0, in1, op, reads, writes):
        return S.op('pool', lambda e: e.tensor_tensor(out=out, in0=in0, in1=in1, op=op), reads, writes)

    def pts(out, in0, s1, s2, op0, op1, reads, writes):
        return S.op('pool', lambda e: e.tensor_scalar(out=out, in0=in0, scalar1=s1, scalar2=s2, op0=op0, op1=op1), reads, writes)

    def cp(out, in_, reads, writes):
        return S.op('dve', lambda e: e.tensor_copy(out=out, in_=in_), reads, writes)

    def mm(out, lhsT, rhs, start, stop, reads, writes, signal):
        return S.op('pe', lambda e: e.matmul(out, lhsT=lhsT, rhs=rhs, start=start, stop=stop), reads, writes, signal=signal)

    def tp(out, in_, idn, reads, writes, signal):
        return S.op('pe', lambda e: e.transpose(out=out, in_=in_, identity=idn), reads, writes, signal=signal)

    def retire(new_bufs, old_bufs):
        acc = {}
        for b in old_bufs:
            if b.writer is not None:
                s, v = b.writer
                acc[s] = max(acc.get(s, 0), v)
            for s, v in b.readers.items():
                acc[s] = max(acc.get(s, 0), v)
        for b in new_bufs:
            b.writer = None
            b.readers = dict(acc)

    sem_misc = S.newsem("d_misc")
    sem_x = [S.newsem(f"d_x{j}") for j in range(4)]
    sem_x2 = [S.newsem(f"d_xb{j}") for j in range(4)]
    sem_y = [S.newsem(f"d_y{j}") for j in range(4)]
    sem_w = [S.newsem(f"d_w{i}") for i in range(NSLOT)]
    sem_cc = S.newsem("cc")
    sem_g = S.newsem("d_g")
    sem_pw = S.newsem("d_pw")
    sem_g2 = S.newsem("d_g2")
    sem_wh = [S.newsem(f"d_wh{i}") for i in range(NSLOT)]
    sem_cv = S.newsem("d_cv")
    sem_wr = S.newsem("d_wr")
    sem_stg = [S.newsem("d_stg0"), S.newsem("d_stg1")]

    plan = weight_plan(mode)
    wstate = {"issued": 0, "used": 0}

    post_cc = [False]
    stg_rr = [0]
    b_wsc = Buf("wsc")
    conv = []
    for name_, shp in wshape.items():
        cw = 512 if shp[1] % 512 == 0 else 256
        for r0 in range(0, shp[0], 128):
            conv.append((name_, r0, cw))
    conv_state = [0]

    def conv_issue(n):
        return

    def _conv_issue(n):
        while n > 0 and conv_state[0] < len(conv):
            name_, r0, cw = conv[conv_state[0]]
            src = wd[name_][r0:r0 + 128, :].rearrange("p (a c) -> p a c", c=cw)
            dst = wsc[name_][r0:r0 + 128, :].rearrange("p (a c) -> p a c", c=cw)
            S.dma('pool', dst, src, sem_cv, writes=[])
            conv_state[0] += 1
            n -= 1

    def w_issue_upto(k):
        while wstate["issued"] <= min(k, len(plan) - 1):
            i = wstate["issued"]
            name, kc0, nk, c0, ncols = plan[i]
            sl = i % NSLOT
            if not post_cc[0]:
                src = wd[name].rearrange("(k p) c -> p k c", p=128)[:, kc0:kc0 + nk, c0:c0 + ncols]
                S.dma('pool', wslot[sl][:, 0:nk, 0:ncols], src, sem_w[sl], writes=[b_ws[sl]])
                conv_issue(3)
            else:
                for q in range(0, ncols, 256):
                    k_ = stg_rr[0] % 2
                    stg_rr[0] += 1
                    w_ = min(256, ncols - q)
                    src = wd[name].rearrange("(k p) c -> p k c", p=128)[:, kc0:kc0 + nk, c0 + q:c0 + q + w_]
                    S.dma('sp', wstg[k_][:, 0:nk, 0:w_], src, sem_stg[k_], writes=[b_stg[k_]])
                    S.op('pool', lambda e: e.tensor_copy(out=wslot[sl][:, 0:nk, q:q + w_], in_=wstg[k_][:, 0:nk, 0:w_]),
                         [b_stg[k_]], [b_ws[sl]])
            wstate["issued"] += 1

    hold_prefetch = [mode == "R"]

    def w_next(name, kc0, nk, c0, ncols):
        i = wstate["used"]
        assert plan[i] == (name, kc0, nk, c0, ncols), (i, plan[i], (name, kc0, nk, c0, ncols))
        w_issue_upto(i if hold_prefetch[0] else i + PREFETCH)
        wstate["used"] += 1
        sl = i % NSLOT
        return wslot[sl], b_ws[sl]

    S.dma('sp', cv[:], cvec, sem_misc, writes=[b_cv])
    for i in range(3):
        S.dma('sp', gbc[i][:], gvec[i].partition_broadcast(128), sem_misc, writes=[b_gbc[i]])
    S.dma('pool', pw[:], pool_w.rearrange("g c d -> c g d"), sem_pw, writes=[b_pw])
    for b in [b_cv] + b_gbc:
        b.writer = (sem_misc, sem_misc.count)
    w_issue_upto(0 if mode == "R" else PREFETCH - 1)
    S.op('pool', lambda e: e.memset(ident[:], 1.0), writes=[b_ident])
    S.op('pool', lambda e: e.affine_select(out=ident[:], in_=ident[:], pattern=[[-1, 128]], compare_op=ALU.is_equal,
                                           fill=0.0, base=0, channel_multiplier=1), reads=[b_ident], writes=[b_ident])
    S.op('pool', lambda e: e.memset(cmask[:], 1.0), writes=[b_cmask])
    for hp in range(2):
        S.op('pool', lambda e: e.affine_select(out=cmask[hp * 64:(hp + 1) * 64, :], in_=cmask[hp * 64:(hp + 1) * 64, :],
                                               pattern=[[0, 4], [1, 64]], compare_op=ALU.is_ge, fill=0.0, base=0,
                                               channel_multiplier=-1), reads=[b_cmask], writes=[b_cmask])
    S.op('dve', lambda e: e.memset(ones32[:], 1.0), writes=[b_ones])
    S.op('dve', lambda e: e.memset(rmask[:], 1.0), writes=[b_rmask])
    S.op('dve', lambda e: e.memset(rmask[:].rearrange("p (c t) -> p c t", t=CH)[:, :, 0:1], 0.0), writes=[b_rmask])
    S.op('dve', lambda e: e.memset(Sst[:], 0.0), writes=b_Sst)
    S.op('dve', lambda e: e.memset(Bsum[:], 0.0), writes=[b_Bsum])
    C0, C1, NC1, CEPS, OMM = 0, 8, 16, 24, 25
    CV_LB0, CV_LB1, CV_GH, CV_PS, CV_FLAG, CV_M = 0, 8, 16, 17, 21, 22
    S.op('dve', lambda e: e.memset(cst[:, CEPS:CEPS + 1], EPS), writes=[b_cst])
    tt(cst[:, 33:41], cv[:, CV_LB0:CV_LB0 + 8], cv[:, CV_LB1:CV_LB1 + 8], ALU.subtract, [b_cv], [b_cst])
    act(cst[:, 33:41], cst[:, 33:41], AF.Tanh, [b_cst], [b_cst], scale=0.5)
    ts(cst[:, C0:C0 + 8], cst[:, 33:41], 0.25, 0.75, ALU.mult, ALU.add, [b_cst], [b_cst])
    ts(cst[:, C1:C1 + 8], cst[:, 33:41], -0.25, 0.25, ALU.mult, ALU.add, [b_cst], [b_cst])
    ts(cst[:, NC1:NC1 + 8], cst[:, 33:41], 0.25, -0.25, ALU.mult, ALU.add, [b_cst], [b_cst])
    ts(cst[:, OMM:OMM + 8], cv[:, CV_M:CV_M + 8], -1.0, 1.0, ALU.mult, ALU.add, [b_cv], [b_cst])

    class Grp:
        pass

    def main_group(g):
        G = Grp()
        G.ntok = T
        G.tiles = [128] * 4
        G.tcol = [j * 128 for j in range(4)]
        G.chunks = [(c // 2, (c % 2) * 64, CH, c * CH) for c in range(8)]
        G.halo = False
        G.g = g
        G.flagcol = None
        G.src = xm
        G.xt, G.b_xt = xt, b_xt
        G.tf, G.b_tf, G.V, G.b_V = tf, b_tf, V, b_V
        return G

    def halo_group():
        G = Grp()
        G.ntok = NHALO
        G.tiles = [NHALO]
        G.tcol = [0]
        G.chunks = [(0, 0, NHALO, 0)]
        G.halo = True
        G.g = -1
        G.flagcol = CV_FLAG
        G.src = xh
        G.xt, G.b_xt = xt, b_xt
        G.tf, G.b_tf, G.V, G.b_V = tf, b_tf, V, b_V
        return G

    ntm_rr = [0]
    hook = [lambda name: None]

    def load_x(G):
        if G.halo:
            S.dma('sp', G.xt[0:NHALO, 0, :], G.src, sem_x[0], writes=G.b_xt[0])
        else:
            sx = sem_x if G.xt is xt else sem_x2
            for j in range(4):
                r0 = G.g * T + j * 128
                S.dma('sp', G.xt[:, j, :], G.src[r0:r0 + 128, :], sx[j], writes=G.b_xt[j])

    def norm_T(G, gi, dstT, b_dstT):
        nt = len(G.tiles)
        pc0 = G.tiles[0]
        for j, pc in enumerate(G.tiles):
            act(junk[0:pc, :], G.xt[0:pc, j, :], AF.Square, G.b_xt[j], [b_junk, b_stat], accum=stat[0:pc, j:j + 1])
        act(stat[0:pc0, 8:8 + nt], stat[0:pc0, 0:nt], AF.Ln, [b_stat, b_cst], [b_stat], scale=1.0 / D, bias=cst[0:pc0, CEPS:CEPS + 1])
        act(stat[0:pc0, 8:8 + nt], stat[0:pc0, 8:8 + nt], AF.Exp, [b_stat], [b_stat], scale=-0.5)
        for j, pc in enumerate(G.tiles):
            k = ntm_rr[0] % 2
            ntm_rr[0] += 1
            stt(ntm[k][0:pc, :], G.xt[0:pc, j, :], stat[0:pc, 8 + j:9 + j], gbc[gi][0:pc, :], ALU.mult, ALU.mult,
                G.b_xt[j] + [b_stat, b_gbc[gi]], [b_ntm[k]])
            for kc in range(8):
                tp(ptr[:, kc * 128:kc * 128 + pc], ntm[k][0:pc, kc * 128:(kc + 1) * 128], ident[0:pc, 0:pc],
                   [b_ntm[k], b_ident], [b_ptr], signal=(kc == 7))
            c0 = G.tcol[j]
            cp(dstT[:, :, c0:c0 + pc], ptr[:].rearrange("p (k t) -> p k t", t=128)[:, :, 0:pc], [b_ptr], [b_dstT[j]])

    def proj_fm(ws, bws, wc0, nk, rhsT, b_rhs, n, ps, bps):
        for k in range(nk):
            mm(ps[:, 0:n], ws[:, k, wc0:wc0 + 128], rhsT[:, k, 0:n], k == 0, k == nk - 1, [bws] + b_rhs, [bps], signal=(k == nk - 1))

    def rescan_units(G):
        units = []
        nb = b_nT
        for half in range(2):
            for hh in range(4):
                def u_f(half=half, hh=hh):
                    h = half * 4 + hh
                    ps, bps = next_pp()
                    proj_fm(WR[:, 2 * half], b_WR, hh * 128, 8, nT, nb, T, ps, bps)
                    act(G.tf[:, h, :], ps[:, :], AF.Tanh, [bps], [G.b_tf[h]], scale=0.5)
                units.append(u_f)
            for j in range(4):
                def u_i(half=half, j=j):
                    ps, bps = next_pp()
                    for k in range(8):
                        mm(ps[:, :], nT[:, k, j * 128:(j + 1) * 128], WR[:, 2 * half + 1, k, :], k == 0, k == 7, [b_WR, b_nT[j]], [bps], signal=(k == 7))
                    ts(G.V[:, j, half * 512:(half + 1) * 512], ps[:, :], cv[:, G.flagcol:G.flagcol + 1], None, ALU.mult, None,
                       [bps, b_cv], [G.b_V[j][half]])
                units.append(u_i)
        return units

    def f_proj(G, half, with_q, resident=False):
        n = G.ntok
        nb = b_nT[0:len(G.tiles)]
        if resident:
            ws, bws = WR[:, 2 * half], b_WR
        else:
            ws, bws = w_next("w_in", 0, 8, F0 + half * 512, 512)
        for hh in range(4):
            h = half * 4 + hh
            ps, bps = next_pp()
            proj_fm(ws, bws, hh * 128, 8, nT, nb, n, ps, bps)
            act(tf[:, h, 0:n], ps[:, 0:n], AF.Tanh, [bps], [b_tf[h]], scale=0.5)
        if with_q:
            hook[0]('Bf_f%d' % half)
            ws, bws = w_next("w_in", 0, 8, Q0 + half * 512, 512)
            for hh in range(4):
                h = half * 4 + hh
                ps, bps = next_pp()
                proj_fm(ws, bws, hh * 128, 8, nT, nb, n, ps, bps)
                act(qT[:, h, 0:n], ps[:, 0:n], SILU, [bps], [b_qT[h]])
            hook[0]('Bf_q%d' % half)
            ws, bws = w_next("w_in", 0, 8, OG0 + half * 512, 512)
            for hh in range(4):
                h = half * 4 + hh
                ps, bps = next_pp()
                proj_fm(ws, bws, hh * 128, 8, nT, nb, n, ps, bps)
                act(sog[:, h, 0:n], ps[:, 0:n], SILU, [bps], [b_sog[h]])
        if with_q: hook[0]('Bf_og%d' % half)
        if resident:
            ws, bws = WR[:, 2 * half + 1], b_WR
        else:
            ws, bws = w_next("w_in", 0, 8, I0 + half * 512, 512)
        for j, pc in enumerate(G.tiles):
            ps, bps = next_pp()
            c0 = G.tcol[j]
            for k in range(8):
                mm(ps[0:pc, :], nT[:, k, c0:c0 + pc], ws[:, k, :], k == 0, k == 7, [bws, b_nT[j]], [bps], signal=(k == 7))
            if G.flagcol is not None:
                ts(V[0:pc, j, half * 512:(half + 1) * 512], ps[0:pc, :], cv[0:pc, G.flagcol:G.flagcol + 1], None, ALU.mult, None,
                   [bps, b_cv], [b_V[j][half]])
            else:
                cp(V[0:pc, j, half * 512:(half + 1) * 512], ps[0:pc, :], [bps], [b_V[j][half]])

    def head_state(G, h, phaseB):
        n = G.ntok
        nch = len(G.chunks)
        half = h // 4
        act(lf[:, 0:n], tf[:, h, 0:n], AF.Ln, [b_tf[h], b_cst], [b_lf], scale=cst[:, C1 + h:C1 + h + 1], bias=cst[:, C0 + h:C0 + h + 1])
        if G.flagcol is not None:
            ts(lf[:, 0:n], lf[:, 0:n], cv[:, G.flagcol:G.flagcol + 1], None, ALU.mult, None, [b_lf, b_cv], [b_lf])
        S.op('dve', lambda e: e.tensor_tensor_scan(out=bS[:, 0:n], data0=rmask[:, 0:n], data1=lf[:, 0:n], initial=0.0,
                                                   op0=ALU.mult, op1=ALU.add), [b_lf, b_rmask], [b_bS])
        if G.halo:
            blast = bS[:, n - 1:n]
            blast_bc = blast.to_broadcast([128, n])
            cview = lf[:, 0:n]
            bview = bS[:, 0:n]
        else:
            b3 = bS[:].rearrange("p (c t) -> p c t", t=CH)
            blast = b3[:, :, CH - 1]
            blast_bc = b3[:, :, CH - 1:CH].to_broadcast([128, nch, CH])
            cview = lf[:].rearrange("p (c t) -> p c t", t=CH)
            bview = b3
        if not G.halo: hook[0]('H_scan')
        if not phaseB:
            S.op('dve', lambda e: e.tensor_reduce(out=stat[:, 16:17], in_=blast, axis=AX.X, op=ALU.add), [b_bS], [b_stat])
            tt(Bsum[:, h:h + 1], Bsum[:, h:h + 1], stat[:, 16:17], ALU.add, [b_Bsum, b_stat], [b_Bsum])
        if not G.halo: hook[0]('H_red')
        act(eb[:, 0:nch], blast, AF.Exp, [b_bS], [b_eb])
        if not G.halo: hook[0]('H_eb')
        tt(cview, bview, blast_bc, ALU.subtract, [b_bS], [b_lf])
        if not G.halo: hook[0]('H_c')
        if phaseB:
            act(Epos[:, 0:n], lf[:, 0:n], AF.Exp, [b_lf], [b_Epos])
            act(bS[:, 0:n], bS[:, 0:n], AF.Exp, [b_bS], [b_bS])
        act(lf[:, 0:n], lf[:, 0:n], AF.Exp, [b_lf], [b_lf], scale=-1.0)
        ts(tf[:, h, 0:n], tf[:, h, 0:n], cst[:, NC1 + h:NC1 + h + 1], cst[:, C1 + h:C1 + h + 1], ALU.mult, ALU.add,
           [b_tf[h], b_cst], [b_tf[h]])
        tt(KbT[:, 0:n], tf[:, h, 0:n], lf[:, 0:n], ALU.mult, [b_tf[h], b_lf], [b_KbT])
        if phaseB:
            tt(QdT[:, 0:n], qT[:, h, 0:n], Epos[:, 0:n], ALU.mult, [b_qT[h], b_Epos], [b_QdT])
            tt(QbT[:, 0:n], qT[:, h, 0:n], bS[:, 0:n], ALU.mult, [b_qT[h], b_bS], [b_QbT])
        if not G.halo: hook[0]('H_kb')
        nt = len(G.tiles)
        for j, pc in enumerate(G.tiles):
            c0 = G.tcol[j]
            tp(ptr[0:pc, j * 128:(j + 1) * 128], KbT[:, c0:c0 + pc], ident[:, :], [b_KbT, b_ident], [b_ptr], signal=(j == nt - 1))
        pc0 = G.tiles[0]
        cp(Kbtm[0:pc0, 0:nt, :], ptr[0:pc0, 0:nt * 128].rearrange("p (j d) -> p j d", d=128), [b_ptr], [b_Kbtm])
        if not G.halo: hook[0]('H_tp')
        for ci, (j, p0, C, col0) in enumerate(G.chunks):
            pu, bpu = (pu0, b_pu0) if ci % 2 == 0 else (pu1, b_pu1)
            mm(pu[:, (ci // 2) * 128:(ci // 2 + 1) * 128], Kbtm[p0:p0 + C, j, :], V[p0:p0 + C, j, h * 128:(h + 1) * 128], True, True,
               [b_Kbtm, b_V[j][half]], [bpu], signal=(ci >= nch - 2))
        if not G.halo: hook[0]('H_u')
        cp(Sch[:, 0, :], Sst[:, h, :], [b_Sst[h]], [b_Sch[0]])
        for ci in range(nch):
            pu, bpu = (pu0, b_pu0) if ci % 2 == 0 else (pu1, b_pu1)
            stt(Sch[:, ci + 1, :], Sch[:, ci, :], eb[:, ci:ci + 1], pu[:, (ci // 2) * 128:(ci // 2 + 1) * 128], ALU.mult, ALU.add,
                [b_Sch[ci], b_eb, bpu], [b_Sch[ci + 1]])
        cp(Sst[:, h, :], Sch[:, nch, :], [b_Sch[nch]], [b_Sst[h]])
        if not G.halo: hook[0]('H_chain')

    def hs2_A(G, h):
        q = h % 3
        act(lf2[q][:, :], G.tf[:, h, :], AF.Ln, [G.b_tf[h], b_cst], [b_lf2[q]], scale=cst[:, C1 + h:C1 + h + 1], bias=cst[:, C0 + h:C0 + h + 1])
        pts(G.tf[:, h, :], G.tf[:, h, :], cst[:, NC1 + h:NC1 + h + 1], cst[:, C1 + h:C1 + h + 1], ALU.mult, ALU.add,
            [G.b_tf[h], b_cst], [G.b_tf[h]])

    def hs2_B(G, h):
        q = h % 3
        S.op('dve', lambda e: e.tensor_tensor_scan(out=bS2[q][:, :], data0=onesT[:, :], data1=lf2[q][:, :], initial=0.0,
                                                   op0=ALU.mult, op1=ALU.add), [b_lf2[q], b_onesT], [b_bS2[q]])

    def hs2_C(G, h):
        q = h % 3
        fc = G.flagcol
        blast = bS2[q][:, T - 1:T]
        act(eb[:, 8 + h % 4:9 + h % 4], blast, AF.Exp, [b_bS2[q], b_cv], [b_eb2[h % 4]], scale=cv[:, fc:fc + 1])
        act(lf2[q][:, :], bS2[q][:, :], AF.Exp, [b_bS2[q]], [b_lf2[q]], scale=-1.0, bias=blast)

    def hs2_D1(G, h):
        q = h % 3; p = h % 2
        ptt(KbT2[p][:, :], G.tf[:, h, :], lf2[q][:, :], ALU.mult, [G.b_tf[h], b_lf2[q]], [b_KbT2[p]])

    def hs2_D2(G, h):
        p = h % 2
        for j in range(4):
            tp(ptr[:, j * 128:(j + 1) * 128], KbT2[p][:, j * 128:(j + 1) * 128], ident[:, :], [b_KbT2[p], b_ident], [b_ptr], signal=(j == 3))

    def hs2_D3(G, h):
        p = h % 2
        cp(Kbtm2[p][:, :, :], ptr[:, 0:512].rearrange("p (j d) -> p j d", d=128), [b_ptr], [b_Kbtm2[p]])

    def hs2_D4(G, h):
        p = h % 2
        half = h // 4
        pu, bpu = (pu0, b_pu0) if p == 0 else (pu1, b_pu1)
        for j in range(4):
            mm(pu[:, 0:128], Kbtm2[p][:, j, :], G.V[:, j, h * 128:(h + 1) * 128], j == 0, j == 3, [b_Kbtm2[p], G.b_V[j][half]], [bpu], signal=(j == 3))

    def hs2_D5(G, h):
        q = h % 3; p = h % 2
        pu, bpu = (pu0, b_pu0) if p == 0 else (pu1, b_pu1)
        stt(Sst[:, h, :], Sst[:, h, :], eb[:, 8 + h % 4:9 + h % 4], pu[:, 0:128], ALU.mult, ALU.add, [b_Sst[h], b_eb2[h % 4], bpu], [b_Sst[h]])

    def heads_state2(G, units=()):
        units = list(units)

        def take(n):
            for _ in range(n):
                if units:
                    units.pop(0)()
        hs2_A(G, 0)
        hs2_A(G, 1)
        hs2_B(G, 0)
        for it in range(8 + 2):
            take(2 if it < 8 else 0)
            if it - 1 >= 0 and it - 1 < 8:
                hs2_D2(G, it - 1)
            if it - 2 >= 0 and it - 2 < 8:
                hs2_D4(G, it - 2)
            if it + 2 < 8:
                hs2_A(G, it + 2)
            if it + 1 < 8:
                hs2_B(G, it + 1)
            if it < 8:
                hs2_C(G, it)
                hs2_D1(G, it)
            if it - 1 >= 0 and it - 1 < 8:
                hs2_D3(G, it - 1)
            if it - 2 >= 0 and it - 2 < 8:
                hs2_D5(G, it - 2)
        take(len(units))

    def head_out(G, h):
        half = h // 4
        hook[0]('O_start')
        S.op('act', lambda e: e.activation(out=Sbf[:].rearrange("p c e -> p (c e)"), in_=Sch[:, 0:8, :].rearrange("p c e -> p (c e)"),
                                           func=AF.Copy), b_Sch[0:8], [b_Sbf])
        for ci, (j, p0, C, col0) in enumerate(G.chunks):
            mm(pa[p0:p0 + CH, (ci // 2) * CH:(ci // 2 + 1) * CH], KbT[:, col0:col0 + CH], QdT[:, col0:col0 + CH], True, True,
               [b_KbT, b_QdT], [b_pa], signal=(ci == 7))
        hook[0]('O_at')
        tt(ATs[:, :], pa[:, 0:256], cmask[:, :], ALU.mult, [b_pa, b_cmask], [b_AT])
        hook[0]('O_mask')
        pos = ((po0, b_po0), (po1, b_po1))
        for ci, (j, p0, C, col0) in enumerate(G.chunks):
            po, b_po = pos[ci % 2]
            oc = (ci // 2) * CH
            mm(po[:, oc:oc + CH], V[p0:p0 + CH, j, h * 128:(h + 1) * 128], ATs[p0:p0 + CH, oc:oc + CH],
               True, False, [b_V[j][half], b_AT], [b_po], signal=False)
            mm(po[:, oc:oc + CH], Sbf[:, ci, :], QbT[:, col0:col0 + CH], False, True, [b_Sbf, b_QbT], [b_po], signal=(ci >= 6))

        hook[0]('O_o')

        def par_view(t2d, par):
            return t2d.rearrange("p (c two t) -> p c two t", two=2, t=CH)[:, :, par, :]

        def po_view(par):
            return pos[par][0][:, 0:256].rearrange("p (c t) -> p c t", t=CH)
        for par in range(2):
            act(par_view(Epos[:, :], par), po_view(par), AF.Square, [pos[par][1]], [b_Epos])
        hook[0]('O_sq')
        ps, bps = next_pp()
        mm(ps[:, :], ones32[:, :], Epos[:, :], True, True, [b_ones, b_Epos], [bps], signal=True)
        hook[0]('O_ones')
        act(lf[:, :], ps[:, :], AF.Ln, [bps, b_cst], [b_lf], scale=1.0 / 128, bias=cst[:, CEPS:CEPS + 1])
        act(lf[:, :], lf[:, :], AF.Exp, [b_lf], [b_lf], scale=-0.5)
        for par in range(2):
            tt(par_view(lf[:, :], par), po_view(par), par_view(lf[:, :], par), ALU.mult, [pos[par][1], b_lf], [b_lf])
        stt(yaT[:, h, :], lf[:, :], cv[:, CV_GH:CV_GH + 1], sog[:, h, :], ALU.mult, ALU.mult, [b_lf, b_cv, b_sog[h]], [b_yaT[h]])

    def head_chain(G, h, X):
        half = h // 4
        n = T
        pc = X.pc
        pt = 512 if pc else 0
        b3 = X.bS[:].rearrange("p (c t) -> p c t", t=CH)
        blast = b3[:, :, CH - 1]
        blast_bc = b3[:, :, CH - 1:CH].to_broadcast([128, 8, CH])
        cview = X.lf[:].rearrange("p (c t) -> p c t", t=CH)
        pos = ((po0, b_po0), (po1, b_po1))

        def par_view(t2d, par):
            return t2d.rearrange("p (c two t) -> p c two t", two=2, t=CH)[:, :, par, :]

        def po_view(par):
            return pos[par][0][:, pc:pc + 256].rearrange("p (c t) -> p c t", t=CH)

        def s1():
            act(X.lf[:, :], tf[:, h, :], AF.Ln, [b_tf[h], b_cst], [X.b_lf], scale=cst[:, C1 + h:C1 + h + 1], bias=cst[:, C0 + h:C0 + h + 1])
            pts(tf[:, h, :], tf[:, h, :], cst[:, NC1 + h:NC1 + h + 1], cst[:, C1 + h:C1 + h + 1], ALU.mult, ALU.add,
                [b_tf[h], b_cst], [b_tf[h]])

        def s2():
            S.op('dve', lambda e: e.tensor_tensor_scan(out=X.bS[:, :], data0=rmask[:, :], data1=X.lf[:, :], initial=0.0,
                                                       op0=ALU.mult, op1=ALU.add), [X.b_lf, b_rmask], [X.b_bS])

        def s3():
            act(X.eb[:, 0:8], blast, AF.Exp, [X.b_bS], [X.b_eb])
            tt(cview, b3, blast_bc, ALU.subtract, [X.b_bS], [X.b_lf])

        def s4():
            act(X.Epos[:, :], X.lf[:, :], AF.Exp, [X.b_lf], [X.b_Epos])
            act(X.bS[:, :], X.bS[:, :], AF.Exp, [X.b_bS], [X.b_bS])
            act(X.lf[:, :], X.lf[:, :], AF.Exp, [X.b_lf], [X.b_lf], scale=-1.0)

        def s5():
            tt(X.KbT[:, :], tf[:, h, :], X.lf[:, :], ALU.mult, [b_tf[h], X.b_lf], [X.b_KbT])
            tt(X.QdT[:, :], qT[:, h, :], X.Epos[:, :], ALU.mult, [b_qT[h], X.b_Epos], [X.b_QdT])
            tt(X.QbT[:, :], qT[:, h, :], X.bS[:, :], ALU.mult, [b_qT[h], X.b_bS], [X.b_QbT])

        def s6():
            for j in range(4):
                tp(ptr[:, pt + j * 128:pt + (j + 1) * 128], X.KbT[:, j * 128:(j + 1) * 128], ident[:, :], [X.b_KbT, b_ident], [b_ptr], signal=(j == 3))

        def s7():
            act(X.Kbtm[:].rearrange("p j d -> p (j d)"), ptr[:, pt:pt + 512], AF.Copy, [b_ptr], [X.b_Kbtm])

        def s8():
            for ci, (j, p0, C, col0) in enumerate(G.chunks):
                pu, bpu = (pu0, b_pu0) if ci % 2 == 0 else (pu1, b_pu1)
                mm(pu[:, (ci // 2) * 128:(ci // 2 + 1) * 128], X.Kbtm[p0:p0 + C, j, :], V[p0:p0 + C, j, h * 128:(h + 1) * 128], True, True,
                   [X.b_Kbtm, b_V[j][half]], [bpu], signal=(ci >= 6))

        def s9():
            act(X.Sbf[:, 0, :], Sst[:, h, :], AF.Copy, [b_Sst[h]], [X.b_Sbf])
            for ci in range(8):
                pu, bpu = (pu0, b_pu0) if ci % 2 == 0 else (pu1, b_pu1)
                src, bsrc = (Sst[:, h, :], b_Sst[h]) if ci == 0 else (X.Sch[:, ci, :], X.b_Sch[ci])
                dst, bdst = (Sst[:, h, :], b_Sst[h]) if ci == 7 else (X.Sch[:, ci + 1, :], X.b_Sch[ci + 1])
                stt(dst, src, X.eb[:, ci:ci + 1], pu[:, (ci // 2) * 128:(ci // 2 + 1) * 128], ALU.mult, ALU.add,
                    [bsrc, X.b_eb, bpu], [bdst])

        def s10():
            S.op('act', lambda e: e.activation(out=X.Sbf[:, 1:8, :].rearrange("p c e -> p (c e)"), in_=X.Sch[:, 1:8, :].rearrange("p c e -> p (c e)"),
                                               func=AF.Copy), X.b_Sch[1:8], [X.b_Sbf])

        def s11():
            for ci, (j, p0, C, col0) in enumerate(G.chunks):
                mm(pa[p0:p0 + CH, pc + (ci // 2) * CH:pc + (ci // 2 + 1) * CH], X.KbT[:, col0:col0 + CH], X.QdT[:, col0:col0 + CH], True, True,
                   [X.b_KbT, X.b_QdT], [b_pa], signal=(ci == 7))

        def s12():
            tt(X.AT[:, :], pa[:, pc:pc + 256], cmask[:, :], ALU.mult, [b_pa, b_cmask], [X.b_AT])

        def s13():
            for ci, (j, p0, C, col0) in enumerate(G.chunks):
                po, b_po = pos[ci % 2]
                oc = (ci // 2) * CH
                mm(po[:, pc + oc:pc + oc + CH], V[p0:p0 + CH, j, h * 128:(h + 1) * 128], X.AT[p0:p0 + CH, oc:oc + CH],
                   True, False, [b_V[j][half], X.b_AT], [b_po], signal=False)
                mm(po[:, pc + oc:pc + oc + CH], X.Sbf[:, ci, :], X.QbT[:, col0:col0 + CH], False, True, [X.b_Sbf, X.b_QbT], [b_po], signal=(ci >= 6))

        def s14():
            for par in range(2):
                act(par_view(X.Epos[:, :], par), po_view(par), AF.Square, [pos[par][1]], [X.b_Epos])

        def s15():
            ps, bps = next_pp()
            X.ps, X.bps = ps, bps
            mm(ps[:, :], ones32[:, :], X.Epos[:, :], True, True, [b_ones, X.b_Epos], [bps], signal=True)

        def s16():
            act(X.lf[:, :], X.ps[:, :], AF.Ln, [X.bps, b_cst], [X.b_lf], scale=1.0 / 128, bias=cst[:, CEPS:CEPS + 1])
            act(X.lf[:, :], X.lf[:, :], AF.Exp, [X.b_lf], [X.b_lf], scale=-0.5)

        def s17():
            for par in range(2):
                tt(par_view(X.lf[:, :], par), po_view(par), par_view(X.lf[:, :], par), ALU.mult, [pos[par][1], X.b_lf], [X.b_lf])
            stt(yaT[:, h, :], X.lf[:, :], cv[:, CV_GH:CV_GH + 1], sog[:, h, :], ALU.mult, ALU.mult, [X.b_lf, b_cv, b_sog[h]], [b_yaT[h]])
        return [s1, s2, s3, s4, s5, s6, s7, s8, s9, s10, s11, s12, s13, s14, s15, s16, s17]

    def heads_phase_b(G):
        for hp in range(4):
            ca = head_chain(G, 2 * hp, TS0)
            cb = head_chain(G, 2 * hp + 1, TS1)
            ca[0]()
            for k in range(len(ca)):
                if k + 1 < len(ca):
                    ca[k + 1]()
                cb[k]()

    def pool_proj(G, col_off):
        n = G.ntok
        nb = b_nT[0:len(G.tiles)]
        ws, bws = w_next("w_in", 0, 8, PL0, 512)
        for gi in range(4):
            ps, bps = next_pp()
            proj_fm(ws, bws, gi * 128, 8, nT, nb, n, ps, bps)
            cp(uT[:, gi, col_off:col_off + n], ps[:, 0:n], [bps], [b_uT[gi]])

    def pool_branch(G):
        L = NHALO + T
        b_pool = [Buf(), Buf(), Buf()]
        retire(b_pool, b_tmp_all)
        b_sA, b_sB, b_pl = b_pool
        for gi in range(4):
            w = 2 ** (gi + 1)
            cur, bcur = uT[:, gi, :], b_uT[gi]
            lo = 0
            bufs = [(sA, b_sA), (sB, b_sB)]
            st = 1
            k = 0
            while st < w:
                nxt, bn = bufs[k % 2]
                lo2 = lo + st
                tt(nxt[:, lo2:L], cur[:, lo2:L], cur[:, lo2 - st:L - st], ALU.add, [bcur], [bn])
                cur, bcur = nxt[:, :], bn
                lo = lo2
                st *= 2
                k += 1
            stt(pooledT[:, gi, :], cur[:, NHALO:L], 1.0 / w, uT[:, gi, NHALO:L], ALU.mult, ALU.subtract, [bcur, b_uT[gi]], [b_pl])
        for gi in range(4):
            ps, bps = next_pp()
            mm(ps[:, :], pw[:, gi, :], pooledT[:, gi, :], True, True, [b_pw, b_pl], [bps], signal=True)
            ts(ybT[:, gi, :], ps[:, :], cv[:, CV_PS + gi:CV_PS + gi + 1], None, ALU.mult, None, [bps, b_cv], [b_ybT[gi]])
            cp(uT[:, gi, 0:NHALO], uT[:, gi, T:T + NHALO], [b_uT[gi]], [b_uT[gi]])
        retire(b_tmp_all, b_pool)

    def merge_branches(G):
        b_tga = [Buf() for _ in range(4)]
        b_tgb = [Buf() for _ in range(4)]
        retire(b_tga + b_tgb, b_qT)
        for dh in range(2):
            for (c0, dst, bd) in ((GA0, tga, b_tga), (GB0, tgb, b_tgb)):
                ws, bws = w_next("w_in", 0, 8, c0 + dh * 512, 512)
                for i in range(4):
                    ps, bps = next_pp()
                    proj_fm(ws, bws, i * 128, 8, nT, b_nT, T, ps, bps)
                    act(dst[:, i, :], ps[:, :], AF.Tanh, [bps], [bd[i]], scale=0.5)
            ws, bws = w_next("w_ba", 0, 8, dh * 512, 512)
            for i in range(4):
                ps, bps = next_pp()
                for h in range(8):
                    mm(ps[:, :], ws[:, h, i * 128:(i + 1) * 128], yaT[:, h, :], h == 0, h == 7, [bws, b_yaT[h]], [bps], signal=(h == 7))
                stt(tga[:, i, :], tga[:, i, :], 1.0, ps[:, :], ALU.add, ALU.mult, [b_tga[i], bps], [b_tga[i]])
            ws, bws = w_next("w_bb", 0, 4, dh * 512, 512)
            for i in range(4):
                ps, bps = next_pp()
                for gi in range(4):
                    mm(ps[:, :], ws[:, gi, i * 128:(i + 1) * 128], ybT[:, gi, :], gi == 0, gi == 3, [bws, b_ybT[gi]], [bps], signal=(gi == 3))
                stt(tgb[:, i, :], tgb[:, i, :], 1.0, ps[:, :], ALU.add, ALU.mult, [b_tgb[i], bps], [b_tgb[i]])
                tt(mergedT[:, dh * 4 + i, :], tga[:, i, :], tgb[:, i, :], ALU.add, [b_tga[i], b_tgb[i]], [b_mg[dh * 4 + i]])
        retire(b_qT, b_tga + b_tgb)

    def out_proj(G):
        for half in range(2):
            ws, bws = w_next("w_out", 0, 8, half * 512, 512)
            for j in range(4):
                ps, bps = next_pp()
                for k in range(8):
                    mm(ps[:, :], mergedT[:, k, j * 128:(j + 1) * 128], ws[:, k, :], k == 0, k == 7, [bws, b_mg[k]], [bps], signal=(k == 7))
                xs = xt[:, j, half * 512:(half + 1) * 512]
                stt(xs, ps[:, :], 0.5, xs, ALU.mult, ALU.add, [bps, b_xt[j][half]], [b_xt[j][half]])

    def ffn(G):
        b_sg = [Buf() for _ in range(4)]
        retire(b_sg, b_tf)
        for fblk in range(6):
            ncols = 512 if fblk < 5 else 256
            nfb = ncols // 128
            ws, bws = w_next("w_g", 0, 8, fblk * 512, ncols)
            for fb in range(nfb):
                ps, bps = next_pp()
                proj_fm(ws, bws, fb * 128, 8, nT, b_nT, T, ps, bps)
                act(sg[:, fb, :], ps[:, :], SILU, [bps], [b_sg[fb]])
            ws, bws = w_next("w_u", 0, 8, fblk * 512, ncols)
            for fb in range(nfb):
                ps, bps = next_pp()
                proj_fm(ws, bws, fb * 128, 8, nT, b_nT, T, ps, bps)
                tt(hidT[:, fblk * 4 + fb, :], sg[:, fb, :], ps[:, :], ALU.mult, [b_sg[fb], bps], [b_hid[fblk * 4 + fb]])
        retire(b_tf, b_sg)
        for half in range(2):
            for kg, (kc0, nk) in enumerate(((0, 8), (8, 8), (16, 6))):
                ws, bws = w_next("w_d", kc0, nk, half * 512, 512)
                for j in range(4):
                    for k in range(nk):
                        mm(pacc[j][:, :], hidT[:, kc0 + k, j * 128:(j + 1) * 128], ws[:, k, :], kg == 0 and k == 0, kg == 2 and k == nk - 1,
                           [bws, b_hid[kc0 + k]], [b_pacc[j]], signal=(k == nk - 1))
            for j in range(4):
                xs = xt[:, j, half * 512:(half + 1) * 512]
                tt(xs, pacc[j][:, :], xs, ALU.add, [b_pacc[j], b_xt[j][half]], [b_xt[j][half]])

    def final_norm_store(G):
        for j in range(4):
            act(junk[:, :], xt[:, j, :], AF.Square, b_xt[j], [b_junk, b_stat], accum=stat[:, j:j + 1])
        act(stat[:, 8:12], stat[:, 0:4], AF.Ln, [b_stat, b_cst], [b_stat], scale=1.0 / D, bias=cst[:, CEPS:CEPS + 1])
        act(stat[:, 8:12], stat[:, 8:12], AF.Exp, [b_stat], [b_stat], scale=-0.5)
        for j in range(4):
            stt(xt[:, j, :], xt[:, j, :], stat[:, 8 + j:9 + j], gbc[2][:, :], ALU.mult, ALU.mult, b_xt[j] + [b_stat, b_gbc[2]], b_xt[j])
            r0 = G.g * T + j * 128
            S.dma('sp', y[r0:r0 + 128, :], xt[:, j, :], sem_y[j], reads=b_xt[j])

    def phase_a_group(G, do_load=True):
        if do_load:
            load_x(G)
        if not G.halo: hook[0]('A_load')
        norm_T(G, 0, nT, b_nT)
        if not G.halo: hook[0]('A_norm')
        for half in range(2):
            f_proj(G, half, with_q=False, resident=(mode == "R"))
        if not G.halo: hook[0]('A_fproj')
        if G.halo and mode != "R":
            pool_proj(G, 0)
        if mode == "R" and not G.halo:
            heads_state2(G)
        else:
            for h in range(8):
                head_state(G, h, phaseB=False)

    def emit_all(stage):
        hook[0] = stage
        stage('setup')
        if mode == "R":
            for bi, c0_ in enumerate((F0, I0, F0 + 512, I0 + 512)):
                S.dma('pool', WR[:, bi], wd["w_in"].rearrange("(k p) c -> p k c", p=128)[:, :, c0_:c0_ + 512], sem_wr, writes=[b_WR])
            b_WR.writer = (sem_wr, sem_wr.count)
            retire(b_lf2 + b_bS2 + [b_onesT], b_qT)
            retire(b_KbT2 + b_Kbtm2, b_sog)
            S.op('dve', lambda e: e.memset(onesT[:], 1.0), writes=[b_onesT])
            ts(cst[:, 41:53], cv[:, 40:52], -1.0, 1.0, ALU.mult, ALU.add, [b_cv], [b_cst])
            Gm = halo_group()
            Gm.src = xmeta
            phase_a_group(Gm)
            Gh = halo_group()
            load_x(Gh)
            norm_T(Gh, 0, nT, b_nT)
            pool_proj(Gh, 0)
            retire([b for jj in b_xt2 for b in jj], b_ws[1:3])

            def pred_group(gi):
                Gp = main_group(gi)
                Gp.src = xp
                Gp.flagcol = 40 + gi
                if gi % 2 == 1:
                    Gp.xt, Gp.b_xt = xt2, b_xt2
                return Gp
            npg = NPRED * NG
            retire(b_tfB, b_tmp_all)
            retire([b for jj in b_VB for b in jj], b_stg)
            Gs = []
            for gi in range(npg):
                Gp = pred_group(gi)
                if gi % 2 == 1:
                    Gp.tf, Gp.b_tf, Gp.V, Gp.b_V = tfB, b_tfB, VB, b_VB
                Gs.append(Gp)
            load_x(Gs[0])
            if npg > 1:
                load_x(Gs[1])
            norm_T(Gs[0], 0, nT, b_nT)
            for u in rescan_units(Gs[0]):
                u()
            for gi in range(npg):
                if gi + 2 < npg:
                    load_x(Gs[gi + 2])
                if gi + 1 < npg:
                    norm_T(Gs[gi + 1], 0, nT, b_nT)
                    heads_state2(Gs[gi], rescan_units(Gs[gi + 1]))
                else:
                    heads_state2(Gs[gi])
            stage('phaseA')
            retire(b_mg + b_hid, [b_WR])
            retire(b_qT, b_lf2 + b_bS2 + [b_onesT])
            retire(b_sog, b_KbT2 + b_Kbtm2)
            retire(b_ws[1:3], [b for jj in b_xt2 for b in jj])
            retire(b_tmp_all, b_tfB)
            retire(b_ts1_all, [b_WR] + [b for jj in b_VB for b in jj])
            hold_prefetch[0] = False
            phase_b(stage)
            return
        phase_a_group(halo_group())
        stage('halo')
        cp(Sh[:], Sst[:], b_Sst, [b_Sh])
        act(Dh[:], Bsum[:], AF.Exp, [b_Bsum], [b_Dh])
        b_xtmp = Buf("xtmp")
        b_gin, b_gout = Buf("gin"), Buf("gout")
        if mode != "B":
            for g in range(NG):
                phase_a_group(main_group(g))
            stage('phaseA')
            retire([b_xtmp], b_hid)
            cp(xtmp[:, 0:1024], Sst[:].rearrange("p h e -> p (h e)"), b_Sst, [b_xtmp])
            act(xtmp[:, 1024:1032], Bsum[:], AF.Exp, [b_Bsum], [b_xtmp])
        else:
            retire([b_xtmp], b_hid)
        if mode == "A":
            S.dma('sp', su, xtmp[:], sem_g2, reads=[b_xtmp])
            return
        if mode == "fused":
            S.dma('pool', gin.ap(), xtmp[:], sem_g, reads=[b_xtmp], writes=[b_gin])
            S.deps('pool', [b_gin], [b_gout])
            conv_issue(len(conv))
            S.wait_all('pool', sem_w + [sem_g, sem_pw, sem_cv])
            b_wsc.writer = (sem_cv, sem_cv.count)
            nc.gpsimd.collective_compute("AllGather", ALU.bypass, replica_groups=[list(range(NCORES))],
                                         ins=[gin.ap().opt()], outs=[gout.ap().opt()]).then_inc(sem_cc.h, 1)
            sem_cc.count += 1
            S.mark((sem_cc, sem_cc.count), [b_gin], [b_gout])
            S.wait_all('pool', [sem_cc])
            post_cc[0] = True
            gsrc = gout.ap()
        else:
            gsrc = gall
        S.op('dve', lambda e: e.memset(Sst[:], 0.0), writes=b_Sst)
        for j in range(NCORES):
            S.dma('sp', xtmp[:], gsrc[j * 128:(j + 1) * 128, :], sem_g2, reads=[b_gout], writes=[b_xtmp])
            ts(stat[:, 20:28], xtmp[:, 1024:1032], cv[:, CV_M + j:CV_M + j + 1], cst[:, OMM + j:OMM + j + 1], ALU.mult, ALU.add,
               [b_xtmp, b_cv, b_cst], [b_stat])
            ts(xtmp[:, 0:1024], xtmp[:, 0:1024], cv[:, CV_M + j:CV_M + j + 1], None, ALU.mult, None, [b_xtmp, b_cv], [b_xtmp])
            for h in range(8):
                stt(Sst[:, h, :], Sst[:, h, :], stat[:, 20 + h:21 + h], xtmp[:, h * 128:(h + 1) * 128], ALU.mult, ALU.add,
                    [b_Sst[h], b_stat, b_xtmp], [b_Sst[h]])
        for h in range(8):
            stt(Sst[:, h, :], Sst[:, h, :], Dh[:, h:h + 1], Sh[:, h, :], ALU.mult, ALU.add, [b_Sst[h], b_Dh, b_Sh], [b_Sst[h]])
        retire(b_hid, [b_xtmp])
        dv = os.environ.get('K_DUMMY')
        if dv:
            sem_dm = S.newsem("d_dm")
            bdm = Buf("dm")
            retire([bdm], b_hid)
            if dv == '1':
                S.dma('sp', hidT[:, 0, :], wsc['w_in'][0:128, 0:512], sem_dm, writes=[bdm])
            elif dv == '2':
                S.dma('sp', xtmp[:, 0:512], wd['w_in'][0:128, 0:512], sem_dm, writes=[bdm])
            elif dv == '3':
                S.dma('sp', xtmp[:, 0:512], xm[0:128, 0:512], sem_dm, writes=[bdm])
            S.wait_all('sp', [sem_dm])
        stage('exch')

        phase_b(stage)

    def phase_b(stage):
        for g in range(NG):
            G = main_group(g)
            load_x(G)
            stage('B_load')
            norm_T(G, 0, nT, b_nT)
            stage('B_norm')
            for half in range(2):
                f_proj(G, half, with_q=True)
            stage('B_fproj')
            if mode == "R":
                heads_phase_b(G)
            else:
                for h in range(8):
                    head_state(G, h, phaseB=True)
                    head_out(G, h)
            stage('B_heads')
            pool_proj(G, NHALO)
            pool_branch(G)
            stage('B_pool')
            merge_branches(G)
            stage('B_merge')
            out_proj(G)
            stage('B_out')
            norm_T(G, 1, nT, b_nT)
            ffn(G)
            stage('B_ffn')
            final_norm_store(G)
            stage('B_g%d' % g)


    class StopBuild(Exception):
        pass
    STOP = os.environ.get('K_STOP', '')

    def stage(name):
        if STOP == name:
            raise StopBuild()

    try:
        emit_all(stage)
    except StopBuild:
        print("STOPPED at", STOP)
    else:
        assert wstate["used"] == len(plan), (wstate, len(plan))
    S.wait_all('sp', sem_y + [sem_misc, sem_g2] + sem_x + sem_x2 + sem_stg)
    S.wait_all('pool', sem_w + [sem_g, sem_cc, sem_pw, sem_cv, sem_wr])
    S.wait_all('act', [S.esem['dve'], S.esem['pe']])
    S.wait_all('dve', [S.esem['act']])
    return nc


_CACHE = {}


def kernel(x, meta_tokens, norm_mix_g, w_in, lb_raw, hgrn_norm_g, pool_w, pool_scale,
           w_branch_a, w_branch_b, w_out, norm_ffn_g, w_ffn_gate, w_ffn_up, w_ffn_down, norm_final_g):
    f = lambda a: np.ascontiguousarray(np.asarray(a, dtype=np.float32))
    x = f(x)
    meta = f(meta_tokens)
    B = x.shape[0]
    segs = NCORES // B
    MODE = os.environ.get("K_MODE", "R")
    if MODE not in _CACHE:
        _CACHE[MODE] = (build_program({"fused": "fused", "R": "R"}[MODE]),) if MODE in ("fused", "R") else (build_program("A"), build_program("B"))
    progs = _CACHE[MODE]
    shared = {
        "w_in": f(w_in[0]), "w_ba": f(w_branch_a[0]), "w_bb": f(w_branch_b[0]), "w_out": f(w_out[0]),
        "w_g": f(w_ffn_gate[0]), "w_u": f(w_ffn_up[0]), "w_d": f(w_ffn_down[0]), "pool_w": f(pool_w[0]),
        "gvec": np.ascontiguousarray(np.stack([f(norm_mix_g[0]), f(norm_ffn_g[0]), f(norm_final_g)], 0)),
    }
    lb = f(lb_raw)
    in_maps = []
    for c in range(NCORES):
        b, s = divmod(c, segs)
        cvec = np.zeros((128, NCV), np.float32)
        cvec[:, 0:8] = lb[0].reshape(H, 128).T
        cvec[:, 8:16] = lb[1].reshape(H, 128).T
        cvec[:, 16] = f(hgrn_norm_g[0])
        cvec[:, 17:21] = f(pool_scale[0]).reshape(4, 128).T
        cvec[:, 21] = 1.0 if (s == 0 or MODE == "R") else 0.0
        if MODE == "R":
            xp = np.zeros((NPRED * NTOK, D), np.float32)
            npre = min(s, NPRED) * NTOK
            if npre:
                xp[NPRED * NTOK - npre:] = x[b, s * NTOK - npre:s * NTOK]
            for gi in range(NPRED * NG):
                cvec[:, 40 + gi] = 1.0 if gi * T >= NPRED * NTOK - npre else 0.0
        for j in range(NCORES):
            bj, sj = divmod(j, segs)
            cvec[:, 22 + j] = 1.0 if (bj == b and sj < s) else 0.0
        xm = x[b, s * NTOK:(s + 1) * NTOK]
        xh = meta if s == 0 else x[b, s * NTOK - NHALO:s * NTOK]
        m = dict(shared)
        m.update({"xm": np.ascontiguousarray(xm), "xh": np.ascontiguousarray(xh), "cvec": cvec})
        if MODE == "R":
            m.update({"xp": xp, "xmeta": meta})
        in_maps.append(m)
    if MODE in ("fused", "R"):
        res = run_bass_kernel_spmd(progs[0], in_maps, core_ids=list(range(NCORES)))
    else:
        ra = run_bass_kernel_spmd(progs[0], in_maps, core_ids=list(range(NCORES)))
        gall = np.ascontiguousarray(np.concatenate([ra.results[c]["su"] for c in range(NCORES)], 0))
        for m in in_maps:
            m["gall"] = gall
        res = run_bass_kernel_spmd(progs[1], in_maps, core_ids=list(range(NCORES)))
    out = np.empty((B, segs * NTOK, D), np.float32)
    for c in range(NCORES):
        b, s = divmod(c, segs)
        out[b, s * NTOK:(s + 1) * NTOK] = res.results[c]["y"]
    return out
```

```python
import numpy as np
import concourse.bass as bass
import concourse.mybir as mybir
from concourse.bass_utils import run_bass_kernel_spmd

F32 = mybir.dt.float32
BF16 = mybir.dt.bfloat16
AF = mybir.ActivationFunctionType
import os as _os
SILU = AF.Tanh if _os.environ.get('K_NOSILU') else AF.Silu
ALU = mybir.AluOpType
AX = mybir.AxisListType

NCORES = 8
D = 1024
H = 8
import os
NTOK = int(os.environ.get('K_NTOK', '2048'))
T = 512
NG = NTOK // T
CH = 64
NHALO = 16
DFF = 2816
INW = 6656
Q0, F0, I0, OG0, PL0, GA0, GB0 = 0, 1024, 2048, 3072, 4096, 4608, 5632
EPS = 1e-6
NCV = 56
NPRED = 3
SB_BASE = 16512
SB_LIMIT = 229376
NSLOT = 3
PREFETCH = 2


class Sem:
    def __init__(self, h):
        self.h = h
        self.count = 0


class Buf:
    __slots__ = ("name", "writer", "readers")

    def __init__(self, name=""):
        self.name = name
        self.writer = None
        self.readers = {}


class Sched:
    def __init__(self, nc):
        self.nc = nc
        self.eng = {'pe': nc.tensor, 'act': nc.scalar, 'dve': nc.vector, 'pool': nc.gpsimd, 'sp': nc.sync}
        self.esem = {e: Sem(nc.alloc_semaphore(f"s_{e}")) for e in self.eng}
        self.known = {e: {} for e in self.eng}

    def newsem(self, name):
        return Sem(self.nc.alloc_semaphore(name))

    def deps(self, e, reads, writes):
        deps = {}
        pes = self.esem['pe']

        def add(s, v):
            if e == 'pe' and s is pes:
                return
            if deps.get(s, 0) < v:
                deps[s] = v
        for b in reads:
            if b.writer is not None:
                add(*b.writer)
        for b in writes:
            if b.writer is not None:
                add(*b.writer)
            for s, v in b.readers.items():
                add(s, v)
        kn = self.known[e]
        for s, v in deps.items():
            if kn.get(s, 0) >= v:
                continue
            self.eng[e].wait_ge(s.h, v)
            kn[s] = v

    def mark(self, tag, reads, writes):
        s, v = tag
        for b in reads:
            if b.readers.get(s, 0) < v:
                b.readers[s] = v
        for b in writes:
            b.writer = tag
            b.readers = {}

    def op(self, e, fn, reads=(), writes=(), signal=True):
        self.deps(e, reads, writes)
        ins = fn(self.eng[e])
        s = self.esem[e]
        if signal:
            s.count += 1
            ins.then_inc(s.h, 1)
            tag = (s, s.count)
        else:
            tag = (s, s.count + 1)
        self.mark(tag, reads, writes)
        return ins

    def dma(self, e, out, in_, sem, reads=(), writes=(), **kw):
        self.deps(e, reads, writes)
        ins = self.eng[e].dma_start(out=out, in_=in_, **kw)
        sem.count += 16
        ins.then_inc(sem.h, 16)
        self.mark((sem, sem.count), reads, writes)
        return ins

    def wait_all(self, e, sems):
        for s in sems:
            if s.count > 0 and self.known[e].get(s, 0) < s.count:
                self.eng[e].wait_ge(s.h, s.count)
                self.known[e][s] = s.count


def _dtsize(dt):
    return 4 if dt == F32 else 2


class Arena:
    def __init__(self, nc):
        self.nc = nc
        self.top = SB_BASE
        self.n = 0

    def at(self, shape, dt, addr):
        self.n += 1
        return self.nc.alloc_sbuf_tensor_at(f"t{self.n}", list(shape), dt, offset=addr)

    def take(self, shape, dt):
        nbytes = int(np.prod(shape[1:])) * _dtsize(dt)
        nbytes = (nbytes + 63) // 64 * 64
        addr = self.top
        self.top += nbytes
        assert self.top <= SB_LIMIT, f"SBUF overflow {self.top}"
        return self.at(shape, dt, addr), addr


def weight_plan(mode="fused"):
    plan = []
    if mode == "R":
        plan.append(("w_in", 0, 8, PL0, 512))
    na = {"B": 0, "R": -1}.get(mode, NG)
    for g in range(-1, na):
        for half in range(2):
            plan.append(("w_in", 0, 8, F0 + half * 512, 512))
            plan.append(("w_in", 0, 8, I0 + half * 512, 512))
        if g == -1:
            plan.append(("w_in", 0, 8, PL0, 512))
    for g in range(NG if mode != "A" else 0):
        for half in range(2):
            plan.append(("w_in", 0, 8, F0 + half * 512, 512))
            plan.append(("w_in", 0, 8, Q0 + half * 512, 512))
            plan.append(("w_in", 0, 8, OG0 + half * 512, 512))
            plan.append(("w_in", 0, 8, I0 + half * 512, 512))
        plan.append(("w_in", 0, 8, PL0, 512))
        for dh in range(2):
            plan.append(("w_in", 0, 8, GA0 + dh * 512, 512))
            plan.append(("w_in", 0, 8, GB0 + dh * 512, 512))
            plan.append(("w_ba", 0, 8, dh * 512, 512))
            plan.append(("w_bb", 0, 4, dh * 512, 512))
        for half in range(2):
            plan.append(("w_out", 0, 8, half * 512, 512))
        for fblk in range(6):
            nc_ = 512 if fblk < 5 else 256
            plan.append(("w_g", 0, 8, fblk * 512, nc_))
            plan.append(("w_u", 0, 8, fblk * 512, nc_))
        for half in range(2):
            for kc0, nk in ((0, 8), (8, 8), (16, 6)):
                plan.append(("w_d", kc0, nk, half * 512, 512))
    return plan


def build_program(mode="fused"):
    nc = bass.Bass("TRN2", target_bir_lowering=False)
    S = Sched(nc)
    AR = Arena(nc)

    def din(name, shape):
        return nc.dram_tensor(name, list(shape), F32, kind="ExternalInput").ap()

    xm = din("xm", [NTOK, D])
    xh = din("xh", [NHALO, D])
    if mode == "R":
        xmeta = din("xmeta", [NHALO, D])
        xp = din("xp", [NPRED * NTOK, D])
    cvec = din("cvec", [128, NCV])
    gvec = din("gvec", [3, D])
    wd = {
        "w_in": din("w_in", [D, INW]), "w_ba": din("w_ba", [D, D]), "w_bb": din("w_bb", [512, D]),
        "w_out": din("w_out", [D, D]), "w_g": din("w_g", [D, DFF]), "w_u": din("w_u", [D, DFF]),
        "w_d": din("w_d", [DFF, D]),
    }
    pool_w = din("pool_w", [4, 128, 128])
    wshape = {"w_in": [D, INW], "w_ba": [D, D], "w_bb": [512, D], "w_out": [D, D], "w_g": [D, DFF], "w_u": [D, DFF], "w_d": [DFF, D]}
    wsc = {}
    if mode == "A":
        su = nc.dram_tensor("su", [128, 1032], F32, kind="ExternalOutput").ap()
        y = None
    else:
        y = nc.dram_tensor("y", [NTOK, D], F32, kind="ExternalOutput").ap()
    if mode == "B":
        gall = din("gall", [NCORES * 128, 1032])
    if mode == "fused":
        gin = nc.dram_tensor("gin", [128, 1032], F32)
        gout = nc.dram_tensor("gout", [NCORES * 128, 1032], F32)

    def T_(shape, dt):
        return AR.take(shape, dt)[0]

    a_ws0 = AR.top
    wslot = [T_([128, 8, 512], BF16) for _ in range(NSLOT)]; b_ws = [Buf() for _ in range(NSLOT)]
    xt2 = AR.at([128, 4, D], F32, a_ws0 + 8192); b_xt2 = [[Buf(), Buf()] for _ in range(4)]
    cv = T_([128, NCV], F32); b_cv = Buf("cv")
    cst = T_([128, 64], F32); b_cst = Buf("cst")
    gbc = [T_([128, D], F32) for _ in range(3)]; b_gbc = [Buf() for _ in range(3)]
    pw = T_([128, 4, 128], BF16); b_pw = Buf("pw")
    ident = T_([128, 128], BF16); b_ident = Buf("ident")
    ones32 = T_([128, 128], F32); b_ones = Buf("ones")
    rmask = T_([128, T], F32); b_rmask = Buf("rmask")
    cmask = T_([128, 256], F32); b_cmask = Buf("cmask")
    xt = T_([128, 4, D], F32); b_xt = [[Buf(), Buf()] for _ in range(4)]
    junk = T_([128, D], BF16); b_junk = Buf("junk")
    ntm = [T_([128, D], BF16) for _ in range(2)]; b_ntm = [Buf(), Buf()]
    nT = T_([128, 8, T], BF16); b_nT = [Buf() for _ in range(4)]
    stat = T_([128, 32], F32); b_stat = Buf("stat")
    tf, a_tf = AR.take([128, 8, T], F32); b_tf = [Buf() for _ in range(8)]
    qT, a_qT = AR.take([128, 8, T], F32); b_qT = [Buf() for _ in range(8)]
    sog, a_sog = AR.take([128, 8, T], BF16); b_sog = [Buf() for _ in range(8)]
    V = T_([128, 4, D], BF16); b_V = [[Buf(), Buf()] for _ in range(4)]
    lf, a_tmp = AR.take([128, T], F32); b_lf = Buf("lf")
    bS = T_([128, T], F32); b_bS = Buf("bS")
    Epos = T_([128, T], F32); b_Epos = Buf("Epos")
    QdT = T_([128, T], BF16); b_QdT = Buf("QdT")
    KbT = T_([128, T], BF16); b_KbT = Buf("KbT")
    QbT = T_([128, T], BF16); b_QbT = Buf("QbT")
    Kbtm = T_([128, 4, 128], BF16); b_Kbtm = Buf("Kbtm")
    ATs = T_([128, 256], BF16); b_AT = Buf("AT")
    Sch = T_([128, 9, 128], F32); b_Sch = [Buf() for _ in range(9)]
    Sbf = T_([128, 8, 128], BF16); b_Sbf = Buf("Sbf")
    eb = T_([128, 16], F32); b_eb = Buf("eb")
    a_tmp_end = AR.top
    Sst = T_([128, 8, 128], F32); b_Sst = [Buf() for _ in range(8)]
    Bsum = T_([128, 8], F32); b_Bsum = Buf("Bsum")
    b_Sh = Buf("Sh")
    Dh = T_([128, 8], F32); b_Dh = Buf("Dh")
    yaT = T_([128, 8, T], BF16); b_yaT = [Buf() for _ in range(8)]
    ybT = T_([128, 4, T], BF16); b_ybT = [Buf() for _ in range(4)]
    uT = T_([128, 4, NHALO + T], F32); b_uT = [Buf() for _ in range(4)]
    mergedT, a_mg = AR.take([128, 8, T], BF16); b_mg = [Buf() for _ in range(8)]
    hidT, a_hid = AR.take([128, 22, T], BF16); b_hid = [Buf() for _ in range(22)]
    wstg = [T_([128, 8, 256], F32) for _ in range(2)]; b_stg = [Buf(), Buf()]
    a_stg1 = AR.top - 8192
    a_ts1 = AR.top - 16384
    print("SBUF top", AR.top, "limit", SB_LIMIT)
    sA = AR.at([128, NHALO + T], F32, a_tmp); sB = AR.at([128, NHALO + T], F32, a_tmp + 2176)
    pooledT = AR.at([128, 4, T], BF16, a_tmp + 4352)
    assert a_tmp + 4352 + 4096 <= a_tmp_end
    b_tmp_all = [b_lf, b_bS, b_Epos, b_QdT, b_KbT, b_QbT, b_Kbtm, b_AT, b_Sbf, b_eb] + b_Sch
    tga = AR.at([128, 4, T], F32, a_qT); tgb = AR.at([128, 4, T], F32, a_qT + 8192)
    sg = AR.at([128, 4, T], F32, a_tf)
    xtmp = AR.at([128, 1032], F32, a_hid)
    WR = AR.at([128, 4, 8, 512], BF16, a_mg)
    assert a_mg + 4 * 8 * 512 * 2 <= AR.top and a_hid == a_mg + 8 * T * 2
    b_WR = Buf("WR")
    lf2 = [AR.at([128, T], F32, a_qT + i * 2048) for i in range(3)]
    bS2 = [AR.at([128, T], F32, a_qT + 6144 + i * 2048) for i in range(3)]
    onesT = AR.at([128, T], F32, a_qT + 12288)
    KbT2 = [AR.at([128, T], BF16, a_sog + i * 1024) for i in range(2)]
    Kbtm2 = [AR.at([128, 4, 128], BF16, a_sog + 2048 + i * 1024) for i in range(2)]
    b_lf2 = [Buf() for _ in range(3)]; b_bS2 = [Buf() for _ in range(3)]; b_KbT2 = [Buf(), Buf()]; b_Kbtm2 = [Buf(), Buf()]; b_onesT = Buf()
    b_eb2 = [Buf() for _ in range(4)]
    tfB = AR.at([128, 8, T], F32, a_tmp); b_tfB = [Buf() for _ in range(8)]
    assert a_tmp + 8 * T * 4 <= a_tmp_end
    VB = AR.at([128, 4, D], BF16, a_stg1); b_VB = [[Buf(), Buf()] for _ in range(4)]
    Sh = AR.at([128, 8, 128], F32, a_hid + 4160)
    assert 4160 + 4096 <= 22 * T * 2

    class TS:
        pass
    TS0 = TS()
    TS0.lf, TS0.bS, TS0.Epos, TS0.QdT, TS0.KbT, TS0.QbT, TS0.Kbtm, TS0.AT, TS0.Sch, TS0.Sbf, TS0.eb = lf, bS, Epos, QdT, KbT, QbT, Kbtm, ATs, Sch, Sbf, eb
    TS0.b_lf, TS0.b_bS, TS0.b_Epos, TS0.b_QdT, TS0.b_KbT, TS0.b_QbT, TS0.b_Kbtm, TS0.b_AT, TS0.b_Sch, TS0.b_Sbf, TS0.b_eb = \
        b_lf, b_bS, b_Epos, b_QdT, b_KbT, b_QbT, b_Kbtm, b_AT, b_Sch, b_Sbf, b_eb
    TS0.pc = 0
    TS1 = TS()
    _o = a_ts1
    TS1.lf = AR.at([128, T], F32, _o); TS1.bS = AR.at([128, T], F32, _o + 2048); TS1.Epos = AR.at([128, T], F32, _o + 4096)
    TS1.QdT = AR.at([128, T], BF16, _o + 6144); TS1.KbT = AR.at([128, T], BF16, _o + 7168); TS1.QbT = AR.at([128, T], BF16, _o + 8192)
    TS1.Kbtm = AR.at([128, 4, 128], BF16, _o + 9216); TS1.AT = AR.at([128, 256], BF16, _o + 10240)
    TS1.Sch = AR.at([128, 9, 128], F32, _o + 10752); TS1.Sbf = AR.at([128, 8, 128], BF16, _o + 15360); TS1.eb = AR.at([128, 8], F32, _o + 17408)
    assert _o + 17408 + 32 <= SB_LIMIT - 8, (_o, SB_LIMIT)
    TS1.b_lf, TS1.b_bS, TS1.b_Epos, TS1.b_QdT, TS1.b_KbT, TS1.b_QbT, TS1.b_Kbtm, TS1.b_AT, TS1.b_Sbf, TS1.b_eb = [Buf() for _ in range(10)]
    TS1.b_Sch = [Buf() for _ in range(9)]
    TS1.pc = 256
    b_ts1_all = [TS1.b_lf, TS1.b_bS, TS1.b_Epos, TS1.b_QdT, TS1.b_KbT, TS1.b_QbT, TS1.b_Kbtm, TS1.b_AT, TS1.b_Sbf, TS1.b_eb] + TS1.b_Sch
    NPP = 2
    pp = [nc.alloc_psum_tensor(f"pp{i}", [128, 512], F32) for i in range(NPP)]; b_pp = [Buf() for _ in range(NPP)]
    ptr = nc.alloc_psum_tensor("ptr", [128, 1024], BF16); b_ptr = Buf("ptr")
    pacc5 = [nc.alloc_psum_tensor(f"pa{i}", [128, 512], F32) for i in range(5)]; b_pacc5 = [Buf() for _ in range(5)]
    pa, po0, po1, pu0, pu1 = pacc5
    b_pa, b_po0, b_po1, b_pu0, b_pu1 = b_pacc5
    pacc = pacc5[0:4]; b_pacc = b_pacc5[0:4]
    pp_rr = [0]

    def next_pp():
        i = pp_rr[0] % NPP
        pp_rr[0] += 1
        return pp[i], b_pp[i]

    def act(out, in_, func, reads, writes, scale=1.0, bias=None, accum=None):
        kw = {}
        if bias is not None:
            kw["bias"] = bias
        if accum is not None:
            kw["accum_out"] = accum
        return S.op('act', lambda e: e.activation(out=out, in_=in_, func=func, scale=scale, **kw), reads, writes)

    def tt(out, in0, in1, op, reads, writes):
        return S.op('dve', lambda e: e.tensor_tensor(out=out, in0=in0, in1=in1, op=op), reads, writes)

    def stt(out, in0, scalar, in1, op0, op1, reads, writes):
        return S.op('dve', lambda e: e.scalar_tensor_tensor(out=out, in0=in0, scalar=scalar, in1=in1, op0=op0, op1=op1),
                    reads, writes)

    def ts(out, in0, s1, s2, op0, op1, reads, writes):
        if s2 is None:
            return S.op('dve', lambda e: e.tensor_scalar(out=out, in0=in0, scalar1=s1, scalar2=None, op0=op0), reads, writes)
        return S.op('dve', lambda e: e.tensor_scalar(out=out, in0=in0, scalar1=s1, scalar2=s2, op0=op0, op1=op1), reads, writes)

    def ptt(out, in0, in1, op, reads, writes):
        return S.op('pool', lambda e: e.tensor_tensor(out=out, in0=in0, in1=in1, op=op), reads, writes)

    def pts(out, in0, s1, s2, op0, op1, reads, writes):
        return S.op('pool', lambda e: e.tensor_scalar(out=out, in0=in0, scalar1=s1, scalar2=s2, op0=op0, op1=op1), reads, writes)

    def cp(out, in_, reads, writes):
        return S.op('dve', lambda e: e.tensor_copy(out=out, in_=in_), reads, writes)

    def mm(out, lhsT, rhs, start, stop, reads, writes, signal):
        return S.op('pe', lambda e: e.matmul(out, lhsT=lhsT, rhs=rhs, start=start, stop=stop), reads, writes, signal=signal)

    def tp(out, in_, idn, reads, writes, signal):
        return S.op('pe', lambda e: e.transpose(out=out, in_=in_, identity=idn), reads, writes, signal=signal)

    def retire(new_bufs, old_bufs):
        acc = {}
        for b in old_bufs:
            if b.writer is not None:
                s, v = b.writer
                acc[s] = max(acc.get(s, 0), v)
            for s, v in b.readers.items():
                acc[s] = max(acc.get(s, 0), v)
        for b in new_bufs:
            b.writer = None
            b.readers = dict(acc)

    sem_misc = S.newsem("d_misc")
    sem_x = [S.newsem(f"d_x{j}") for j in range(4)]
    sem_x2 = [S.newsem(f"d_xb{j}") for j in range(4)]
    sem_y = [S.newsem(f"d_y{j}") for j in range(4)]
    sem_w = [S.newsem(f"d_w{i}") for i in range(NSLOT)]
    sem_cc = S.newsem("cc")
    sem_g = S.newsem("d_g")
    sem_pw = S.newsem("d_pw")
    sem_g2 = S.newsem("d_g2")
    sem_wh = [S.newsem(f"d_wh{i}") for i in range(NSLOT)]
    sem_cv = S.newsem("d_cv")
    sem_wr = S.newsem("d_wr")
    sem_stg = [S.newsem("d_stg0"), S.newsem("d_stg1")]

    plan = weight_plan(mode)
    wstate = {"issued": 0, "used": 0}

    post_cc = [False]
    stg_rr = [0]
    b_wsc = Buf("wsc")
    conv = []
    for name_, shp in wshape.items():
        cw = 512 if shp[1] % 512 == 0 else 256
        for r0 in range(0, shp[0], 128):
            conv.append((name_, r0, cw))
    conv_state = [0]

    def conv_issue(n):
        return

    def _conv_issue(n):
        while n > 0 and conv_state[0] < len(conv):
            name_, r0, cw = conv[conv_state[0]]
            src = wd[name_][r0:r0 + 128, :].rearrange("p (a c) -> p a c", c=cw)
            dst = wsc[name_][r0:r0 + 128, :].rearrange("p (a c) -> p a c", c=cw)
            S.dma('pool', dst, src, sem_cv, writes=[])
            conv_state[0] += 1
            n -= 1

    def w_issue_upto(k):
        while wstate["issued"] <= min(k, len(plan) - 1):
            i = wstate["issued"]
            name, kc0, nk, c0, ncols = plan[i]
            sl = i % NSLOT
            if not post_cc[0]:
                src = wd[name].rearrange("(k p) c -> p k c", p=128)[:, kc0:kc0 + nk, c0:c0 + ncols]
                S.dma('pool', wslot[sl][:, 0:nk, 0:ncols], src, sem_w[sl], writes=[b_ws[sl]])
                conv_issue(3)
            else:
                for q in range(0, ncols, 256):
                    k_ = stg_rr[0] % 2
                    stg_rr[0] += 1
                    w_ = min(256, ncols - q)
                    src = wd[name].rearrange("(k p) c -> p k c", p=128)[:, kc0:kc0 + nk, c0 + q:c0 + q + w_]
                    S.dma('sp', wstg[k_][:, 0:nk, 0:w_], src, sem_stg[k_], writes=[b_stg[k_]])
                    S.op('pool', lambda e: e.tensor_copy(out=wslot[sl][:, 0:nk, q:q + w_], in_=wstg[k_][:, 0:nk, 0:w_]),
                         [b_stg[k_]], [b_ws[sl]])
            wstate["issued"] += 1

    hold_prefetch = [mode == "R"]

    def w_next(name, kc0, nk, c0, ncols):
        i = wstate["used"]
        assert plan[i] == (name, kc0, nk, c0, ncols), (i, plan[i], (name, kc0, nk, c0, ncols))
        w_issue_upto(i if hold_prefetch[0] else i + PREFETCH)
        wstate["used"] += 1
        sl = i % NSLOT
        return wslot[sl], b_ws[sl]

    S.dma('sp', cv[:], cvec, sem_misc, writes=[b_cv])
    for i in range(3):
        S.dma('sp', gbc[i][:], gvec[i].partition_broadcast(128), sem_misc, writes=[b_gbc[i]])
    S.dma('pool', pw[:], pool_w.rearrange("g c d -> c g d"), sem_pw, writes=[b_pw])
    for b in [b_cv] + b_gbc:
        b.writer = (sem_misc, sem_misc.count)
    w_issue_upto(0 if mode == "R" else PREFETCH - 1)
    S.op('pool', lambda e: e.memset(ident[:], 1.0), writes=[b_ident])
    S.op('pool', lambda e: e.affine_select(out=ident[:], in_=ident[:], pattern=[[-1, 128]], compare_op=ALU.is_equal,
                                           fill=0.0, base=0, channel_multiplier=1), reads=[b_ident], writes=[b_ident])
    S.op('pool', lambda e: e.memset(cmask[:], 1.0), writes=[b_cmask])
    for hp in range(2):
        S.op('pool', lambda e: e.affine_select(out=cmask[hp * 64:(hp + 1) * 64, :], in_=cmask[hp * 64:(hp + 1) * 64, :],
                                               pattern=[[0, 4], [1, 64]], compare_op=ALU.is_ge, fill=0.0, base=0,
                                               channel_multiplier=-1), reads=[b_cmask], writes=[b_cmask])
    S.op('dve', lambda e: e.memset(ones32[:], 1.0), writes=[b_ones])
    S.op('dve', lambda e: e.memset(rmask[:], 1.0), writes=[b_rmask])
    S.op('dve', lambda e: e.memset(rmask[:].rearrange("p (c t) -> p c t", t=CH)[:, :, 0:1], 0.0), writes=[b_rmask])
    S.op('dve', lambda e: e.memset(Sst[:], 0.0), writes=b_Sst)
    S.op('dve', lambda e: e.memset(Bsum[:], 0.0), writes=[b_Bsum])
    C0, C1, NC1, CEPS, OMM = 0, 8, 16, 24, 25
    CV_LB0, CV_LB1, CV_GH, CV_PS, CV_FLAG, CV_M = 0, 8, 16, 17, 21, 22
    S.op('dve', lambda e: e.memset(cst[:, CEPS:CEPS + 1], EPS), writes=[b_cst])
    tt(cst[:, 33:41], cv[:, CV_LB0:CV_LB0 + 8], cv[:, CV_LB1:CV_LB1 + 8], ALU.subtract, [b_cv], [b_cst])
    act(cst[:, 33:41], cst[:, 33:41], AF.Tanh, [b_cst], [b_cst], scale=0.5)
    ts(cst[:, C0:C0 + 8], cst[:, 33:41], 0.25, 0.75, ALU.mult, ALU.add, [b_cst], [b_cst])
    ts(cst[:, C1:C1 + 8], cst[:, 33:41], -0.25, 0.25, ALU.mult, ALU.add, [b_cst], [b_cst])
    ts(cst[:, NC1:NC1 + 8], cst[:, 33:41], 0.25, -0.25, ALU.mult, ALU.add, [b_cst], [b_cst])
    ts(cst[:, OMM:OMM + 8], cv[:, CV_M:CV_M + 8], -1.0, 1.0, ALU.mult, ALU.add, [b_cv], [b_cst])

    class Grp:
        pass

    def main_group(g):
        G = Grp()
        G.ntok = T
        G.tiles = [128] * 4
        G.tcol = [j * 128 for j in range(4)]
        G.chunks = [(c // 2, (c % 2) * 64, CH, c * CH) for c in range(8)]
        G.halo = False
        G.g = g
        G.flagcol = None
        G.src = xm
        G.xt, G.b_xt = xt, b_xt
        G.tf, G.b_tf, G.V, G.b_V = tf, b_tf, V, b_V
        return G

    def halo_group():
        G = Grp()
        G.ntok = NHALO
        G.tiles = [NHALO]
        G.tcol = [0]
        G.chunks = [(0, 0, NHALO, 0)]
        G.halo = True
        G.g = -1
        G.flagcol = CV_FLAG
        G.src = xh
        G.xt, G.b_xt = xt, b_xt
        G.tf, G.b_tf, G.V, G.b_V = tf, b_tf, V, b_V
        return G

    ntm_rr = [0]
    hook = [lambda name: None]

    def load_x(G):
        if G.halo:
            S.dma('sp', G.xt[0:NHALO, 0, :], G.src, sem_x[0], writes=G.b_xt[0])
        else:
            sx = sem_x if G.xt is xt else sem_x2
            for j in range(4):
                r0 = G.g * T + j * 128
                S.dma('sp', G.xt[:, j, :], G.src[r0:r0 + 128, :], sx[j], writes=G.b_xt[j])

    def norm_T(G, gi, dstT, b_dstT):
        nt = len(G.tiles)
        pc0 = G.tiles[0]
        for j, pc in enumerate(G.tiles):
            act(junk[0:pc, :], G.xt[0:pc, j, :], AF.Square, G.b_xt[j], [b_junk, b_stat], accum=stat[0:pc, j:j + 1])
        act(stat[0:pc0, 8:8 + nt], stat[0:pc0, 0:nt], AF.Ln, [b_stat, b_cst], [b_stat], scale=1.0 / D, bias=cst[0:pc0, CEPS:CEPS + 1])
        act(stat[0:pc0, 8:8 + nt], stat[0:pc0, 8:8 + nt], AF.Exp, [b_stat], [b_stat], scale=-0.5)
        for j, pc in enumerate(G.tiles):
            k = ntm_rr[0] % 2
            ntm_rr[0] += 1
            stt(ntm[k][0:pc, :], G.xt[0:pc, j, :], stat[0:pc, 8 + j:9 + j], gbc[gi][0:pc, :], ALU.mult, ALU.mult,
                G.b_xt[j] + [b_stat, b_gbc[gi]], [b_ntm[k]])
            for kc in range(8):
                tp(ptr[:, kc * 128:kc * 128 + pc], ntm[k][0:pc, kc * 128:(kc + 1) * 128], ident[0:pc, 0:pc],
                   [b_ntm[k], b_ident], [b_ptr], signal=(kc == 7))
            c0 = G.tcol[j]
            cp(dstT[:, :, c0:c0 + pc], ptr[:].rearrange("p (k t) -> p k t", t=128)[:, :, 0:pc], [b_ptr], [b_dstT[j]])

    def proj_fm(ws, bws, wc0, nk, rhsT, b_rhs, n, ps, bps):
        for k in range(nk):
            mm(ps[:, 0:n], ws[:, k, wc0:wc0 + 128], rhsT[:, k, 0:n], k == 0, k == nk - 1, [bws] + b_rhs, [bps], signal=(k == nk - 1))

    def rescan_units(G):
        units = []
        nb = b_nT
        for half in range(2):
            for hh in range(4):
                def u_f(half=half, hh=hh):
                    h = half * 4 + hh
                    ps, bps = next_pp()
                    proj_fm(WR[:, 2 * half], b_WR, hh * 128, 8, nT, nb, T, ps, bps)
                    act(G.tf[:, h, :], ps[:, :], AF.Tanh, [bps], [G.b_tf[h]], scale=0.5)
                units.append(u_f)
            for j in range(4):
                def u_i(half=half, j=j):
                    ps, bps = next_pp()
                    for k in range(8):
                        mm(ps[:, :], nT[:, k, j * 128:(j + 1) * 128], WR[:, 2 * half + 1, k, :], k == 0, k == 7, [b_WR, b_nT[j]], [bps], signal=(k == 7))
                    ts(G.V[:, j, half * 512:(half + 1) * 512], ps[:, :], cv[:, G.flagcol:G.flagcol + 1], None, ALU.mult, None,
                       [bps, b_cv], [G.b_V[j][half]])
                units.append(u_i)
        return units

    def f_proj(G, half, with_q, resident=False):
        n = G.ntok
        nb = b_nT[0:len(G.tiles)]
        if resident:
            ws, bws = WR[:, 2 * half], b_WR
        else:
            ws, bws = w_next("w_in", 0, 8, F0 + half * 512, 512)
        for hh in range(4):
            h = half * 4 + hh
            ps, bps = next_pp()
            proj_fm(ws, bws, hh * 128, 8, nT, nb, n, ps, bps)
            act(tf[:, h, 0:n], ps[:, 0:n], AF.Tanh, [bps], [b_tf[h]], scale=0.5)
        if with_q:
            hook[0]('Bf_f%d' % half)
            ws, bws = w_next("w_in", 0, 8, Q0 + half * 512, 512)
            for hh in range(4):
                h = half * 4 + hh
                ps, bps = next_pp()
                proj_fm(ws, bws, hh * 128, 8, nT, nb, n, ps, bps)
                act(qT[:, h, 0:n], ps[:, 0:n], SILU, [bps], [b_qT[h]])
            hook[0]('Bf_q%d' % half)
            ws, bws = w_next("w_in", 0, 8, OG0 + half * 512, 512)
            for hh in range(4):
                h = half * 4 + hh
                ps, bps = next_pp()
                proj_fm(ws, bws, hh * 128, 8, nT, nb, n, ps, bps)
                act(sog[:, h, 0:n], ps[:, 0:n], SILU, [bps], [b_sog[h]])
        if with_q: hook[0]('Bf_og%d' % half)
        if resident:
            ws, bws = WR[:, 2 * half + 1], b_WR
        else:
            ws, bws = w_next("w_in", 0, 8, I0 + half * 512, 512)
        for j, pc in enumerate(G.tiles):
            ps, bps = next_pp()
            c0 = G.tcol[j]
            for k in range(8):
                mm(ps[0:pc, :], nT[:, k, c0:c0 + pc], ws[:, k, :], k == 0, k == 7, [bws, b_nT[j]], [bps], signal=(k == 7))
            if G.flagcol is not None:
                ts(V[0:pc, j, half * 512:(half + 1) * 512], ps[0:pc, :], cv[0:pc, G.flagcol:G.flagcol + 1], None, ALU.mult, None,
                   [bps, b_cv], [b_V[j][half]])
            else:
                cp(V[0:pc, j, half * 512:(half + 1) * 512], ps[0:pc, :], [bps], [b_V[j][half]])

    def head_state(G, h, phaseB):
        n = G.ntok
        nch = len(G.chunks)
        half = h // 4
        act(lf[:, 0:n], tf[:, h, 0:n], AF.Ln, [b_tf[h], b_cst], [b_lf], scale=cst[:, C1 + h:C1 + h + 1], bias=cst[:, C0 + h:C0 + h + 1])
        if G.flagcol is not None:
            ts(lf[:, 0:n], lf[:, 0:n], cv[:, G.flagcol:G.flagcol + 1], None, ALU.mult, None, [b_lf, b_cv], [b_lf])
        S.op('dve', lambda e: e.tensor_tensor_scan(out=bS[:, 0:n], data0=rmask[:, 0:n], data1=lf[:, 0:n], initial=0.0,
                                                   op0=ALU.mult, op1=ALU.add), [b_lf, b_rmask], [b_bS])
        if G.halo:
            blast = bS[:, n - 1:n]
            blast_bc = blast.to_broadcast([128, n])
            cview = lf[:, 0:n]
            bview = bS[:, 0:n]
        else:
            b3 = bS[:].rearrange("p (c t) -> p c t", t=CH)
            blast = b3[:, :, CH - 1]
            blast_bc = b3[:, :, CH - 1:CH].to_broadcast([128, nch, CH])
            cview = lf[:].rearrange("p (c t) -> p c t", t=CH)
            bview = b3
        if not G.halo: hook[0]('H_scan')
        if not phaseB:
            S.op('dve', lambda e: e.tensor_reduce(out=stat[:, 16:17], in_=blast, axis=AX.X, op=ALU.add), [b_bS], [b_stat])
            tt(Bsum[:, h:h + 1], Bsum[:, h:h + 1], stat[:, 16:17], ALU.add, [b_Bsum, b_stat], [b_Bsum])
        if not G.halo: hook[0]('H_red')
        act(eb[:, 0:nch], blast, AF.Exp, [b_bS], [b_eb])
        if not G.halo: hook[0]('H_eb')
        tt(cview, bview, blast_bc, ALU.subtract, [b_bS], [b_lf])
        if not G.halo: hook[0]('H_c')
        if phaseB:
            act(Epos[:, 0:n], lf[:, 0:n], AF.Exp, [b_lf], [b_Epos])
            act(bS[:, 0:n], bS[:, 0:n], AF.Exp, [b_bS], [b_bS])
        act(lf[:, 0:n], lf[:, 0:n], AF.Exp, [b_lf], [b_lf], scale=-1.0)
        ts(tf[:, h, 0:n], tf[:, h, 0:n], cst[:, NC1 + h:NC1 + h + 1], cst[:, C1 + h:C1 + h + 1], ALU.mult, ALU.add,
           [b_tf[h], b_cst], [b_tf[h]])
        tt(KbT[:, 0:n], tf[:, h, 0:n], lf[:, 0:n], ALU.mult, [b_tf[h], b_lf], [b_KbT])
        if phaseB:
            tt(QdT[:, 0:n], qT[:, h, 0:n], Epos[:, 0:n], ALU.mult, [b_qT[h], b_Epos], [b_QdT])
            tt(QbT[:, 0:n], qT[:, h, 0:n], bS[:, 0:n], ALU.mult, [b_qT[h], b_bS], [b_QbT])
        if not G.halo: hook[0]('H_kb')
        nt = len(G.tiles)
        for j, pc in enumerate(G.tiles):
            c0 = G.tcol[j]
            tp(ptr[0:pc, j * 128:(j + 1) * 128], KbT[:, c0:c0 + pc], ident[:, :], [b_KbT, b_ident], [b_ptr], signal=(j == nt - 1))
        pc0 = G.tiles[0]
        cp(Kbtm[0:pc0, 0:nt, :], ptr[0:pc0, 0:nt * 128].rearrange("p (j d) -> p j d", d=128), [b_ptr], [b_Kbtm])
        if not G.halo: hook[0]('H_tp')
        for ci, (j, p0, C, col0) in enumerate(G.chunks):
            pu, bpu = (pu0, b_pu0) if ci % 2 == 0 else (pu1, b_pu1)
            mm(pu[:, (ci // 2) * 128:(ci // 2 + 1) * 128], Kbtm[p0:p0 + C, j, :], V[p0:p0 + C, j, h * 128:(h + 1) * 128], True, True,
               [b_Kbtm, b_V[j][half]], [bpu], signal=(ci >= nch - 2))
        if not G.halo: hook[0]('H_u')
        cp(Sch[:, 0, :], Sst[:, h, :], [b_Sst[h]], [b_Sch[0]])
        for ci in range(nch):
            pu, bpu = (pu0, b_pu0) if ci % 2 == 0 else (pu1, b_pu1)
            stt(Sch[:, ci + 1, :], Sch[:, ci, :], eb[:, ci:ci + 1], pu[:, (ci // 2) * 128:(ci // 2 + 1) * 128], ALU.mult, ALU.add,
                [b_Sch[ci], b_eb, bpu], [b_Sch[ci + 1]])
        cp(Sst[:, h, :], Sch[:, nch, :], [b_Sch[nch]], [b_Sst[h]])
        if not G.halo: hook[0]('H_chain')

    def hs2_A(G, h):
        q = h % 3
        act(lf2[q][:, :], G.tf[:, h, :], AF.Ln, [G.b_tf[h], b_cst], [b_lf2[q]], scale=cst[:, C1 + h:C1 + h + 1], bias=cst[:, C0 + h:C0 + h + 1])
        pts(G.tf[:, h, :], G.tf[:, h, :], cst[:, NC1 + h:NC1 + h + 1], cst[:, C1 + h:C1 + h + 1], ALU.mult, ALU.add,
            [G.b_tf[h], b_cst], [G.b_tf[h]])

    def hs2_B(G, h):
        q = h % 3
        S.op('dve', lambda e: e.tensor_tensor_scan(out=bS2[q][:, :], data0=onesT[:, :], data1=lf2[q][:, :], initial=0.0,
                                                   op0=ALU.mult, op1=ALU.add), [b_lf2[q], b_onesT], [b_bS2[q]])

    def hs2_C(G, h):
        q = h % 3
        fc = G.flagcol
        blast = bS2[q][:, T - 1:T]
        act(eb[:, 8 + h % 4:9 + h % 4], blast, AF.Exp, [b_bS2[q]], [b_eb2[h % 4]])
        ts(eb[:, 8 + h % 4:9 + h % 4], eb[:, 8 + h % 4:9 + h % 4], cv[:, fc:fc + 1], cst[:, 41 + fc - 40:42 + fc - 40], ALU.mult, ALU.add,
           [b_eb2[h % 4], b_cv, b_cst], [b_eb2[h % 4]])
        act(lf2[q][:, :], bS2[q][:, :], AF.Exp, [b_bS2[q]], [b_lf2[q]], scale=-1.0, bias=blast)

    def hs2_D1(G, h):
        q = h % 3; p = h % 2
        ptt(KbT2[p][:, :], G.tf[:, h, :], lf2[q][:, :], ALU.mult, [G.b_tf[h], b_lf2[q]], [b_KbT2[p]])

    def hs2_D2(G, h):
        p = h % 2
        for j in range(4):
            tp(ptr[:, j * 128:(j + 1) * 128], KbT2[p][:, j * 128:(j + 1) * 128], ident[:, :], [b_KbT2[p], b_ident], [b_ptr], signal=(j == 3))

    def hs2_D3(G, h):
        p = h % 2
        cp(Kbtm2[p][:, :, :], ptr[:, 0:512].rearrange("p (j d) -> p j d", d=128), [b_ptr], [b_Kbtm2[p]])

    def hs2_D4(G, h):
        p = h % 2
        half = h // 4
        pu, bpu = (pu0, b_pu0) if p == 0 else (pu1, b_pu1)
        for j in range(4):
            mm(pu[:, 0:128], Kbtm2[p][:, j, :], G.V[:, j, h * 128:(h + 1) * 128], j == 0, j == 3, [b_Kbtm2[p], G.b_V[j][half]], [bpu], signal=(j == 3))

    def hs2_D5(G, h):
        q = h % 3; p = h % 2
        pu, bpu = (pu0, b_pu0) if p == 0 else (pu1, b_pu1)
        stt(Sst[:, h, :], Sst[:, h, :], eb[:, 8 + h % 4:9 + h % 4], pu[:, 0:128], ALU.mult, ALU.add, [b_Sst[h], b_eb2[h % 4], bpu], [b_Sst[h]])

    def heads_state2(G, units=()):
        units = list(units)

        def take(n):
            for _ in range(n):
                if units:
                    units.pop(0)()
        hs2_A(G, 0)
        hs2_A(G, 1)
        hs2_B(G, 0)
        for it in range(8 + 2):
            take(2 if it < 8 else 0)
            if it - 1 >= 0 and it - 1 < 8:
                hs2_D2(G, it - 1)
            if it - 2 >= 0 and it - 2 < 8:
                hs2_D4(G, it - 2)
            if it + 2 < 8:
                hs2_A(G, it + 2)
            if it + 1 < 8:
                hs2_B(G, it + 1)
            if it < 8:
                hs2_C(G, it)
                hs2_D1(G, it)
            if it - 1 >= 0 and it - 1 < 8:
                hs2_D3(G, it - 1)
            if it - 2 >= 0 and it - 2 < 8:
                hs2_D5(G, it - 2)
        take(len(units))

    def head_out(G, h):
        half = h // 4
        hook[0]('O_start')
        S.op('act', lambda e: e.activation(out=Sbf[:].rearrange("p c e -> p (c e)"), in_=Sch[:, 0:8, :].rearrange("p c e -> p (c e)"),
                                           func=AF.Copy), b_Sch[0:8], [b_Sbf])
        for ci, (j, p0, C, col0) in enumerate(G.chunks):
            mm(pa[p0:p0 + CH, (ci // 2) * CH:(ci // 2 + 1) * CH], KbT[:, col0:col0 + CH], QdT[:, col0:col0 + CH], True, True,
               [b_KbT, b_QdT], [b_pa], signal=(ci == 7))
        hook[0]('O_at')
        tt(ATs[:, :], pa[:, 0:256], cmask[:, :], ALU.mult, [b_pa, b_cmask], [b_AT])
        hook[0]('O_mask')
        pos = ((po0, b_po0), (po1, b_po1))
        for ci, (j, p0, C, col0) in enumerate(G.chunks):
            po, b_po = pos[ci % 2]
            oc = (ci // 2) * CH
            mm(po[:, oc:oc + CH], V[p0:p0 + CH, j, h * 128:(h + 1) * 128], ATs[p0:p0 + CH, oc:oc + CH],
               True, False, [b_V[j][half], b_AT], [b_po], signal=False)
            mm(po[:, oc:oc + CH], Sbf[:, ci, :], QbT[:, col0:col0 + CH], False, True, [b_Sbf, b_QbT], [b_po], signal=(ci >= 6))

        hook[0]('O_o')

        def par_view(t2d, par):
            return t2d.rearrange("p (c two t) -> p c two t", two=2, t=CH)[:, :, par, :]

        def po_view(par):
            return pos[par][0][:, 0:256].rearrange("p (c t) -> p c t", t=CH)
        for par in range(2):
            act(par_view(Epos[:, :], par), po_view(par), AF.Square, [pos[par][1]], [b_Epos])
        hook[0]('O_sq')
        ps, bps = next_pp()
        mm(ps[:, :], ones32[:, :], Epos[:, :], True, True, [b_ones, b_Epos], [bps], signal=True)
        hook[0]('O_ones')
        act(lf[:, :], ps[:, :], AF.Ln, [bps, b_cst], [b_lf], scale=1.0 / 128, bias=cst[:, CEPS:CEPS + 1])
        act(lf[:, :], lf[:, :], AF.Exp, [b_lf], [b_lf], scale=-0.5)
        for par in range(2):
            tt(par_view(lf[:, :], par), po_view(par), par_view(lf[:, :], par), ALU.mult, [pos[par][1], b_lf], [b_lf])
        stt(yaT[:, h, :], lf[:, :], cv[:, CV_GH:CV_GH + 1], sog[:, h, :], ALU.mult, ALU.mult, [b_lf, b_cv, b_sog[h]], [b_yaT[h]])

    def head_chain(G, h, X):
        half = h // 4
        n = T
        pc = X.pc
        pt = 512 if pc else 0
        b3 = X.bS[:].rearrange("p (c t) -> p c t", t=CH)
        blast = b3[:, :, CH - 1]
        blast_bc = b3[:, :, CH - 1:CH].to_broadcast([128, 8, CH])
        cview = X.lf[:].rearrange("p (c t) -> p c t", t=CH)
        pos = ((po0, b_po0), (po1, b_po1))

        def par_view(t2d, par):
            return t2d.rearrange("p (c two t) -> p c two t", two=2, t=CH)[:, :, par, :]

        def po_view(par):
            return pos[par][0][:, pc:pc + 256].rearrange("p (c t) -> p c t", t=CH)

        def s1():
            act(X.lf[:, :], tf[:, h, :], AF.Ln, [b_tf[h], b_cst], [X.b_lf], scale=cst[:, C1 + h:C1 + h + 1], bias=cst[:, C0 + h:C0 + h + 1])
            pts(tf[:, h, :], tf[:, h, :], cst[:, NC1 + h:NC1 + h + 1], cst[:, C1 + h:C1 + h + 1], ALU.mult, ALU.add,
                [b_tf[h], b_cst], [b_tf[h]])

        def s2():
            S.op('dve', lambda e: e.tensor_tensor_scan(out=X.bS[:, :], data0=rmask[:, :], data1=X.lf[:, :], initial=0.0,
                                                       op0=ALU.mult, op1=ALU.add), [X.b_lf, b_rmask], [X.b_bS])

        def s3():
            act(X.eb[:, 0:8], blast, AF.Exp, [X.b_bS], [X.b_eb])
            tt(cview, b3, blast_bc, ALU.subtract, [X.b_bS], [X.b_lf])

        def s4():
            act(X.Epos[:, :], X.lf[:, :], AF.Exp, [X.b_lf], [X.b_Epos])
            act(X.bS[:, :], X.bS[:, :], AF.Exp, [X.b_bS], [X.b_bS])
            act(X.lf[:, :], X.lf[:, :], AF.Exp, [X.b_lf], [X.b_lf], scale=-1.0)

        def s5():
            tt(X.KbT[:, :], tf[:, h, :], X.lf[:, :], ALU.mult, [b_tf[h], X.b_lf], [X.b_KbT])
            tt(X.QdT[:, :], qT[:, h, :], X.Epos[:, :], ALU.mult, [b_qT[h], X.b_Epos], [X.b_QdT])
            tt(X.QbT[:, :], qT[:, h, :], X.bS[:, :], ALU.mult, [b_qT[h], X.b_bS], [X.b_QbT])

        def s6():
            for j in range(4):
                tp(ptr[:, pt + j * 128:pt + (j + 1) * 128], X.KbT[:, j * 128:(j + 1) * 128], ident[:, :], [X.b_KbT, b_ident], [b_ptr], signal=(j == 3))

        def s7():
            act(X.Kbtm[:].rearrange("p j d -> p (j d)"), ptr[:, pt:pt + 512], AF.Copy, [b_ptr], [X.b_Kbtm])

        def s8():
            for ci, (j, p0, C, col0) in enumerate(G.chunks):
                pu, bpu = (pu0, b_pu0) if ci % 2 == 0 else (pu1, b_pu1)
                mm(pu[:, (ci // 2) * 128:(ci // 2 + 1) * 128], X.Kbtm[p0:p0 + C, j, :], V[p0:p0 + C, j, h * 128:(h + 1) * 128], True, True,
                   [X.b_Kbtm, b_V[j][half]], [bpu], signal=(ci >= 6))

        def s9():
            act(X.Sbf[:, 0, :], Sst[:, h, :], AF.Copy, [b_Sst[h]], [X.b_Sbf])
            for ci in range(8):
                pu, bpu = (pu0, b_pu0) if ci % 2 == 0 else (pu1, b_pu1)
                src, bsrc = (Sst[:, h, :], b_Sst[h]) if ci == 0 else (X.Sch[:, ci, :], X.b_Sch[ci])
                dst, bdst = (Sst[:, h, :], b_Sst[h]) if ci == 7 else (X.Sch[:, ci + 1, :], X.b_Sch[ci + 1])
                stt(dst, src, X.eb[:, ci:ci + 1], pu[:, (ci // 2) * 128:(ci // 2 + 1) * 128], ALU.mult, ALU.add,
                    [bsrc, X.b_eb, bpu], [bdst])

        def s10():
            S.op('act', lambda e: e.activation(out=X.Sbf[:, 1:8, :].rearrange("p c e -> p (c e)"), in_=X.Sch[:, 1:8, :].rearrange("p c e -> p (c e)"),
                                               func=AF.Copy), X.b_Sch[1:8], [X.b_Sbf])

        def s11():
            for ci, (j, p0, C, col0) in enumerate(G.chunks):
                mm(pa[p0:p0 + CH, pc + (ci // 2) * CH:pc + (ci // 2 + 1) * CH], X.KbT[:, col0:col0 + CH], X.QdT[:, col0:col0 + CH], True, True,
                   [X.b_KbT, X.b_QdT], [b_pa], signal=(ci == 7))

        def s12():
            tt(X.AT[:, :], pa[:, pc:pc + 256], cmask[:, :], ALU.mult, [b_pa, b_cmask], [X.b_AT])

        def s13():
            for ci, (j, p0, C, col0) in enumerate(G.chunks):
                po, b_po = pos[ci % 2]
                oc = (ci // 2) * CH
                mm(po[:, pc + oc:pc + oc + CH], V[p0:p0 + CH, j, h * 128:(h + 1) * 128], X.AT[p0:p0 + CH, oc:oc + CH],
                   True, False, [b_V[j][half], X.b_AT], [b_po], signal=False)
                mm(po[:, pc + oc:pc + oc + CH], X.Sbf[:, ci, :], X.QbT[:, col0:col0 + CH], False, True, [X.b_Sbf, X.b_QbT], [b_po], signal=(ci >= 6))

        def s14():
            for par in range(2):
                act(par_view(X.Epos[:, :], par), po_view(par), AF.Square, [pos[par][1]], [X.b_Epos])

        def s15():
            ps, bps = next_pp()
            X.ps, X.bps = ps, bps
            mm(ps[:, :], ones32[:, :], X.Epos[:, :], True, True, [b_ones, X.b_Epos], [bps], signal=True)

        def s16():
            act(X.lf[:, :], X.ps[:, :], AF.Ln, [X.bps, b_cst], [X.b_lf], scale=1.0 / 128, bias=cst[:, CEPS:CEPS + 1])
            act(X.lf[:, :], X.lf[:, :], AF.Exp, [X.b_lf], [X.b_lf], scale=-0.5)

        def s17():
            for par in range(2):
                tt(par_view(X.lf[:, :], par), po_view(par), par_view(X.lf[:, :], par), ALU.mult, [pos[par][1], X.b_lf], [X.b_lf])
            stt(yaT[:, h, :], X.lf[:, :], cv[:, CV_GH:CV_GH + 1], sog[:, h, :], ALU.mult, ALU.mult, [X.b_lf, b_cv, b_sog[h]], [b_yaT[h]])
        return [s1, s2, s3, s4, s5, s6, s7, s8, s9, s10, s11, s12, s13, s14, s15, s16, s17]

    def heads_phase_b(G):
        for hp in range(4):
            ca = head_chain(G, 2 * hp, TS0)
            cb = head_chain(G, 2 * hp + 1, TS1)
            ca[0]()
            for k in range(len(ca)):
                if k + 1 < len(ca):
                    ca[k + 1]()
                cb[k]()

    def pool_proj(G, col_off):
        n = G.ntok
        nb = b_nT[0:len(G.tiles)]
        ws, bws = w_next("w_in", 0, 8, PL0, 512)
        for gi in range(4):
            ps, bps = next_pp()
            proj_fm(ws, bws, gi * 128, 8, nT, nb, n, ps, bps)
            if col_off:
                act(uT[:, gi, col_off:col_off + n], ps[:, 0:n], AF.Copy, [bps], [b_uT[gi]])
            else:
                cp(uT[:, gi, col_off:col_off + n], ps[:, 0:n], [bps], [b_uT[gi]])

    def pool_branch(G):
        L = NHALO + T
        b_pool = [Buf(), Buf(), Buf()]
        retire(b_pool, b_tmp_all)
        b_sA, b_sB, b_pl = b_pool
        for gi in range(4):
            w = 2 ** (gi + 1)
            cur, bcur = uT[:, gi, :], b_uT[gi]
            lo = 0
            bufs = [(sA, b_sA), (sB, b_sB)]
            st = 1
            k = 0
            while st < w:
                nxt, bn = bufs[k % 2]
                lo2 = lo + st
                tt(nxt[:, lo2:L], cur[:, lo2:L], cur[:, lo2 - st:L - st], ALU.add, [bcur], [bn])
                cur, bcur = nxt[:, :], bn
                lo = lo2
                st *= 2
                k += 1
            stt(pooledT[:, gi, :], cur[:, NHALO:L], 1.0 / w, uT[:, gi, NHALO:L], ALU.mult, ALU.subtract, [bcur, b_uT[gi]], [b_pl])
        for gi in range(4):
            ps, bps = next_pp()
            mm(ps[:, :], pw[:, gi, :], pooledT[:, gi, :], True, True, [b_pw, b_pl], [bps], signal=True)
            ts(ybT[:, gi, :], ps[:, :], cv[:, CV_PS + gi:CV_PS + gi + 1], None, ALU.mult, None, [bps, b_cv], [b_ybT[gi]])
            cp(uT[:, gi, 0:NHALO], uT[:, gi, T:T + NHALO], [b_uT[gi]], [b_uT[gi]])
        retire(b_tmp_all, b_pool)

    def merge_branches(G):
        b_tga = [Buf() for _ in range(4)]
        b_tgb = [Buf() for _ in range(4)]
        retire(b_tga + b_tgb, b_qT)
        for dh in range(2):
            for (c0, dst, bd) in ((GA0, tga, b_tga), (GB0, tgb, b_tgb)):
                ws, bws = w_next("w_in", 0, 8, c0 + dh * 512, 512)
                for i in range(4):
                    ps, bps = next_pp()
                    proj_fm(ws, bws, i * 128, 8, nT, b_nT, T, ps, bps)
                    act(dst[:, i, :], ps[:, :], AF.Tanh, [bps], [bd[i]], scale=0.5)
            ws, bws = w_next("w_ba", 0, 8, dh * 512, 512)
            for i in range(4):
                ps, bps = next_pp()
                for h in range(8):
                    mm(ps[:, :], ws[:, h, i * 128:(i + 1) * 128], yaT[:, h, :], h == 0, h == 7, [bws, b_yaT[h]], [bps], signal=(h == 7))
                stt(tga[:, i, :], tga[:, i, :], 1.0, ps[:, :], ALU.add, ALU.mult, [b_tga[i], bps], [b_tga[i]])
            ws, bws = w_next("w_bb", 0, 4, dh * 512, 512)
            for i in range(4):
                ps, bps = next_pp()
                for gi in range(4):
                    mm(ps[:, :], ws[:, gi, i * 128:(i + 1) * 128], ybT[:, gi, :], gi == 0, gi == 3, [bws, b_ybT[gi]], [bps], signal=(gi == 3))
                stt(tgb[:, i, :], tgb[:, i, :], 1.0, ps[:, :], ALU.add, ALU.mult, [b_tgb[i], bps], [b_tgb[i]])
                tt(mergedT[:, dh * 4 + i, :], tga[:, i, :], tgb[:, i, :], ALU.add, [b_tga[i], b_tgb[i]], [b_mg[dh * 4 + i]])
        retire(b_qT, b_tga + b_tgb)

    def out_proj(G):
        for half in range(2):
            ws, bws = w_next("w_out", 0, 8, half * 512, 512)
            for j in range(4):
                ps, bps = next_pp()
                for k in range(8):
                    mm(ps[:, :], mergedT[:, k, j * 128:(j + 1) * 128], ws[:, k, :], k == 0, k == 7, [bws, b_mg[k]], [bps], signal=(k == 7))
                xs = xt[:, j, half * 512:(half + 1) * 512]
                stt(xs, ps[:, :], 0.5, xs, ALU.mult, ALU.add, [bps, b_xt[j][half]], [b_xt[j][half]])

    def ffn(G):
        b_sg = [Buf() for _ in range(4)]
        retire(b_sg, b_tf)
        for fblk in range(6):
            ncols = 512 if fblk < 5 else 256
            nfb = ncols // 128
            ws, bws = w_next("w_g", 0, 8, fblk * 512, ncols)
            for fb in range(nfb):
                ps, bps = next_pp()
                proj_fm(ws, bws, fb * 128, 8, nT, b_nT, T, ps, bps)
                act(sg[:, fb, :], ps[:, :], SILU, [bps], [b_sg[fb]])
            ws, bws = w_next("w_u", 0, 8, fblk * 512, ncols)
            for fb in range(nfb):
                ps, bps = next_pp()
                proj_fm(ws, bws, fb * 128, 8, nT, b_nT, T, ps, bps)
                tt(hidT[:, fblk * 4 + fb, :], sg[:, fb, :], ps[:, :], ALU.mult, [b_sg[fb], bps], [b_hid[fblk * 4 + fb]])
        retire(b_tf, b_sg)
        for half in range(2):
            for kg, (kc0, nk) in enumerate(((0, 8), (8, 8), (16, 6))):
                ws, bws = w_next("w_d", kc0, nk, half * 512, 512)
                for j in range(4):
                    for k in range(nk):
                        mm(pacc[j][:, :], hidT[:, kc0 + k, j * 128:(j + 1) * 128], ws[:, k, :], kg == 0 and k == 0, kg == 2 and k == nk - 1,
                           [bws, b_hid[kc0 + k]], [b_pacc[j]], signal=(k == nk - 1))
            for j in range(4):
                xs = xt[:, j, half * 512:(half + 1) * 512]
                tt(xs, pacc[j][:, :], xs, ALU.add, [b_pacc[j], b_xt[j][half]], [b_xt[j][half]])

    def final_norm_store(G):
        for j in range(4):
            act(junk[:, :], xt[:, j, :], AF.Square, b_xt[j], [b_junk, b_stat], accum=stat[:, j:j + 1])
        act(stat[:, 8:12], stat[:, 0:4], AF.Ln, [b_stat, b_cst], [b_stat], scale=1.0 / D, bias=cst[:, CEPS:CEPS + 1])
        act(stat[:, 8:12], stat[:, 8:12], AF.Exp, [b_stat], [b_stat], scale=-0.5)
        for j in range(4):
            stt(xt[:, j, :], xt[:, j, :], stat[:, 8 + j:9 + j], gbc[2][:, :], ALU.mult, ALU.mult, b_xt[j] + [b_stat, b_gbc[2]], b_xt[j])
            r0 = G.g * T + j * 128
            S.dma('sp', y[r0:r0 + 128, :], xt[:, j, :], sem_y[j], reads=b_xt[j])

    def phase_a_group(G, do_load=True):
        if do_load:
            load_x(G)
        if not G.halo: hook[0]('A_load')
        norm_T(G, 0, nT, b_nT)
        if not G.halo: hook[0]('A_norm')
        for half in range(2):
            f_proj(G, half, with_q=False, resident=(mode == "R"))
        if not G.halo: hook[0]('A_fproj')
        if G.halo and mode != "R":
            pool_proj(G, 0)
        if mode == "R" and not G.halo:
            heads_state2(G)
        else:
            for h in range(8):
                head_state(G, h, phaseB=False)

    def emit_all(stage):
        hook[0] = stage
        stage('setup')
        if mode == "R":
            for bi, c0_ in enumerate((F0, I0, F0 + 512, I0 + 512)):
                S.dma('pool', WR[:, bi], wd["w_in"].rearrange("(k p) c -> p k c", p=128)[:, :, c0_:c0_ + 512], sem_wr, writes=[b_WR])
            b_WR.writer = (sem_wr, sem_wr.count)
            retire(b_lf2 + b_bS2 + [b_onesT], b_qT)
            retire(b_KbT2 + b_Kbtm2, b_sog)
            S.op('dve', lambda e: e.memset(onesT[:], 1.0), writes=[b_onesT])
            ts(cst[:, 41:53], cv[:, 40:52], -1.0, 1.0, ALU.mult, ALU.add, [b_cv], [b_cst])
            Gm = halo_group()
            Gm.src = xmeta
            phase_a_group(Gm)
            Gh = halo_group()
            load_x(Gh)
            norm_T(Gh, 0, nT, b_nT)
            pool_proj(Gh, 0)
            retire([b for jj in b_xt2 for b in jj], b_ws[1:3])

            def pred_group(gi):
                Gp = main_group(gi)
                Gp.src = xp
                Gp.flagcol = 40 + gi
                if gi % 2 == 1:
                    Gp.xt, Gp.b_xt = xt2, b_xt2
                return Gp
            npg = NPRED * NG
            retire(b_tfB, b_tmp_all)
            retire([b for jj in b_VB for b in jj], b_stg)
            Gs = []
            for gi in range(npg):
                Gp = pred_group(gi)
                if gi % 2 == 1:
                    Gp.tf, Gp.b_tf, Gp.V, Gp.b_V = tfB, b_tfB, VB, b_VB
                Gs.append(Gp)
            load_x(Gs[0])
            if npg > 1:
                load_x(Gs[1])
            norm_T(Gs[0], 0, nT, b_nT)
            for u in rescan_units(Gs[0]):
                u()
            for gi in range(npg):
                if gi + 2 < npg:
                    load_x(Gs[gi + 2])
                if gi + 1 < npg:
                    norm_T(Gs[gi + 1], 0, nT, b_nT)
                    heads_state2(Gs[gi], rescan_units(Gs[gi + 1]))
                else:
                    heads_state2(Gs[gi])
            stage('phaseA')
            retire(b_mg + b_hid, [b_WR])
            retire(b_qT, b_lf2 + b_bS2 + [b_onesT])
            retire(b_sog, b_KbT2 + b_Kbtm2)
            retire(b_ws[1:3], [b for jj in b_xt2 for b in jj])
            retire(b_tmp_all, b_tfB)
            retire(b_ts1_all, [b_WR] + [b for jj in b_VB for b in jj])
            hold_prefetch[0] = False
            phase_b(stage)
            return
        phase_a_group(halo_group())
        stage('halo')
        cp(Sh[:], Sst[:], b_Sst, [b_Sh])
        act(Dh[:], Bsum[:], AF.Exp, [b_Bsum], [b_Dh])
        b_xtmp = Buf("xtmp")
        b_gin, b_gout = Buf("gin"), Buf("gout")
        if mode != "B":
            for g in range(NG):
                phase_a_group(main_group(g))
            stage('phaseA')
            retire([b_xtmp], b_hid)
            cp(xtmp[:, 0:1024], Sst[:].rearrange("p h e -> p (h e)"), b_Sst, [b_xtmp])
            act(xtmp[:, 1024:1032], Bsum[:], AF.Exp, [b_Bsum], [b_xtmp])
        else:
            retire([b_xtmp], b_hid)
        if mode == "A":
            S.dma('sp', su, xtmp[:], sem_g2, reads=[b_xtmp])
            return
        if mode == "fused":
            S.dma('pool', gin.ap(), xtmp[:], sem_g, reads=[b_xtmp], writes=[b_gin])
            S.deps('pool', [b_gin], [b_gout])
            conv_issue(len(conv))
            S.wait_all('pool', sem_w + [sem_g, sem_pw, sem_cv])
            b_wsc.writer = (sem_cv, sem_cv.count)
            nc.gpsimd.collective_compute("AllGather", ALU.bypass, replica_groups=[list(range(NCORES))],
                                         ins=[gin.ap().opt()], outs=[gout.ap().opt()]).then_inc(sem_cc.h, 1)
            sem_cc.count += 1
            S.mark((sem_cc, sem_cc.count), [b_gin], [b_gout])
            S.wait_all('pool', [sem_cc])
            post_cc[0] = True
            gsrc = gout.ap()
        else:
            gsrc = gall
        S.op('dve', lambda e: e.memset(Sst[:], 0.0), writes=b_Sst)
        for j in range(NCORES):
            S.dma('sp', xtmp[:], gsrc[j * 128:(j + 1) * 128, :], sem_g2, reads=[b_gout], writes=[b_xtmp])
            ts(stat[:, 20:28], xtmp[:, 1024:1032], cv[:, CV_M + j:CV_M + j + 1], cst[:, OMM + j:OMM + j + 1], ALU.mult, ALU.add,
               [b_xtmp, b_cv, b_cst], [b_stat])
            ts(xtmp[:, 0:1024], xtmp[:, 0:1024], cv[:, CV_M + j:CV_M + j + 1], None, ALU.mult, None, [b_xtmp, b_cv], [b_xtmp])
            for h in range(8):
                stt(Sst[:, h, :], Sst[:, h, :], stat[:, 20 + h:21 + h], xtmp[:, h * 128:(h + 1) * 128], ALU.mult, ALU.add,
                    [b_Sst[h], b_stat, b_xtmp], [b_Sst[h]])
        for h in range(8):
            stt(Sst[:, h, :], Sst[:, h, :], Dh[:, h:h + 1], Sh[:, h, :], ALU.mult, ALU.add, [b_Sst[h], b_Dh, b_Sh], [b_Sst[h]])
        retire(b_hid, [b_xtmp])
        dv = os.environ.get('K_DUMMY')
        if dv:
            sem_dm = S.newsem("d_dm")
            bdm = Buf("dm")
            retire([bdm], b_hid)
            if dv == '1':
                S.dma('sp', hidT[:, 0, :], wsc['w_in'][0:128, 0:512], sem_dm, writes=[bdm])
            elif dv == '2':
                S.dma('sp', xtmp[:, 0:512], wd['w_in'][0:128, 0:512], sem_dm, writes=[bdm])
            elif dv == '3':
                S.dma('sp', xtmp[:, 0:512], xm[0:128, 0:512], sem_dm, writes=[bdm])
            S.wait_all('sp', [sem_dm])
        stage('exch')

        phase_b(stage)

    def phase_b(stage):
        for g in range(NG):
            G = main_group(g)
            load_x(G)
            stage('B_load')
            norm_T(G, 0, nT, b_nT)
            stage('B_norm')
            for half in range(2):
                f_proj(G, half, with_q=True)
            stage('B_fproj')
            if mode == "R":
                heads_phase_b(G)
            else:
                for h in range(8):
                    head_state(G, h, phaseB=True)
                    head_out(G, h)
            stage('B_heads')
            pool_proj(G, NHALO)
            pool_branch(G)
            stage('B_pool')
            merge_branches(G)
            stage('B_merge')
            out_proj(G)
            stage('B_out')
            norm_T(G, 1, nT, b_nT)
            ffn(G)
            stage('B_ffn')
            final_norm_store(G)
            stage('B_g%d' % g)


    class StopBuild(Exception):
        pass
    STOP = os.environ.get('K_STOP', '')

    def stage(name):
        if STOP == name:
            raise StopBuild()

    try:
        emit_all(stage)
    except StopBuild:
        print("STOPPED at", STOP)
    else:
        assert wstate["used"] == len(plan), (wstate, len(plan))
    S.wait_all('sp', sem_y + [sem_misc, sem_g2] + sem_x + sem_x2 + sem_stg)
    S.wait_all('pool', sem_w + [sem_g, sem_cc, sem_pw, sem_cv, sem_wr])
    S.wait_all('act', [S.esem['dve'], S.esem['pe']])
    S.wait_all('dve', [S.esem['act']])
    return nc


_CACHE = {}


def kernel(x, meta_tokens, norm_mix_g, w_in, lb_raw, hgrn_norm_g, pool_w, pool_scale,
           w_branch_a, w_branch_b, w_out, norm_ffn_g, w_ffn_gate, w_ffn_up, w_ffn_down, norm_final_g):
    f = lambda a: np.ascontiguousarray(np.asarray(a, dtype=np.float32))
    x = f(x)
    meta = f(meta_tokens)
    B = x.shape[0]
    segs = NCORES // B
    MODE = os.environ.get("K_MODE", "R")
    if MODE not in _CACHE:
        _CACHE[MODE] = (build_program({"fused": "fused", "R": "R"}[MODE]),) if MODE in ("fused", "R") else (build_program("A"), build_program("B"))
    progs = _CACHE[MODE]
    shared = {
        "w_in": f(w_in[0]), "w_ba": f(w_branch_a[0]), "w_bb": f(w_branch_b[0]), "w_out": f(w_out[0]),
        "w_g": f(w_ffn_gate[0]), "w_u": f(w_ffn_up[0]), "w_d": f(w_ffn_down[0]), "pool_w": f(pool_w[0]),
        "gvec": np.ascontiguousarray(np.stack([f(norm_mix_g[0]), f(norm_ffn_g[0]), f(norm_final_g)], 0)),
    }
    lb = f(lb_raw)
    in_maps = []
    for c in range(NCORES):
        b, s = divmod(c, segs)
        cvec = np.zeros((128, NCV), np.float32)
        cvec[:, 0:8] = lb[0].reshape(H, 128).T
        cvec[:, 8:16] = lb[1].reshape(H, 128).T
        cvec[:, 16] = f(hgrn_norm_g[0])
        cvec[:, 17:21] = f(pool_scale[0]).reshape(4, 128).T
        cvec[:, 21] = 1.0 if (s == 0 or MODE == "R") else 0.0
        if MODE == "R":
            xp = np.zeros((NPRED * NTOK, D), np.float32)
            npre = min(s, NPRED) * NTOK
            if npre:
                xp[NPRED * NTOK - npre:] = x[b, s * NTOK - npre:s * NTOK]
            for gi in range(NPRED * NG):
                cvec[:, 40 + gi] = 1.0 if gi * T >= NPRED * NTOK - npre else 0.0
        for j in range(NCORES):
            bj, sj = divmod(j, segs)
            cvec[:, 22 + j] = 1.0 if (bj == b and sj < s) else 0.0
        xm = x[b, s * NTOK:(s + 1) * NTOK]
        xh = meta if s == 0 else x[b, s * NTOK - NHALO:s * NTOK]
        m = dict(shared)
        m.update({"xm": np.ascontiguousarray(xm), "xh": np.ascontiguousarray(xh), "cvec": cvec})
        if MODE == "R":
            m.update({"xp": xp, "xmeta": meta})
        in_maps.append(m)
    if MODE in ("fused", "R"):
        res = run_bass_kernel_spmd(progs[0], in_maps, core_ids=list(range(NCORES)))
    else:
        ra = run_bass_kernel_spmd(progs[0], in_maps, core_ids=list(range(NCORES)))
        gall = np.ascontiguousarray(np.concatenate([ra.results[c]["su"] for c in range(NCORES)], 0))
        for m in in_maps:
            m["gall"] = gall
        res = run_bass_kernel_spmd(progs[1], in_maps, core_ids=list(range(NCORES)))
    out = np.empty((B, segs * NTOK, D), np.float32)
    for c in range(NCORES):
        b, s = divmod(c, segs)
        out[b, s * NTOK:(s + 1) * NTOK] = res.results[c]["y"]
    return out
```

```python
import numpy as np
import concourse.bass as bass
import concourse.mybir as mybir
from concourse.bass_utils import run_bass_kernel_spmd

F32 = mybir.dt.float32
BF16 = mybir.dt.bfloat16
AF = mybir.ActivationFunctionType
import os as _os
SILU = AF.Tanh if _os.environ.get('K_NOSILU') else AF.Silu
ALU = mybir.AluOpType
AX = mybir.AxisListType

NCORES = 8
D = 1024
H = 8
import os
NTOK = int(os.environ.get('K_NTOK', '2048'))
T = 512
NG = NTOK // T
CH = 64
NHALO = 16
DFF = 2816
INW = 6656
Q0, F0, I0, OG0, PL0, GA0, GB0 = 0, 1024, 2048, 3072, 4096, 4608, 5632
EPS = 1e-6
NCV = 56
NPRED = 3
SB_BASE = 16512
SB_LIMIT = 229376
NSLOT = 3
PREFETCH = 2


class Sem:
    def __init__(self, h):
        self.h = h
        self.count = 0


class Buf:
    __slots__ = ("name", "writer", "readers")

    def __init__(self, name=""):
        self.name = name
        self.writer = None
        self.readers = {}


class Sched:
    def __init__(self, nc):
        self.nc = nc
        self.eng = {'pe': nc.tensor, 'act': nc.scalar, 'dve': nc.vector, 'pool': nc.gpsimd, 'sp': nc.sync}
        self.esem = {e: Sem(nc.alloc_semaphore(f"s_{e}")) for e in self.eng}
        self.known = {e: {} for e in self.eng}

    def newsem(self, name):
        return Sem(self.nc.alloc_semaphore(name))

    def deps(self, e, reads, writes):
        deps = {}
        pes = self.esem['pe']

        def add(s, v):
            if e == 'pe' and s is pes:
                return
            if deps.get(s, 0) < v:
                deps[s] = v
        for b in reads:
            if b.writer is not None:
                add(*b.writer)
        for b in writes:
            if b.writer is not None:
                add(*b.writer)
            for s, v in b.readers.items():
                add(s, v)
        kn = self.known[e]
        for s, v in deps.items():
            if kn.get(s, 0) >= v:
                continue
            self.eng[e].wait_ge(s.h, v)
            kn[s] = v

    def mark(self, tag, reads, writes):
        s, v = tag
        for b in reads:
            if b.readers.get(s, 0) < v:
                b.readers[s] = v
        for b in writes:
            b.writer = tag
            b.readers = {}

    def op(self, e, fn, reads=(), writes=(), signal=True):
        self.deps(e, reads, writes)
        ins = fn(self.eng[e])
        s = self.esem[e]
        if signal:
            s.count += 1
            ins.then_inc(s.h, 1)
            tag = (s, s.count)
        else:
            tag = (s, s.count + 1)
        self.mark(tag, reads, writes)
        return ins

    def dma(self, e, out, in_, sem, reads=(), writes=(), **kw):
        self.deps(e, reads, writes)
        ins = self.eng[e].dma_start(out=out, in_=in_, **kw)
        sem.count += 16
        ins.then_inc(sem.h, 16)
        self.mark((sem, sem.count), reads, writes)
        return ins

    def wait_all(self, e, sems):
        for s in sems:
            if s.count > 0 and self.known[e].get(s, 0) < s.count:
                self.eng[e].wait_ge(s.h, s.count)
                self.known[e][s] = s.count


def _dtsize(dt):
    return 4 if dt == F32 else 2


class Arena:
    def __init__(self, nc):
        self.nc = nc
        self.top = SB_BASE
        self.n = 0

    def at(self, shape, dt, addr):
        self.n += 1
        return self.nc.alloc_sbuf_tensor_at(f"t{self.n}", list(shape), dt, offset=addr)

    def take(self, shape, dt):
        nbytes = int(np.prod(shape[1:])) * _dtsize(dt)
        nbytes = (nbytes + 63) // 64 * 64
        addr = self.top
        self.top += nbytes
        assert self.top <= SB_LIMIT, f"SBUF overflow {self.top}"
        return self.at(shape, dt, addr), addr


def weight_plan(mode="fused"):
    plan = []
    if mode == "R":
        plan.append(("w_in", 0, 8, PL0, 512))
    na = {"B": 0, "R": -1}.get(mode, NG)
    for g in range(-1, na):
        for half in range(2):
            plan.append(("w_in", 0, 8, F0 + half * 512, 512))
            plan.append(("w_in", 0, 8, I0 + half * 512, 512))
        if g == -1:
            plan.append(("w_in", 0, 8, PL0, 512))
    for g in range(NG if mode != "A" else 0):
        for half in range(2):
            plan.append(("w_in", 0, 8, F0 + half * 512, 512))
            plan.append(("w_in", 0, 8, Q0 + half * 512, 512))
            plan.append(("w_in", 0, 8, OG0 + half * 512, 512))
            plan.append(("w_in", 0, 8, I0 + half * 512, 512))
        plan.append(("w_in", 0, 8, PL0, 512))
        for dh in range(2):
            plan.append(("w_in", 0, 8, GA0 + dh * 512, 512))
            plan.append(("w_in", 0, 8, GB0 + dh * 512, 512))
            plan.append(("w_ba", 0, 8, dh * 512, 512))
            plan.append(("w_bb", 0, 4, dh * 512, 512))
        for half in range(2):
            plan.append(("w_out", 0, 8, half * 512, 512))
        for fblk in range(6):
            nc_ = 512 if fblk < 5 else 256
            plan.append(("w_g", 0, 8, fblk * 512, nc_))
            plan.append(("w_u", 0, 8, fblk * 512, nc_))
        for half in range(2):
            for kc0, nk in ((0, 8), (8, 8), (16, 6)):
                plan.append(("w_d", kc0, nk, half * 512, 512))
    return plan


def build_program(mode="fused"):
    nc = bass.Bass("TRN2", target_bir_lowering=False)
    S = Sched(nc)
    AR = Arena(nc)

    def din(name, shape):
        return nc.dram_tensor(name, list(shape), F32, kind="ExternalInput").ap()

    xm = din("xm", [NTOK, D])
    xh = din("xh", [NHALO, D])
    if mode == "R":
        xmeta = din("xmeta", [NHALO, D])
        xp = din("xp", [NPRED * NTOK, D])
    cvec = din("cvec", [128, NCV])
    gvec = din("gvec", [3, D])
    wd = {
        "w_in": din("w_in", [D, INW]), "w_ba": din("w_ba", [D, D]), "w_bb": din("w_bb", [512, D]),
        "w_out": din("w_out", [D, D]), "w_g": din("w_g", [D, DFF]), "w_u": din("w_u", [D, DFF]),
        "w_d": din("w_d", [DFF, D]),
    }
    pool_w = din("pool_w", [4, 128, 128])
    wshape = {"w_in": [D, INW], "w_ba": [D, D], "w_bb": [512, D], "w_out": [D, D], "w_g": [D, DFF], "w_u": [D, DFF], "w_d": [DFF, D]}
    wsc = {}
    if mode == "A":
        su = nc.dram_tensor("su", [128, 1032], F32, kind="ExternalOutput").ap()
        y = None
    else:
        y = nc.dram_tensor("y", [NTOK, D], F32, kind="ExternalOutput").ap()
    if mode == "B":
        gall = din("gall", [NCORES * 128, 1032])
    if mode == "fused":
        gin = nc.dram_tensor("gin", [128, 1032], F32)
        gout = nc.dram_tensor("gout", [NCORES * 128, 1032], F32)

    def T_(shape, dt):
        return AR.take(shape, dt)[0]

    a_ws0 = AR.top
    wslot = [T_([128, 8, 512], BF16) for _ in range(NSLOT)]; b_ws = [Buf() for _ in range(NSLOT)]
    xt2 = AR.at([128, 4, D], F32, a_ws0 + 8192); b_xt2 = [[Buf(), Buf()] for _ in range(4)]
    cv = T_([128, NCV], F32); b_cv = Buf("cv")
    cst = T_([128, 64], F32); b_cst = Buf("cst")
    gbc = [T_([128, D], F32) for _ in range(3)]; b_gbc = [Buf() for _ in range(3)]
    pw = T_([128, 4, 128], BF16); b_pw = Buf("pw")
    ident = T_([128, 128], BF16); b_ident = Buf("ident")
    ones32 = T_([128, 128], F32); b_ones = Buf("ones")
    rmask = T_([128, T], F32); b_rmask = Buf("rmask")
    cmask = T_([128, 256], F32); b_cmask = Buf("cmask")
    xt = T_([128, 4, D], F32); b_xt = [[Buf(), Buf()] for _ in range(4)]
    junk = T_([128, D], BF16); b_junk = Buf("junk")
    ntm = [T_([128, D], BF16) for _ in range(2)]; b_ntm = [Buf(), Buf()]
    nT = T_([128, 8, T], BF16); b_nT = [Buf() for _ in range(4)]
    stat = T_([128, 32], F32); b_stat = Buf("stat")
    tf, a_tf = AR.take([128, 8, T], F32); b_tf = [Buf() for _ in range(8)]
    qT, a_qT = AR.take([128, 8, T], F32); b_qT = [Buf() for _ in range(8)]
    sog, a_sog = AR.take([128, 8, T], BF16); b_sog = [Buf() for _ in range(8)]
    V = T_([128, 4, D], BF16); b_V = [[Buf(), Buf()] for _ in range(4)]
    lf, a_tmp = AR.take([128, T], F32); b_lf = Buf("lf")
    bS = T_([128, T], F32); b_bS = Buf("bS")
    Epos = T_([128, T], F32); b_Epos = Buf("Epos")
    QdT = T_([128, T], BF16); b_QdT = Buf("QdT")
    KbT = T_([128, T], BF16); b_KbT = Buf("KbT")
    QbT = T_([128, T], BF16); b_QbT = Buf("QbT")
    Kbtm = T_([128, 4, 128], BF16); b_Kbtm = Buf("Kbtm")
    ATs = T_([128, 256], BF16); b_AT = Buf("AT")
    Sch = T_([128, 9, 128], F32); b_Sch = [Buf() for _ in range(9)]
    Sbf = T_([128, 8, 128], BF16); b_Sbf = Buf("Sbf")
    eb = T_([128, 16], F32); b_eb = Buf("eb")
    a_tmp_end = AR.top
    Sst = T_([128, 8, 128], F32); b_Sst = [Buf() for _ in range(8)]
    Bsum = T_([128, 8], F32); b_Bsum = Buf("Bsum")
    b_Sh = Buf("Sh")
    Dh = T_([128, 8], F32); b_Dh = Buf("Dh")
    yaT = T_([128, 8, T], BF16); b_yaT = [Buf() for _ in range(8)]
    ybT = T_([128, 4, T], BF16); b_ybT = [Buf() for _ in range(4)]
    uT = T_([128, 4, NHALO + T], F32); b_uT = [Buf() for _ in range(4)]
    mergedT, a_mg = AR.take([128, 8, T], BF16); b_mg = [Buf() for _ in range(8)]
    hidT, a_hid = AR.take([128, 22, T], BF16); b_hid = [Buf() for _ in range(22)]
    wstg = [T_([128, 8, 256], F32) for _ in range(2)]; b_stg = [Buf(), Buf()]
    a_stg1 = AR.top - 8192
    a_ts1 = AR.top - 16384
    print("SBUF top", AR.top, "limit", SB_LIMIT)
    sA = AR.at([128, NHALO + T], F32, a_tmp); sB = AR.at([128, NHALO + T], F32, a_tmp + 2176)
    pooledT = AR.at([128, 4, T], BF16, a_tmp + 4352)
    assert a_tmp + 4352 + 4096 <= a_tmp_end
    b_tmp_all = [b_lf, b_bS, b_Epos, b_QdT, b_KbT, b_QbT, b_Kbtm, b_AT, b_Sbf, b_eb] + b_Sch
    tga = AR.at([128, 4, T], F32, a_qT); tgb = AR.at([128, 4, T], F32, a_qT + 8192)
    sg = AR.at([128, 4, T], F32, a_tf)
    xtmp = AR.at([128, 1032], F32, a_hid)
    WR = AR.at([128, 4, 8, 512], BF16, a_mg)
    assert a_mg + 4 * 8 * 512 * 2 <= AR.top and a_hid == a_mg + 8 * T * 2
    b_WR = Buf("WR")
    lf2 = [AR.at([128, T], F32, a_qT + i * 2048) for i in range(3)]
    bS2 = [AR.at([128, T], F32, a_qT + 6144 + i * 2048) for i in range(3)]
    onesT = AR.at([128, T], F32, a_qT + 12288)
    KbT2 = [AR.at([128, T], BF16, a_sog + i * 1024) for i in range(2)]
    Kbtm2 = [AR.at([128, 4, 128], BF16, a_sog + 2048 + i * 1024) for i in range(2)]
    b_lf2 = [Buf() for _ in range(3)]; b_bS2 = [Buf() for _ in range(3)]; b_KbT2 = [Buf(), Buf()]; b_Kbtm2 = [Buf(), Buf()]; b_onesT = Buf()
    b_eb2 = [Buf() for _ in range(4)]
    tfB = AR.at([128, 8, T], F32, a_tmp); b_tfB = [Buf() for _ in range(8)]
    assert a_tmp + 8 * T * 4 <= a_tmp_end
    VB = AR.at([128, 4, D], BF16, a_stg1); b_VB = [[Buf(), Buf()] for _ in range(4)]
    Sh = AR.at([128, 8, 128], F32, a_hid + 4160)
    assert 4160 + 4096 <= 22 * T * 2

    class TS:
        pass
    TS0 = TS()
    TS0.lf, TS0.bS, TS0.Epos, TS0.QdT, TS0.KbT, TS0.QbT, TS0.Kbtm, TS0.AT, TS0.Sch, TS0.Sbf, TS0.eb = lf, bS, Epos, QdT, KbT, QbT, Kbtm, ATs, Sch, Sbf, eb
    TS0.b_lf, TS0.b_bS, TS0.b_Epos, TS0.b_QdT, TS0.b_KbT, TS0.b_QbT, TS0.b_Kbtm, TS0.b_AT, TS0.b_Sch, TS0.b_Sbf, TS0.b_eb = \
        b_lf, b_bS, b_Epos, b_QdT, b_KbT, b_QbT, b_Kbtm, b_AT, b_Sch, b_Sbf, b_eb
    TS0.pc = 0
    TS1 = TS()
    _o = a_ts1
    TS1.lf = AR.at([128, T], F32, _o); TS1.bS = AR.at([128, T], F32, _o + 2048); TS1.Epos = AR.at([128, T], F32, _o + 4096)
    TS1.QdT = AR.at([128, T], BF16, _o + 6144); TS1.KbT = AR.at([128, T], BF16, _o + 7168); TS1.QbT = AR.at([128, T], BF16, _o + 8192)
    TS1.Kbtm = AR.at([128, 4, 128], BF16, _o + 9216); TS1.AT = AR.at([128, 256], BF16, _o + 10240)
    TS1.Sch = AR.at([128, 9, 128], F32, _o + 10752); TS1.Sbf = AR.at([128, 8, 128], BF16, _o + 15360); TS1.eb = AR.at([128, 8], F32, _o + 17408)
    assert _o + 17408 + 32 <= SB_LIMIT - 8, (_o, SB_LIMIT)
    TS1.b_lf, TS1.b_bS, TS1.b_Epos, TS1.b_QdT, TS1.b_KbT, TS1.b_QbT, TS1.b_Kbtm, TS1.b_AT, TS1.b_Sbf, TS1.b_eb = [Buf() for _ in range(10)]
    TS1.b_Sch = [Buf() for _ in range(9)]
    TS1.pc = 256
    b_ts1_all = [TS1.b_lf, TS1.b_bS, TS1.b_Epos, TS1.b_QdT, TS1.b_KbT, TS1.b_QbT, TS1.b_Kbtm, TS1.b_AT, TS1.b_Sbf, TS1.b_eb] + TS1.b_Sch
    NPP = 2
    pp = [nc.alloc_psum_tensor(f"pp{i}", [128, 512], F32) for i in range(NPP)]; b_pp = [Buf() for _ in range(NPP)]
    ptr = nc.alloc_psum_tensor("ptr", [128, 1024], BF16); b_ptr = Buf("ptr")
    pacc5 = [nc.alloc_psum_tensor(f"pa{i}", [128, 512], F32) for i in range(5)]; b_pacc5 = [Buf() for _ in range(5)]
    pa, po0, po1, pu0, pu1 = pacc5
    b_pa, b_po0, b_po1, b_pu0, b_pu1 = b_pacc5
    pacc = pacc5[0:4]; b_pacc = b_pacc5[0:4]
    pp_rr = [0]

    def next_pp():
        i = pp_rr[0] % NPP
        pp_rr[0] += 1
        return pp[i], b_pp[i]

    def act(out, in_, func, reads, writes, scale=1.0, bias=None, accum=None):
        kw = {}
        if bias is not None:
            kw["bias"] = bias
        if accum is not None:
            kw["accum_out"] = accum
        return S.op('act', lambda e: e.activation(out=out, in_=in_, func=func, scale=scale, **kw), reads, writes)

    def tt(out, in0, in1, op, reads, writes):
        return S.op('dve', lambda e: e.tensor_tensor(out=out, in0=in0, in1=in1, op=op), reads, writes)

    def stt(out, in0, scalar, in1, op0, op1, reads, writes):
        return S.op('dve', lambda e: e.scalar_tensor_tensor(out=out, in0=in0, scalar=scalar, in1=in1, op0=op0, op1=op1),
                    reads, writes)

    def ts(out, in0, s1, s2, op0, op1, reads, writes):
        if s2 is None:
            return S.op('dve', lambda e: e.tensor_scalar(out=out, in0=in0, scalar1=s1, scalar2=None, op0=op0), reads, writes)
        return S.op('dve', lambda e: e.tensor_scalar(out=out, in0=in0, scalar1=s1, scalar2=s2, op0=op0, op1=op1), reads, writes)

    def ptt(out, in0, in1, op, reads, writes):
        return S.op('pool', lambda e: e.tensor_tensor(out=out, in0=in0, in1=in1, op=op), reads, writes)

    def pts(out, in0, s1, s2, op0, op1, reads, writes):
        return S.op('pool', lambda e: e.tensor_scalar(out=out, in0=in0, scalar1=s1, scalar2=s2, op0=op0, op1=op1), reads, writes)

    def cp(out, in_, reads, writes):
        return S.op('dve', lambda e: e.tensor_copy(out=out, in_=in_), reads, writes)

    def mm(out, lhsT, rhs, start, stop, reads, writes, signal):
        return S.op('pe', lambda e: e.matmul(out, lhsT=lhsT, rhs=rhs, start=start, stop=stop), reads, writes, signal=signal)

    def tp(out, in_, idn, reads, writes, signal):
        return S.op('pe', lambda e: e.transpose(out=out, in_=in_, identity=idn), reads, writes, signal=signal)

    def retire(new_bufs, old_bufs):
        acc = {}
        for b in old_bufs:
            if b.writer is not None:
                s, v = b.writer
                acc[s] = max(acc.get(s, 0), v)
            for s, v in b.readers.items():
                acc[s] = max(acc.get(s, 0), v)
        for b in new_bufs:
            b.writer = None
            b.readers = dict(acc)

    sem_misc = S.newsem("d_misc")
    sem_x = [S.newsem(f"d_x{j}") for j in range(4)]
    sem_x2 = [S.newsem(f"d_xb{j}") for j in range(4)]
    sem_y = [S.newsem(f"d_y{j}") for j in range(4)]
    sem_w = [S.newsem(f"d_w{i}") for i in range(NSLOT)]
    sem_cc = S.newsem("cc")
    sem_g = S.newsem("d_g")
    sem_pw = S.newsem("d_pw")
    sem_g2 = S.newsem("d_g2")
    sem_wh = [S.newsem(f"d_wh{i}") for i in range(NSLOT)]
    sem_cv = S.newsem("d_cv")
    sem_wr = S.newsem("d_wr")
    sem_stg = [S.newsem("d_stg0"), S.newsem("d_stg1")]

    plan = weight_plan(mode)
    wstate = {"issued": 0, "used": 0}

    post_cc = [False]
    stg_rr = [0]
    b_wsc = Buf("wsc")
    conv = []
    for name_, shp in wshape.items():
        cw = 512 if shp[1] % 512 == 0 else 256
        for r0 in range(0, shp[0], 128):
            conv.append((name_, r0, cw))
    conv_state = [0]

    def conv_issue(n):
        return

    def _conv_issue(n):
        while n > 0 and conv_state[0] < len(conv):
            name_, r0, cw = conv[conv_state[0]]
            src = wd[name_][r0:r0 + 128, :].rearrange("p (a c) -> p a c", c=cw)
            dst = wsc[name_][r0:r0 + 128, :].rearrange("p (a c) -> p a c", c=cw)
            S.dma('pool', dst, src, sem_cv, writes=[])
            conv_state[0] += 1
            n -= 1

    def w_issue_upto(k):
        while wstate["issued"] <= min(k, len(plan) - 1):
            i = wstate["issued"]
            name, kc0, nk, c0, ncols = plan[i]
            sl = i % NSLOT
            if not post_cc[0]:
                src = wd[name].rearrange("(k p) c -> p k c", p=128)[:, kc0:kc0 + nk, c0:c0 + ncols]
                S.dma('pool', wslot[sl][:, 0:nk, 0:ncols], src, sem_w[sl], writes=[b_ws[sl]])
                conv_issue(3)
            else:
                for q in range(0, ncols, 256):
                    k_ = stg_rr[0] % 2
                    stg_rr[0] += 1
                    w_ = min(256, ncols - q)
                    src = wd[name].rearrange("(k p) c -> p k c", p=128)[:, kc0:kc0 + nk, c0 + q:c0 + q + w_]
                    S.dma('sp', wstg[k_][:, 0:nk, 0:w_], src, sem_stg[k_], writes=[b_stg[k_]])
                    S.op('pool', lambda e: e.tensor_copy(out=wslot[sl][:, 0:nk, q:q + w_], in_=wstg[k_][:, 0:nk, 0:w_]),
                         [b_stg[k_]], [b_ws[sl]])
            wstate["issued"] += 1

    hold_prefetch = [mode == "R"]

    def w_next(name, kc0, nk, c0, ncols):
        i = wstate["used"]
        assert plan[i] == (name, kc0, nk, c0, ncols), (i, plan[i], (name, kc0, nk, c0, ncols))
        w_issue_upto(i if hold_prefetch[0] else i + PREFETCH)
        wstate["used"] += 1
        sl = i % NSLOT
        return wslot[sl], b_ws[sl]

    S.dma('sp', cv[:], cvec, sem_misc, writes=[b_cv])
    for i in range(3):
        S.dma('sp', gbc[i][:], gvec[i].partition_broadcast(128), sem_misc, writes=[b_gbc[i]])
    S.dma('pool', pw[:], pool_w.rearrange("g c d -> c g d"), sem_pw, writes=[b_pw])
    for b in [b_cv] + b_gbc:
        b.writer = (sem_misc, sem_misc.count)
    w_issue_upto(0 if mode == "R" else PREFETCH - 1)
    S.op('pool', lambda e: e.memset(ident[:], 1.0), writes=[b_ident])
    S.op('pool', lambda e: e.affine_select(out=ident[:], in_=ident[:], pattern=[[-1, 128]], compare_op=ALU.is_equal,
                                           fill=0.0, base=0, channel_multiplier=1), reads=[b_ident], writes=[b_ident])
    S.op('pool', lambda e: e.memset(cmask[:], 1.0), writes=[b_cmask])
    for hp in range(2):
        S.op('pool', lambda e: e.affine_select(out=cmask[hp * 64:(hp + 1) * 64, :], in_=cmask[hp * 64:(hp + 1) * 64, :],
                                               pattern=[[0, 4], [1, 64]], compare_op=ALU.is_ge, fill=0.0, base=0,
                                               channel_multiplier=-1), reads=[b_cmask], writes=[b_cmask])
    S.op('dve', lambda e: e.memset(ones32[:], 1.0), writes=[b_ones])
    S.op('dve', lambda e: e.memset(rmask[:], 1.0), writes=[b_rmask])
    S.op('dve', lambda e: e.memset(rmask[:].rearrange("p (c t) -> p c t", t=CH)[:, :, 0:1], 0.0), writes=[b_rmask])
    S.op('dve', lambda e: e.memset(Sst[:], 0.0), writes=b_Sst)
    S.op('dve', lambda e: e.memset(Bsum[:], 0.0), writes=[b_Bsum])
    C0, C1, NC1, CEPS, OMM = 0, 8, 16, 24, 25
    CV_LB0, CV_LB1, CV_GH, CV_PS, CV_FLAG, CV_M = 0, 8, 16, 17, 21, 22
    S.op('dve', lambda e: e.memset(cst[:, CEPS:CEPS + 1], EPS), writes=[b_cst])
    tt(cst[:, 33:41], cv[:, CV_LB0:CV_LB0 + 8], cv[:, CV_LB1:CV_LB1 + 8], ALU.subtract, [b_cv], [b_cst])
    act(cst[:, 33:41], cst[:, 33:41], AF.Tanh, [b_cst], [b_cst], scale=0.5)
    ts(cst[:, C0:C0 + 8], cst[:, 33:41], 0.25, 0.75, ALU.mult, ALU.add, [b_cst], [b_cst])
    ts(cst[:, C1:C1 + 8], cst[:, 33:41], -0.25, 0.25, ALU.mult, ALU.add, [b_cst], [b_cst])
    ts(cst[:, NC1:NC1 + 8], cst[:, 33:41], 0.25, -0.25, ALU.mult, ALU.add, [b_cst], [b_cst])
    ts(cst[:, OMM:OMM + 8], cv[:, CV_M:CV_M + 8], -1.0, 1.0, ALU.mult, ALU.add, [b_cv], [b_cst])

    class Grp:
        pass

    def main_group(g):
        G = Grp()
        G.ntok = T
        G.tiles = [128] * 4
        G.tcol = [j * 128 for j in range(4)]
        G.chunks = [(c // 2, (c % 2) * 64, CH, c * CH) for c in range(8)]
        G.halo = False
        G.g = g
        G.flagcol = None
        G.src = xm
        G.xt, G.b_xt = xt, b_xt
        G.tf, G.b_tf, G.V, G.b_V = tf, b_tf, V, b_V
        return G

    def halo_group():
        G = Grp()
        G.ntok = NHALO
        G.tiles = [NHALO]
        G.tcol = [0]
        G.chunks = [(0, 0, NHALO, 0)]
        G.halo = True
        G.g = -1
        G.flagcol = CV_FLAG
        G.src = xh
        G.xt, G.b_xt = xt, b_xt
        G.tf, G.b_tf, G.V, G.b_V = tf, b_tf, V, b_V
        return G

    ntm_rr = [0]
    hook = [lambda name: None]

    def load_x(G):
        if G.halo:
            S.dma('sp', G.xt[0:NHALO, 0, :], G.src, sem_x[0], writes=G.b_xt[0])
        else:
            sx = sem_x if G.xt is xt else sem_x2
            for j in range(4):
                r0 = G.g * T + j * 128
                S.dma('sp', G.xt[:, j, :], G.src[r0:r0 + 128, :], sx[j], writes=G.b_xt[j])

    def norm_T(G, gi, dstT, b_dstT):
        nt = len(G.tiles)
        pc0 = G.tiles[0]
        for j, pc in enumerate(G.tiles):
            act(junk[0:pc, :], G.xt[0:pc, j, :], AF.Square, G.b_xt[j], [b_junk, b_stat], accum=stat[0:pc, j:j + 1])
        act(stat[0:pc0, 8:8 + nt], stat[0:pc0, 0:nt], AF.Ln, [b_stat, b_cst], [b_stat], scale=1.0 / D, bias=cst[0:pc0, CEPS:CEPS + 1])
        act(stat[0:pc0, 8:8 + nt], stat[0:pc0, 8:8 + nt], AF.Exp, [b_stat], [b_stat], scale=-0.5)
        for j, pc in enumerate(G.tiles):
            k = ntm_rr[0] % 2
            ntm_rr[0] += 1
            stt(ntm[k][0:pc, :], G.xt[0:pc, j, :], stat[0:pc, 8 + j:9 + j], gbc[gi][0:pc, :], ALU.mult, ALU.mult,
                G.b_xt[j] + [b_stat, b_gbc[gi]], [b_ntm[k]])
            for kc in range(8):
                tp(ptr[:, kc * 128:kc * 128 + pc], ntm[k][0:pc, kc * 128:(kc + 1) * 128], ident[0:pc, 0:pc],
                   [b_ntm[k], b_ident], [b_ptr], signal=(kc == 7))
            c0 = G.tcol[j]
            if G.flagcol is None:
                act(dstT[:, :, c0:c0 + pc], ptr[:].rearrange("p (k t) -> p k t", t=128)[:, :, 0:pc], AF.Copy, [b_ptr], [b_dstT[j]])
            else:
                cp(dstT[:, :, c0:c0 + pc], ptr[:].rearrange("p (k t) -> p k t", t=128)[:, :, 0:pc], [b_ptr], [b_dstT[j]])

    def proj_fm(ws, bws, wc0, nk, rhsT, b_rhs, n, ps, bps):
        for k in range(nk):
            mm(ps[:, 0:n], ws[:, k, wc0:wc0 + 128], rhsT[:, k, 0:n], k == 0, k == nk - 1, [bws] + b_rhs, [bps], signal=(k == nk - 1))

    def rescan_units(G):
        units = []
        nb = b_nT
        for half in range(2):
            for hh in range(4):
                def u_f(half=half, hh=hh):
                    h = half * 4 + hh
                    ps, bps = next_pp()
                    proj_fm(WR[:, 2 * half], b_WR, hh * 128, 8, nT, nb, T, ps, bps)
                    act(G.tf[:, h, :], ps[:, :], AF.Tanh, [bps], [G.b_tf[h]], scale=0.5)
                units.append(u_f)
            for j in range(4):
                def u_i(half=half, j=j):
                    ps, bps = next_pp()
                    for k in range(8):
                        mm(ps[:, :], nT[:, k, j * 128:(j + 1) * 128], WR[:, 2 * half + 1, k, :], k == 0, k == 7, [b_WR, b_nT[j]], [bps], signal=(k == 7))
                    ts(G.V[:, j, half * 512:(half + 1) * 512], ps[:, :], cv[:, G.flagcol:G.flagcol + 1], None, ALU.mult, None,
                       [bps, b_cv], [G.b_V[j][half]])
                units.append(u_i)
        return units

    def f_proj(G, half, with_q, resident=False):
        n = G.ntok
        nb = b_nT[0:len(G.tiles)]
        if resident:
            ws, bws = WR[:, 2 * half], b_WR
        else:
            ws, bws = w_next("w_in", 0, 8, F0 + half * 512, 512)
        for hh in range(4):
            h = half * 4 + hh
            ps, bps = next_pp()
            proj_fm(ws, bws, hh * 128, 8, nT, nb, n, ps, bps)
            act(tf[:, h, 0:n], ps[:, 0:n], AF.Tanh, [bps], [b_tf[h]], scale=0.5)
        if with_q:
            hook[0]('Bf_f%d' % half)
            ws, bws = w_next("w_in", 0, 8, Q0 + half * 512, 512)
            for hh in range(4):
                h = half * 4 + hh
                ps, bps = next_pp()
                proj_fm(ws, bws, hh * 128, 8, nT, nb, n, ps, bps)
                act(qT[:, h, 0:n], ps[:, 0:n], SILU, [bps], [b_qT[h]])
            hook[0]('Bf_q%d' % half)
            ws, bws = w_next("w_in", 0, 8, OG0 + half * 512, 512)
            for hh in range(4):
                h = half * 4 + hh
                ps, bps = next_pp()
                proj_fm(ws, bws, hh * 128, 8, nT, nb, n, ps, bps)
                act(sog[:, h, 0:n], ps[:, 0:n], SILU, [bps], [b_sog[h]])
        if with_q: hook[0]('Bf_og%d' % half)
        if resident:
            ws, bws = WR[:, 2 * half + 1], b_WR
        else:
            ws, bws = w_next("w_in", 0, 8, I0 + half * 512, 512)
        for j, pc in enumerate(G.tiles):
            ps, bps = next_pp()
            c0 = G.tcol[j]
            for k in range(8):
                mm(ps[0:pc, :], nT[:, k, c0:c0 + pc], ws[:, k, :], k == 0, k == 7, [bws, b_nT[j]], [bps], signal=(k == 7))
            if G.flagcol is not None:
                ts(V[0:pc, j, half * 512:(half + 1) * 512], ps[0:pc, :], cv[0:pc, G.flagcol:G.flagcol + 1], None, ALU.mult, None,
                   [bps, b_cv], [b_V[j][half]])
            else:
                cp(V[0:pc, j, half * 512:(half + 1) * 512], ps[0:pc, :], [bps], [b_V[j][half]])

    def head_state(G, h, phaseB):
        n = G.ntok
        nch = len(G.chunks)
        half = h // 4
        act(lf[:, 0:n], tf[:, h, 0:n], AF.Ln, [b_tf[h], b_cst], [b_lf], scale=cst[:, C1 + h:C1 + h + 1], bias=cst[:, C0 + h:C0 + h + 1])
        if G.flagcol is not None:
            ts(lf[:, 0:n], lf[:, 0:n], cv[:, G.flagcol:G.flagcol + 1], None, ALU.mult, None, [b_lf, b_cv], [b_lf])
        S.op('dve', lambda e: e.tensor_tensor_scan(out=bS[:, 0:n], data0=rmask[:, 0:n], data1=lf[:, 0:n], initial=0.0,
                                                   op0=ALU.mult, op1=ALU.add), [b_lf, b_rmask], [b_bS])
        if G.halo:
            blast = bS[:, n - 1:n]
            blast_bc = blast.to_broadcast([128, n])
            cview = lf[:, 0:n]
            bview = bS[:, 0:n]
        else:
            b3 = bS[:].rearrange("p (c t) -> p c t", t=CH)
            blast = b3[:, :, CH - 1]
            blast_bc = b3[:, :, CH - 1:CH].to_broadcast([128, nch, CH])
            cview = lf[:].rearrange("p (c t) -> p c t", t=CH)
            bview = b3
        if not G.halo: hook[0]('H_scan')
        if not phaseB:
            S.op('dve', lambda e: e.tensor_reduce(out=stat[:, 16:17], in_=blast, axis=AX.X, op=ALU.add), [b_bS], [b_stat])
            tt(Bsum[:, h:h + 1], Bsum[:, h:h + 1], stat[:, 16:17], ALU.add, [b_Bsum, b_stat], [b_Bsum])
        if not G.halo: hook[0]('H_red')
        act(eb[:, 0:nch], blast, AF.Exp, [b_bS], [b_eb])
        if not G.halo: hook[0]('H_eb')
        tt(cview, bview, blast_bc, ALU.subtract, [b_bS], [b_lf])
        if not G.halo: hook[0]('H_c')
        if phaseB:
            act(Epos[:, 0:n], lf[:, 0:n], AF.Exp, [b_lf], [b_Epos])
            act(bS[:, 0:n], bS[:, 0:n], AF.Exp, [b_bS], [b_bS])
        act(lf[:, 0:n], lf[:, 0:n], AF.Exp, [b_lf], [b_lf], scale=-1.0)
        ts(tf[:, h, 0:n], tf[:, h, 0:n], cst[:, NC1 + h:NC1 + h + 1], cst[:, C1 + h:C1 + h + 1], ALU.mult, ALU.add,
           [b_tf[h], b_cst], [b_tf[h]])
        tt(KbT[:, 0:n], tf[:, h, 0:n], lf[:, 0:n], ALU.mult, [b_tf[h], b_lf], [b_KbT])
        if phaseB:
            tt(QdT[:, 0:n], qT[:, h, 0:n], Epos[:, 0:n], ALU.mult, [b_qT[h], b_Epos], [b_QdT])
            tt(QbT[:, 0:n], qT[:, h, 0:n], bS[:, 0:n], ALU.mult, [b_qT[h], b_bS], [b_QbT])
        if not G.halo: hook[0]('H_kb')
        nt = len(G.tiles)
        for j, pc in enumerate(G.tiles):
            c0 = G.tcol[j]
            tp(ptr[0:pc, j * 128:(j + 1) * 128], KbT[:, c0:c0 + pc], ident[:, :], [b_KbT, b_ident], [b_ptr], signal=(j == nt - 1))
        pc0 = G.tiles[0]
        cp(Kbtm[0:pc0, 0:nt, :], ptr[0:pc0, 0:nt * 128].rearrange("p (j d) -> p j d", d=128), [b_ptr], [b_Kbtm])
        if not G.halo: hook[0]('H_tp')
        for ci, (j, p0, C, col0) in enumerate(G.chunks):
            pu, bpu = (pu0, b_pu0) if ci % 2 == 0 else (pu1, b_pu1)
            mm(pu[:, (ci // 2) * 128:(ci // 2 + 1) * 128], Kbtm[p0:p0 + C, j, :], V[p0:p0 + C, j, h * 128:(h + 1) * 128], True, True,
               [b_Kbtm, b_V[j][half]], [bpu], signal=(ci >= nch - 2))
        if not G.halo: hook[0]('H_u')
        cp(Sch[:, 0, :], Sst[:, h, :], [b_Sst[h]], [b_Sch[0]])
        for ci in range(nch):
            pu, bpu = (pu0, b_pu0) if ci % 2 == 0 else (pu1, b_pu1)
            stt(Sch[:, ci + 1, :], Sch[:, ci, :], eb[:, ci:ci + 1], pu[:, (ci // 2) * 128:(ci // 2 + 1) * 128], ALU.mult, ALU.add,
                [b_Sch[ci], b_eb, bpu], [b_Sch[ci + 1]])
        cp(Sst[:, h, :], Sch[:, nch, :], [b_Sch[nch]], [b_Sst[h]])
        if not G.halo: hook[0]('H_chain')

    def hs2_A(G, h):
        q = h % 3
        act(lf2[q][:, :], G.tf[:, h, :], AF.Ln, [G.b_tf[h], b_cst], [b_lf2[q]], scale=cst[:, C1 + h:C1 + h + 1], bias=cst[:, C0 + h:C0 + h + 1])
        pts(G.tf[:, h, :], G.tf[:, h, :], cst[:, NC1 + h:NC1 + h + 1], cst[:, C1 + h:C1 + h + 1], ALU.mult, ALU.add,
            [G.b_tf[h], b_cst], [G.b_tf[h]])

    def hs2_B(G, h):
        q = h % 3
        S.op('dve', lambda e: e.tensor_tensor_scan(out=bS2[q][:, :], data0=onesT[:, :], data1=lf2[q][:, :], initial=0.0,
                                                   op0=ALU.mult, op1=ALU.add), [b_lf2[q], b_onesT], [b_bS2[q]])

    def hs2_C(G, h):
        q = h % 3
        fc = G.flagcol
        blast = bS2[q][:, T - 1:T]
        act(eb[:, 8 + h % 4:9 + h % 4], blast, AF.Exp, [b_bS2[q]], [b_eb2[h % 4]])
        ts(eb[:, 8 + h % 4:9 + h % 4], eb[:, 8 + h % 4:9 + h % 4], cv[:, fc:fc + 1], cst[:, 41 + fc - 40:42 + fc - 40], ALU.mult, ALU.add,
           [b_eb2[h % 4], b_cv, b_cst], [b_eb2[h % 4]])
        act(lf2[q][:, :], bS2[q][:, :], AF.Exp, [b_bS2[q]], [b_lf2[q]], scale=-1.0, bias=blast)

    def hs2_D1(G, h):
        q = h % 3; p = h % 2
        ptt(KbT2[p][:, :], G.tf[:, h, :], lf2[q][:, :], ALU.mult, [G.b_tf[h], b_lf2[q]], [b_KbT2[p]])

    def hs2_D2(G, h):
        p = h % 2
        for j in range(4):
            tp(ptr[:, j * 128:(j + 1) * 128], KbT2[p][:, j * 128:(j + 1) * 128], ident[:, :], [b_KbT2[p], b_ident], [b_ptr], signal=(j == 3))

    def hs2_D3(G, h):
        p = h % 2
        cp(Kbtm2[p][:, :, :], ptr[:, 0:512].rearrange("p (j d) -> p j d", d=128), [b_ptr], [b_Kbtm2[p]])

    def hs2_D4(G, h):
        p = h % 2
        half = h // 4
        pu, bpu = (pu0, b_pu0) if p == 0 else (pu1, b_pu1)
        for j in range(4):
            mm(pu[:, 0:128], Kbtm2[p][:, j, :], G.V[:, j, h * 128:(h + 1) * 128], j == 0, j == 3, [b_Kbtm2[p], G.b_V[j][half]], [bpu], signal=(j == 3))

    def hs2_D5(G, h):
        q = h % 3; p = h % 2
        pu, bpu = (pu0, b_pu0) if p == 0 else (pu1, b_pu1)
        stt(Sst[:, h, :], Sst[:, h, :], eb[:, 8 + h % 4:9 + h % 4], pu[:, 0:128], ALU.mult, ALU.add, [b_Sst[h], b_eb2[h % 4], bpu], [b_Sst[h]])

    def heads_state2(G, units=()):
        units = list(units)

        def take(n):
            for _ in range(n):
                if units:
                    units.pop(0)()
        hs2_A(G, 0)
        hs2_A(G, 1)
        hs2_B(G, 0)
        for it in range(8 + 2):
            take(2 if it < 8 else 0)
            if it - 1 >= 0 and it - 1 < 8:
                hs2_D2(G, it - 1)
            if it - 2 >= 0 and it - 2 < 8:
                hs2_D4(G, it - 2)
            if it + 2 < 8:
                hs2_A(G, it + 2)
            if it + 1 < 8:
                hs2_B(G, it + 1)
            if it < 8:
                hs2_C(G, it)
                hs2_D1(G, it)
            if it - 1 >= 0 and it - 1 < 8:
                hs2_D3(G, it - 1)
            if it - 2 >= 0 and it - 2 < 8:
                hs2_D5(G, it - 2)
        take(len(units))

    def head_out(G, h):
        half = h // 4
        hook[0]('O_start')
        S.op('act', lambda e: e.activation(out=Sbf[:].rearrange("p c e -> p (c e)"), in_=Sch[:, 0:8, :].rearrange("p c e -> p (c e)"),
                                           func=AF.Copy), b_Sch[0:8], [b_Sbf])
        for ci, (j, p0, C, col0) in enumerate(G.chunks):
            mm(pa[p0:p0 + CH, (ci // 2) * CH:(ci // 2 + 1) * CH], KbT[:, col0:col0 + CH], QdT[:, col0:col0 + CH], True, True,
               [b_KbT, b_QdT], [b_pa], signal=(ci == 7))
        hook[0]('O_at')
        tt(ATs[:, :], pa[:, 0:256], cmask[:, :], ALU.mult, [b_pa, b_cmask], [b_AT])
        hook[0]('O_mask')
        pos = ((po0, b_po0), (po1, b_po1))
        for ci, (j, p0, C, col0) in enumerate(G.chunks):
            po, b_po = pos[ci % 2]
            oc = (ci // 2) * CH
            mm(po[:, oc:oc + CH], V[p0:p0 + CH, j, h * 128:(h + 1) * 128], ATs[p0:p0 + CH, oc:oc + CH],
               True, False, [b_V[j][half], b_AT], [b_po], signal=False)
            mm(po[:, oc:oc + CH], Sbf[:, ci, :], QbT[:, col0:col0 + CH], False, True, [b_Sbf, b_QbT], [b_po], signal=(ci >= 6))

        hook[0]('O_o')

        def par_view(t2d, par):
            return t2d.rearrange("p (c two t) -> p c two t", two=2, t=CH)[:, :, par, :]

        def po_view(par):
            return pos[par][0][:, 0:256].rearrange("p (c t) -> p c t", t=CH)
        for par in range(2):
            act(par_view(Epos[:, :], par), po_view(par), AF.Square, [pos[par][1]], [b_Epos])
        hook[0]('O_sq')
        ps, bps = next_pp()
        mm(ps[:, :], ones32[:, :], Epos[:, :], True, True, [b_ones, b_Epos], [bps], signal=True)
        hook[0]('O_ones')
        act(lf[:, :], ps[:, :], AF.Ln, [bps, b_cst], [b_lf], scale=1.0 / 128, bias=cst[:, CEPS:CEPS + 1])
        act(lf[:, :], lf[:, :], AF.Exp, [b_lf], [b_lf], scale=-0.5)
        for par in range(2):
            tt(par_view(lf[:, :], par), po_view(par), par_view(lf[:, :], par), ALU.mult, [pos[par][1], b_lf], [b_lf])
        stt(yaT[:, h, :], lf[:, :], cv[:, CV_GH:CV_GH + 1], sog[:, h, :], ALU.mult, ALU.mult, [b_lf, b_cv, b_sog[h]], [b_yaT[h]])

    def head_chain(G, h, X):
        half = h // 4
        n = T
        pc = X.pc
        pt = 512 if pc else 0
        b3 = X.bS[:].rearrange("p (c t) -> p c t", t=CH)
        blast = b3[:, :, CH - 1]
        blast_bc = b3[:, :, CH - 1:CH].to_broadcast([128, 8, CH])
        cview = X.lf[:].rearrange("p (c t) -> p c t", t=CH)
        pos = ((po0, b_po0), (po1, b_po1))

        def par_view(t2d, par):
            return t2d.rearrange("p (c two t) -> p c two t", two=2, t=CH)[:, :, par, :]

        def po_view(par):
            return pos[par][0][:, pc:pc + 256].rearrange("p (c t) -> p c t", t=CH)

        def s1():
            act(X.lf[:, :], tf[:, h, :], AF.Ln, [b_tf[h], b_cst], [X.b_lf], scale=cst[:, C1 + h:C1 + h + 1], bias=cst[:, C0 + h:C0 + h + 1])
            pts(tf[:, h, :], tf[:, h, :], cst[:, NC1 + h:NC1 + h + 1], cst[:, C1 + h:C1 + h + 1], ALU.mult, ALU.add,
                [b_tf[h], b_cst], [b_tf[h]])

        def s2():
            S.op('dve', lambda e: e.tensor_tensor_scan(out=X.bS[:, :], data0=rmask[:, :], data1=X.lf[:, :], initial=0.0,
                                                       op0=ALU.mult, op1=ALU.add), [X.b_lf, b_rmask], [X.b_bS])

        def s3():
            act(X.eb[:, 0:8], blast, AF.Exp, [X.b_bS], [X.b_eb])
            tt(cview, b3, blast_bc, ALU.subtract, [X.b_bS], [X.b_lf])

        def s4():
            act(X.Epos[:, :], X.lf[:, :], AF.Exp, [X.b_lf], [X.b_Epos])
            act(X.bS[:, :], X.bS[:, :], AF.Exp, [X.b_bS], [X.b_bS])
            act(X.lf[:, :], X.lf[:, :], AF.Exp, [X.b_lf], [X.b_lf], scale=-1.0)

        def s5():
            tt(X.KbT[:, :], tf[:, h, :], X.lf[:, :], ALU.mult, [b_tf[h], X.b_lf], [X.b_KbT])
            tt(X.QdT[:, :], qT[:, h, :], X.Epos[:, :], ALU.mult, [b_qT[h], X.b_Epos], [X.b_QdT])
            tt(X.QbT[:, :], qT[:, h, :], X.bS[:, :], ALU.mult, [b_qT[h], X.b_bS], [X.b_QbT])

        def s6():
            for j in range(4):
                tp(ptr[:, pt + j * 128:pt + (j + 1) * 128], X.KbT[:, j * 128:(j + 1) * 128], ident[:, :], [X.b_KbT, b_ident], [b_ptr], signal=(j == 3))

        def s7():
            act(X.Kbtm[:].rearrange("p j d -> p (j d)"), ptr[:, pt:pt + 512], AF.Copy, [b_ptr], [X.b_Kbtm])

        def s8():
            for ci, (j, p0, C, col0) in enumerate(G.chunks):
                pu, bpu = (pu0, b_pu0) if ci % 2 == 0 else (pu1, b_pu1)
                mm(pu[:, (ci // 2) * 128:(ci // 2 + 1) * 128], X.Kbtm[p0:p0 + C, j, :], V[p0:p0 + C, j, h * 128:(h + 1) * 128], True, True,
                   [X.b_Kbtm, b_V[j][half]], [bpu], signal=(ci >= 6))

        def s9():
            act(X.Sbf[:, 0, :], Sst[:, h, :], AF.Copy, [b_Sst[h]], [X.b_Sbf])
            for ci in range(8):
                pu, bpu = (pu0, b_pu0) if ci % 2 == 0 else (pu1, b_pu1)
                src, bsrc = (Sst[:, h, :], b_Sst[h]) if ci == 0 else (X.Sch[:, ci, :], X.b_Sch[ci])
                dst, bdst = (Sst[:, h, :], b_Sst[h]) if ci == 7 else (X.Sch[:, ci + 1, :], X.b_Sch[ci + 1])
                stt(dst, src, X.eb[:, ci:ci + 1], pu[:, (ci // 2) * 128:(ci // 2 + 1) * 128], ALU.mult, ALU.add,
                    [bsrc, X.b_eb, bpu], [bdst])

        def s10():
            S.op('act', lambda e: e.activation(out=X.Sbf[:, 1:8, :].rearrange("p c e -> p (c e)"), in_=X.Sch[:, 1:8, :].rearrange("p c e -> p (c e)"),
                                               func=AF.Copy), X.b_Sch[1:8], [X.b_Sbf])

        def s11():
            for ci, (j, p0, C, col0) in enumerate(G.chunks):
                mm(pa[p0:p0 + CH, pc + (ci // 2) * CH:pc + (ci // 2 + 1) * CH], X.KbT[:, col0:col0 + CH], X.QdT[:, col0:col0 + CH], True, True,
                   [X.b_KbT, X.b_QdT], [b_pa], signal=(ci == 7))

        def s12():
            tt(X.AT[:, :], pa[:, pc:pc + 256], cmask[:, :], ALU.mult, [b_pa, b_cmask], [X.b_AT])

        def s13():
            for ci, (j, p0, C, col0) in enumerate(G.chunks):
                po, b_po = pos[ci % 2]
                oc = (ci // 2) * CH
                mm(po[:, pc + oc:pc + oc + CH], V[p0:p0 + CH, j, h * 128:(h + 1) * 128], X.AT[p0:p0 + CH, oc:oc + CH],
                   True, False, [b_V[j][half], X.b_AT], [b_po], signal=False)
                mm(po[:, pc + oc:pc + oc + CH], X.Sbf[:, ci, :], X.QbT[:, col0:col0 + CH], False, True, [X.b_Sbf, X.b_QbT], [b_po], signal=(ci >= 6))

        def s14():
            for par in range(2):
                act(par_view(X.Epos[:, :], par), po_view(par), AF.Square, [pos[par][1]], [X.b_Epos])

        def s15():
            ps, bps = next_pp()
            X.ps, X.bps = ps, bps
            mm(ps[:, :], ones32[:, :], X.Epos[:, :], True, True, [b_ones, X.b_Epos], [bps], signal=True)

        def s16():
            act(X.lf[:, :], X.ps[:, :], AF.Ln, [X.bps, b_cst], [X.b_lf], scale=1.0 / 128, bias=cst[:, CEPS:CEPS + 1])
            act(X.lf[:, :], X.lf[:, :], AF.Exp, [X.b_lf], [X.b_lf], scale=-0.5)

        def s17():
            for par in range(2):
                tt(par_view(X.lf[:, :], par), po_view(par), par_view(X.lf[:, :], par), ALU.mult, [pos[par][1], X.b_lf], [X.b_lf])
            stt(yaT[:, h, :], X.lf[:, :], cv[:, CV_GH:CV_GH + 1], sog[:, h, :], ALU.mult, ALU.mult, [X.b_lf, b_cv, b_sog[h]], [b_yaT[h]])
        return [s1, s2, s3, s4, s5, s6, s7, s8, s9, s10, s11, s12, s13, s14, s15, s16, s17]

    def heads_phase_b(G):
        for hp in range(4):
            ca = head_chain(G, 2 * hp, TS0)
            cb = head_chain(G, 2 * hp + 1, TS1)
            ca[0]()
            for k in range(len(ca)):
                if k + 1 < len(ca):
                    ca[k + 1]()
                cb[k]()

    def pool_proj(G, col_off):
        n = G.ntok
        nb = b_nT[0:len(G.tiles)]
        ws, bws = w_next("w_in", 0, 8, PL0, 512)
        for gi in range(4):
            ps, bps = next_pp()
            proj_fm(ws, bws, gi * 128, 8, nT, nb, n, ps, bps)
            if col_off:
                act(uT[:, gi, col_off:col_off + n], ps[:, 0:n], AF.Copy, [bps], [b_uT[gi]])
            else:
                cp(uT[:, gi, col_off:col_off + n], ps[:, 0:n], [bps], [b_uT[gi]])

    def pool_branch(G):
        L = NHALO + T
        b_pool = [Buf(), Buf(), Buf()]
        retire(b_pool, b_tmp_all)
        b_sA, b_sB, b_pl = b_pool
        for gi in range(4):
            w = 2 ** (gi + 1)
            cur, bcur = uT[:, gi, :], b_uT[gi]
            lo = 0
            bufs = [(sA, b_sA), (sB, b_sB)]
            st = 1
            k = 0
            while st < w:
                nxt, bn = bufs[k % 2]
                lo2 = lo + st
                tt(nxt[:, lo2:L], cur[:, lo2:L], cur[:, lo2 - st:L - st], ALU.add, [bcur], [bn])
                cur, bcur = nxt[:, :], bn
                lo = lo2
                st *= 2
                k += 1
            stt(pooledT[:, gi, :], cur[:, NHALO:L], 1.0 / w, uT[:, gi, NHALO:L], ALU.mult, ALU.subtract, [bcur, b_uT[gi]], [b_pl])
        for gi in range(4):
            ps, bps = next_pp()
            mm(ps[:, :], pw[:, gi, :], pooledT[:, gi, :], True, True, [b_pw, b_pl], [bps], signal=True)
            ts(ybT[:, gi, :], ps[:, :], cv[:, CV_PS + gi:CV_PS + gi + 1], None, ALU.mult, None, [bps, b_cv], [b_ybT[gi]])
            cp(uT[:, gi, 0:NHALO], uT[:, gi, T:T + NHALO], [b_uT[gi]], [b_uT[gi]])
        retire(b_tmp_all, b_pool)

    def merge_branches(G):
        b_tga = [Buf() for _ in range(4)]
        b_tgb = [Buf() for _ in range(4)]
        retire(b_tga + b_tgb, b_qT)
        for dh in range(2):
            for (c0, dst, bd) in ((GA0, tga, b_tga), (GB0, tgb, b_tgb)):
                ws, bws = w_next("w_in", 0, 8, c0 + dh * 512, 512)
                for i in range(4):
                    ps, bps = next_pp()
                    proj_fm(ws, bws, i * 128, 8, nT, b_nT, T, ps, bps)
                    act(dst[:, i, :], ps[:, :], AF.Tanh, [bps], [bd[i]], scale=0.5)
            ws, bws = w_next("w_ba", 0, 8, dh * 512, 512)
            for i in range(4):
                ps, bps = next_pp()
                for h in range(8):
                    mm(ps[:, :], ws[:, h, i * 128:(i + 1) * 128], yaT[:, h, :], h == 0, h == 7, [bws, b_yaT[h]], [bps], signal=(h == 7))
                stt(tga[:, i, :], tga[:, i, :], 1.0, ps[:, :], ALU.add, ALU.mult, [b_tga[i], bps], [b_tga[i]])
            ws, bws = w_next("w_bb", 0, 4, dh * 512, 512)
            for i in range(4):
                ps, bps = next_pp()
                for gi in range(4):
                    mm(ps[:, :], ws[:, gi, i * 128:(i + 1) * 128], ybT[:, gi, :], gi == 0, gi == 3, [bws, b_ybT[gi]], [bps], signal=(gi == 3))
                stt(tgb[:, i, :], tgb[:, i, :], 1.0, ps[:, :], ALU.add, ALU.mult, [b_tgb[i], bps], [b_tgb[i]])
                tt(mergedT[:, dh * 4 + i, :], tga[:, i, :], tgb[:, i, :], ALU.add, [b_tga[i], b_tgb[i]], [b_mg[dh * 4 + i]])
        retire(b_qT, b_tga + b_tgb)

    def out_proj(G):
        for half in range(2):
            ws, bws = w_next("w_out", 0, 8, half * 512, 512)
            for j in range(4):
                ps, bps = next_pp()
                for k in range(8):
                    mm(ps[:, :], mergedT[:, k, j * 128:(j + 1) * 128], ws[:, k, :], k == 0, k == 7, [bws, b_mg[k]], [bps], signal=(k == 7))
                xs = xt[:, j, half * 512:(half + 1) * 512]
                stt(xs, ps[:, :], 0.5, xs, ALU.mult, ALU.add, [bps, b_xt[j][half]], [b_xt[j][half]])

    def ffn(G):
        b_sg = [Buf() for _ in range(4)]
        retire(b_sg, b_tf)
        for fblk in range(6):
            ncols = 512 if fblk < 5 else 256
            nfb = ncols // 128
            ws, bws = w_next("w_g", 0, 8, fblk * 512, ncols)
            for fb in range(nfb):
                ps, bps = next_pp()
                proj_fm(ws, bws, fb * 128, 8, nT, b_nT, T, ps, bps)
                act(sg[:, fb, :], ps[:, :], SILU, [bps], [b_sg[fb]])
            ws, bws = w_next("w_u", 0, 8, fblk * 512, ncols)
            for fb in range(nfb):
                ps, bps = next_pp()
                proj_fm(ws, bws, fb * 128, 8, nT, b_nT, T, ps, bps)
                tt(hidT[:, fblk * 4 + fb, :], sg[:, fb, :], ps[:, :], ALU.mult, [b_sg[fb], bps], [b_hid[fblk * 4 + fb]])
        retire(b_tf, b_sg)
        for half in range(2):
            for kg, (kc0, nk) in enumerate(((0, 8), (8, 8), (16, 6))):
                ws, bws = w_next("w_d", kc0, nk, half * 512, 512)
                for j in range(4):
                    for k in range(nk):
                        mm(pacc[j][:, :], hidT[:, kc0 + k, j * 128:(j + 1) * 128], ws[:, k, :], kg == 0 and k == 0, kg == 2 and k == nk - 1,
                           [bws, b_hid[kc0 + k]], [b_pacc[j]], signal=(k == nk - 1))
            for j in range(4):
                xs = xt[:, j, half * 512:(half + 1) * 512]
                tt(xs, pacc[j][:, :], xs, ALU.add, [b_pacc[j], b_xt[j][half]], [b_xt[j][half]])

    def final_norm_store(G):
        for j in range(4):
            act(junk[:, :], xt[:, j, :], AF.Square, b_xt[j], [b_junk, b_stat], accum=stat[:, j:j + 1])
        act(stat[:, 8:12], stat[:, 0:4], AF.Ln, [b_stat, b_cst], [b_stat], scale=1.0 / D, bias=cst[:, CEPS:CEPS + 1])
        act(stat[:, 8:12], stat[:, 8:12], AF.Exp, [b_stat], [b_stat], scale=-0.5)
        for j in range(4):
            stt(xt[:, j, :], xt[:, j, :], stat[:, 8 + j:9 + j], gbc[2][:, :], ALU.mult, ALU.mult, b_xt[j] + [b_stat, b_gbc[2]], b_xt[j])
            r0 = G.g * T + j * 128
            S.dma('sp', y[r0:r0 + 128, :], xt[:, j, :], sem_y[j], reads=b_xt[j])

    def phase_a_group(G, do_load=True):
        if do_load:
            load_x(G)
        if not G.halo: hook[0]('A_load')
        norm_T(G, 0, nT, b_nT)
        if not G.halo: hook[0]('A_norm')
        for half in range(2):
            f_proj(G, half, with_q=False, resident=(mode == "R"))
        if not G.halo: hook[0]('A_fproj')
        if G.halo and mode != "R":
            pool_proj(G, 0)
        if mode == "R" and not G.halo:
            heads_state2(G)
        else:
            for h in range(8):
                head_state(G, h, phaseB=False)

    def emit_all(stage):
        hook[0] = stage
        stage('setup')
        if mode == "R":
            for bi, c0_ in enumerate((F0, I0, F0 + 512, I0 + 512)):
                S.dma('pool', WR[:, bi], wd["w_in"].rearrange("(k p) c -> p k c", p=128)[:, :, c0_:c0_ + 512], sem_wr, writes=[b_WR])
            b_WR.writer = (sem_wr, sem_wr.count)
            retire(b_lf2 + b_bS2 + [b_onesT], b_qT)
            retire(b_KbT2 + b_Kbtm2, b_sog)
            S.op('dve', lambda e: e.memset(onesT[:], 1.0), writes=[b_onesT])
            ts(cst[:, 41:53], cv[:, 40:52], -1.0, 1.0, ALU.mult, ALU.add, [b_cv], [b_cst])
            Gm = halo_group()
            Gm.src = xmeta
            phase_a_group(Gm)
            Gh = halo_group()
            load_x(Gh)
            norm_T(Gh, 0, nT, b_nT)
            pool_proj(Gh, 0)
            retire([b for jj in b_xt2 for b in jj], b_ws[1:3])

            def pred_group(gi):
                Gp = main_group(gi)
                Gp.src = xp
                Gp.flagcol = 40 + gi
                if gi % 2 == 1:
                    Gp.xt, Gp.b_xt = xt2, b_xt2
                return Gp
            npg = NPRED * NG
            retire(b_tfB, b_tmp_all)
            retire([b for jj in b_VB for b in jj], b_stg)
            Gs = []
            for gi in range(npg):
                Gp = pred_group(gi)
                if gi % 2 == 1:
                    Gp.tf, Gp.b_tf, Gp.V, Gp.b_V = tfB, b_tfB, VB, b_VB
                Gs.append(Gp)
            load_x(Gs[0])
            if npg > 1:
                load_x(Gs[1])
            norm_T(Gs[0], 0, nT, b_nT)
            for u in rescan_units(Gs[0]):
                u()
            for gi in range(npg):
                if gi + 2 < npg:
                    load_x(Gs[gi + 2])
                if gi + 1 < npg:
                    norm_T(Gs[gi + 1], 0, nT, b_nT)
                    heads_state2(Gs[gi], rescan_units(Gs[gi + 1]))
                else:
                    heads_state2(Gs[gi])
            stage('phaseA')
            retire(b_mg + b_hid, [b_WR])
            retire(b_qT, b_lf2 + b_bS2 + [b_onesT])
            retire(b_sog, b_KbT2 + b_Kbtm2)
            retire(b_ws[1:3], [b for jj in b_xt2 for b in jj])
            retire(b_tmp_all, b_tfB)
            retire(b_ts1_all, [b_WR] + [b for jj in b_VB for b in jj])
            hold_prefetch[0] = False
            phase_b(stage)
            return
        phase_a_group(halo_group())
        stage('halo')
        cp(Sh[:], Sst[:], b_Sst, [b_Sh])
        act(Dh[:], Bsum[:], AF.Exp, [b_Bsum], [b_Dh])
        b_xtmp = Buf("xtmp")
        b_gin, b_gout = Buf("gin"), Buf("gout")
        if mode != "B":
            for g in range(NG):
                phase_a_group(main_group(g))
            stage('phaseA')
            retire([b_xtmp], b_hid)
            cp(xtmp[:, 0:1024], Sst[:].rearrange("p h e -> p (h e)"), b_Sst, [b_xtmp])
            act(xtmp[:, 1024:1032], Bsum[:], AF.Exp, [b_Bsum], [b_xtmp])
        else:
            retire([b_xtmp], b_hid)
        if mode == "A":
            S.dma('sp', su, xtmp[:], sem_g2, reads=[b_xtmp])
            return
        if mode == "fused":
            S.dma('pool', gin.ap(), xtmp[:], sem_g, reads=[b_xtmp], writes=[b_gin])
            S.deps('pool', [b_gin], [b_gout])
            conv_issue(len(conv))
            S.wait_all('pool', sem_w + [sem_g, sem_pw, sem_cv])
            b_wsc.writer = (sem_cv, sem_cv.count)
            nc.gpsimd.collective_compute("AllGather", ALU.bypass, replica_groups=[list(range(NCORES))],
                                         ins=[gin.ap().opt()], outs=[gout.ap().opt()]).then_inc(sem_cc.h, 1)
            sem_cc.count += 1
            S.mark((sem_cc, sem_cc.count), [b_gin], [b_gout])
            S.wait_all('pool', [sem_cc])
            post_cc[0] = True
            gsrc = gout.ap()
        else:
            gsrc = gall
        S.op('dve', lambda e: e.memset(Sst[:], 0.0), writes=b_Sst)
        for j in range(NCORES):
            S.dma('sp', xtmp[:], gsrc[j * 128:(j + 1) * 128, :], sem_g2, reads=[b_gout], writes=[b_xtmp])
            ts(stat[:, 20:28], xtmp[:, 1024:1032], cv[:, CV_M + j:CV_M + j + 1], cst[:, OMM + j:OMM + j + 1], ALU.mult, ALU.add,
               [b_xtmp, b_cv, b_cst], [b_stat])
            ts(xtmp[:, 0:1024], xtmp[:, 0:1024], cv[:, CV_M + j:CV_M + j + 1], None, ALU.mult, None, [b_xtmp, b_cv], [b_xtmp])
            for h in range(8):
                stt(Sst[:, h, :], Sst[:, h, :], stat[:, 20 + h:21 + h], xtmp[:, h * 128:(h + 1) * 128], ALU.mult, ALU.add,
                    [b_Sst[h], b_stat, b_xtmp], [b_Sst[h]])
        for h in range(8):
            stt(Sst[:, h, :], Sst[:, h, :], Dh[:, h:h + 1], Sh[:, h, :], ALU.mult, ALU.add, [b_Sst[h], b_Dh, b_Sh], [b_Sst[h]])
        retire(b_hid, [b_xtmp])
        dv = os.environ.get('K_DUMMY')
        if dv:
            sem_dm = S.newsem("d_dm")
            bdm = Buf("dm")
            retire([bdm], b_hid)
            if dv == '1':
                S.dma('sp', hidT[:, 0, :], wsc['w_in'][0:128, 0:512], sem_dm, writes=[bdm])
            elif dv == '2':
                S.dma('sp', xtmp[:, 0:512], wd['w_in'][0:128, 0:512], sem_dm, writes=[bdm])
            elif dv == '3':
                S.dma('sp', xtmp[:, 0:512], xm[0:128, 0:512], sem_dm, writes=[bdm])
            S.wait_all('sp', [sem_dm])
        stage('exch')

        phase_b(stage)

    def phase_b(stage):
        for g in range(NG):
            G = main_group(g)
            load_x(G)
            stage('B_load')
            norm_T(G, 0, nT, b_nT)
            stage('B_norm')
            for half in range(2):
                f_proj(G, half, with_q=True)
            stage('B_fproj')
            if mode == "R":
                heads_phase_b(G)
            else:
                for h in range(8):
                    head_state(G, h, phaseB=True)
                    head_out(G, h)
            stage('B_heads')
            pool_proj(G, NHALO)
            pool_branch(G)
            stage('B_pool')
            merge_branches(G)
            stage('B_merge')
            out_proj(G)
            stage('B_out')
            norm_T(G, 1, nT, b_nT)
            ffn(G)
            stage('B_ffn')
            final_norm_store(G)
            stage('B_g%d' % g)


    class StopBuild(Exception):
        pass
    STOP = os.environ.get('K_STOP', '')

    def stage(name):
        if STOP == name:
            raise StopBuild()

    try:
        emit_all(stage)
    except StopBuild:
        print("STOPPED at", STOP)
    else:
        assert wstate["used"] == len(plan), (wstate, len(plan))
    S.wait_all('sp', sem_y + [sem_misc, sem_g2] + sem_x + sem_x2 + sem_stg)
    S.wait_all('pool', sem_w + [sem_g, sem_cc, sem_pw, sem_cv, sem_wr])
    S.wait_all('act', [S.esem['dve'], S.esem['pe']])
    S.wait_all('dve', [S.esem['act']])
    return nc


_CACHE = {}


def kernel(x, meta_tokens, norm_mix_g, w_in, lb_raw, hgrn_norm_g, pool_w, pool_scale,
           w_branch_a, w_branch_b, w_out, norm_ffn_g, w_ffn_gate, w_ffn_up, w_ffn_down, norm_final_g):
    f = lambda a: np.ascontiguousarray(np.asarray(a, dtype=np.float32))
    x = f(x)
    meta = f(meta_tokens)
    B = x.shape[0]
    segs = NCORES // B
    MODE = os.environ.get("K_MODE", "R")
    if MODE not in _CACHE:
        _CACHE[MODE] = (build_program({"fused": "fused", "R": "R"}[MODE]),) if MODE in ("fused", "R") else (build_program("A"), build_program("B"))
    progs = _CACHE[MODE]
    shared = {
        "w_in": f(w_in[0]), "w_ba": f(w_branch_a[0]), "w_bb": f(w_branch_b[0]), "w_out": f(w_out[0]),
        "w_g": f(w_ffn_gate[0]), "w_u": f(w_ffn_up[0]), "w_d": f(w_ffn_down[0]), "pool_w": f(pool_w[0]),
        "gvec": np.ascontiguousarray(np.stack([f(norm_mix_g[0]), f(norm_ffn_g[0]), f(norm_final_g)], 0)),
    }
    lb = f(lb_raw)
    in_maps = []
    for c in range(NCORES):
        b, s = divmod(c, segs)
        cvec = np.zeros((128, NCV), np.float32)
        cvec[:, 0:8] = lb[0].reshape(H, 128).T
        cvec[:, 8:16] = lb[1].reshape(H, 128).T
        cvec[:, 16] = f(hgrn_norm_g[0])
        cvec[:, 17:21] = f(pool_scale[0]).reshape(4, 128).T
        cvec[:, 21] = 1.0 if (s == 0 or MODE == "R") else 0.0
        if MODE == "R":
            xp = np.zeros((NPRED * NTOK, D), np.float32)
            npre = min(s, NPRED) * NTOK
            if npre:
                xp[NPRED * NTOK - npre:] = x[b, s * NTOK - npre:s * NTOK]
            for gi in range(NPRED * NG):
                cvec[:, 40 + gi] = 1.0 if gi * T >= NPRED * NTOK - npre else 0.0
        for j in range(NCORES):
            bj, sj = divmod(j, segs)
            cvec[:, 22 + j] = 1.0 if (bj == b and sj < s) else 0.0
        xm = x[b, s * NTOK:(s + 1) * NTOK]
        xh = meta if s == 0 else x[b, s * NTOK - NHALO:s * NTOK]
        m = dict(shared)
        m.update({"xm": np.ascontiguousarray(xm), "xh": np.ascontiguousarray(xh), "cvec": cvec})
        if MODE == "R":
            m.update({"xp": xp, "xmeta": meta})
        in_maps.append(m)
    if MODE in ("fused", "R"):
        res = run_bass_kernel_spmd(progs[0], in_maps, core_ids=list(range(NCORES)))
    else:
        ra = run_bass_kernel_spmd(progs[0], in_maps, core_ids=list(range(NCORES)))
        gall = np.ascontiguousarray(np.concatenate([ra.results[c]["su"] for c in range(NCORES)], 0))
        for m in in_maps:
            m["gall"] = gall
        res = run_bass_kernel_spmd(progs[1], in_maps, core_ids=list(range(NCORES)))
    out = np.empty((B, segs * NTOK, D), np.float32)
    for c in range(NCORES):
        b, s = divmod(c, segs)
        out[b, s * NTOK:(s + 1) * NTOK] = res.results[c]["y"]
    return out
```
